# Optimizing a Trainium2 kernel written in Bass

```python
import jax, jax.numpy as jnp
from jax import lax
import numpy as np

D_MODEL = 1024
BATCH = 2
SEQ = 16384
DEPTH = 1
DEC_BATCH = 32
DEC_SEQ = 32
PAST_LEN = 4096

CHUNK = 64
Q_BLOCK = 128
HEAD_DIM = 64
N_HEADS_A = (D_MODEL // 2) // HEAD_DIM
N_KV_A = 2
N_HEADS_B = (D_MODEL // 2) // HEAD_DIM
N_IDX_HEADS = 8
IDX_DIM = 32
TOPK_MAX = 256
D_FF = 4 * D_MODEL
PLE_DIM = 256
ROPE_THETA = 500000.0
EPS = 1e-6
WIDTH_A = N_HEADS_A * HEAD_DIM
WIDTH_B = N_HEADS_B * HEAD_DIM
IN_SIZES = (WIDTH_A, N_KV_A * HEAD_DIM, N_KV_A * HEAD_DIM, N_IDX_HEADS * IDX_DIM, IDX_DIM, N_IDX_HEADS,
            WIDTH_B, WIDTH_B, WIDTH_B, D_MODEL, D_MODEL)
IN_WIDTH = sum(IN_SIZES)

kernel_name = 'dsa_stickbreaking_gated_hybrid_step'


def _split_offsets():
    offs = []
    acc = 0
    for s in IN_SIZES[:-1]:
        acc += s
        offs.append(acc)
    return offs


def rmsnorm(x, g):
    xf = x.astype(jnp.float32)
    y = xf * lax.rsqrt(jnp.mean(xf * xf, axis=-1, keepdims=True) + EPS)
    return y.astype(x.dtype) * g


def rope_partial(x, pos):
    d = x.shape[-1]
    r = d // 4
    hr = r // 2
    inv = jnp.power(jnp.float32(ROPE_THETA), -jnp.arange(hr, dtype=jnp.float32) * (2.0 / r))
    ang = pos.astype(jnp.float32)[:, None] * inv[None, :]
    ang = ang.reshape((1, pos.shape[0]) + (1,) * (x.ndim - 3) + (hr,))
    cos, sin = jnp.cos(ang), jnp.sin(ang)
    xr = x[..., :r].astype(jnp.float32)
    x1, x2 = xr[..., :hr], xr[..., hr:]
    rot = jnp.concatenate([x1 * cos - x2 * sin, x2 * cos + x1 * sin], axis=-1)
    return jnp.concatenate([rot.astype(x.dtype), x[..., r:]], axis=-1)


def mixer_inputs(h, pos, w_in, g_qa, g_ka):
    B, T, _ = h.shape
    u = h @ w_in
    qa, ka, va, qi, ki, wi, qb, kb, vb, ga, gb = jnp.split(u, _split_offsets(), axis=-1)
    qa = rope_partial(rmsnorm(qa.reshape(B, T, N_HEADS_A, HEAD_DIM), g_qa), pos)
    ka = rope_partial(rmsnorm(ka.reshape(B, T, N_KV_A, HEAD_DIM), g_ka), pos)
    va = va.reshape(B, T, N_KV_A, HEAD_DIM)
    qi = rope_partial(qi.reshape(B, T, N_IDX_HEADS, IDX_DIM), pos)
    ki = rope_partial(ki, pos)
    wi = wi * (N_IDX_HEADS ** -0.5)
    qb = qb.reshape(B, T, N_HEADS_B, HEAD_DIM)
    kb = kb.reshape(B, T, N_HEADS_B, HEAD_DIM)
    vb = vb.reshape(B, T, N_HEADS_B, HEAD_DIM)
    return qa, ka, va, qi, ki, wi, qb, kb, vb, ga, gb


def dsa_block(q, qi, wi, q_pos, k, v, ki, n_sel):
    B, T, H, D = q.shape
    L = k.shape[1]
    logits = jnp.einsum('bthi,bsi->bths', qi, ki).astype(jnp.float32) * (IDX_DIM ** -0.5)
    score = jnp.einsum('bths,bth->bts', jax.nn.relu(logits), wi.astype(jnp.float32))
    limit = (q_pos // CHUNK + 1) * CHUNK
    k_pos = jnp.arange(L, dtype=jnp.int32)
    score = jnp.where((k_pos[None, :] < limit[:, None])[None], score, -jnp.inf)
    _, idx = lax.top_k(score, n_sel)
    valid = idx < limit[None, :, None]
    gather = jax.vmap(lambda rows, ids: rows[ids])
    k_sel = gather(k, idx)
    v_sel = gather(v, idx)
    qg = q.reshape(B, T, N_KV_A, H // N_KV_A, D)
    s = jnp.einsum('btkgd,btnkd->btkgn', qg, k_sel).astype(jnp.float32) * (D ** -0.5)
    s = jnp.where(valid[:, :, None, None, :], s, -jnp.inf)
    pr = jax.nn.softmax(s, axis=-1).astype(v.dtype)
    o = jnp.einsum('btkgn,btnkd->btkgd', pr, v_sel)
    return o.reshape(B, T, H * D)


def stick_breaking_block(q, q_pos, k, v):
    B, T, H, D = q.shape
    L = k.shape[1]
    z = jnp.einsum('bthd,bshd->bhts', q, k).astype(jnp.float32) * (D ** -0.5)
    mask = jnp.arange(L, dtype=jnp.int32)[None, :] < q_pos[:, None]
    log_beta = jax.nn.log_sigmoid(z)
    log_keep = jnp.where(mask, jax.nn.log_sigmoid(-z), 0.0)
    incl = lax.cumsum(log_keep, axis=3, reverse=True)
    excl = jnp.concatenate([incl[..., 1:], jnp.zeros_like(incl[..., :1])], axis=-1)
    a = jnp.where(mask, jnp.exp(log_beta + excl), 0.0).astype(v.dtype)
    o = jnp.einsum('bhts,bshd->bthd', a, v)
    return o.reshape(B, T, H * D)


def to_blocks(a):
    B, T = a.shape[:2]
    return jnp.moveaxis(a.reshape((B, T // Q_BLOCK, Q_BLOCK) + a.shape[2:]), 1, 0)


def from_blocks(a):
    NB, B, QB = a.shape[:3]
    return jnp.moveaxis(a, 0, 1).reshape((B, NB * QB) + a.shape[3:])


def merge(o_a, o_b, ga, gb, proj_a, proj_b, w_out):
    m = jax.nn.sigmoid(ga) * (o_a @ proj_a) + jax.nn.sigmoid(gb) * (o_b @ proj_b)
    return m @ w_out


def channel_mlp(x, g, w_up, w_down):
    return jnp.square(jax.nn.relu(rmsnorm(x, g) @ w_up)) @ w_down


def per_layer_embed(x, p, g, w_ple, w_gate):
    return (p @ w_ple) * jax.nn.sigmoid(rmsnorm(x, g) @ w_gate)


def setup_inputs(seed: int = 0) -> dict:
    key = jax.random.key(seed)
    ks = jax.random.split(key, 24)
    f32 = jnp.float32

    def nrm(k, shape, scale=1.0):
        return jax.random.normal(k, shape, f32) * scale

    def gain(k, shape):
        return 1.0 + 0.01 * jax.random.normal(k, shape, f32)

    return {
        'x_prompt': nrm(ks[0], (BATCH, SEQ, D_MODEL)),
        'x_sample': nrm(ks[1], (DEC_BATCH, DEC_SEQ, D_MODEL)),
        'cache_a_k': nrm(ks[2], (DEPTH, DEC_BATCH, PAST_LEN, N_KV_A, HEAD_DIM)),
        'cache_a_v': nrm(ks[3], (DEPTH, DEC_BATCH, PAST_LEN, N_KV_A, HEAD_DIM)),
        'cache_a_kidx': nrm(ks[4], (DEPTH, DEC_BATCH, PAST_LEN, IDX_DIM)),
        'cache_b_k': nrm(ks[5], (DEPTH, DEC_BATCH, PAST_LEN, N_HEADS_B, HEAD_DIM)),
        'cache_b_v': nrm(ks[6], (DEPTH, DEC_BATCH, PAST_LEN, N_HEADS_B, HEAD_DIM)),
        'p_prompt': nrm(ks[7], (DEPTH, BATCH, SEQ, PLE_DIM)),
        'p_sample': nrm(ks[8], (DEPTH, DEC_BATCH, DEC_SEQ, PLE_DIM)),
        'norm_mix': gain(ks[9], (DEPTH, D_MODEL)),
        'w_in': nrm(ks[10], (DEPTH, D_MODEL, IN_WIDTH), D_MODEL ** -0.5),
        'g_qa': gain(ks[11], (DEPTH, HEAD_DIM)),
        'g_ka': gain(ks[12], (DEPTH, HEAD_DIM)),
        'proj_a': nrm(ks[13], (DEPTH, WIDTH_A, D_MODEL), WIDTH_A ** -0.5),
        'proj_b': nrm(ks[14], (DEPTH, WIDTH_B, D_MODEL), WIDTH_B ** -0.5),
        'w_out': nrm(ks[15], (DEPTH, D_MODEL, D_MODEL), D_MODEL ** -0.5),
        'norm_ffn': gain(ks[16], (DEPTH, D_MODEL)),
        'w_up': nrm(ks[17], (DEPTH, D_MODEL, D_FF), D_MODEL ** -0.5),
        'w_down': nrm(ks[18], (DEPTH, D_FF, D_MODEL), D_FF ** -0.5),
        'norm_ple': gain(ks[19], (DEPTH, D_MODEL)),
        'w_ple': nrm(ks[20], (DEPTH, PLE_DIM, D_MODEL), PLE_DIM ** -0.5),
        'w_ple_gate': nrm(ks[21], (DEPTH, D_MODEL, D_MODEL), D_MODEL ** -0.5),
    }


def reference(x_prompt, x_sample, cache_a_k, cache_a_v, cache_a_kidx, cache_b_k, cache_b_v,
              p_prompt, p_sample, norm_mix, w_in, g_qa, g_ka, proj_a, proj_b, w_out,
              norm_ffn, w_up, w_down, norm_ple, w_ple, w_ple_gate):
    seq = x_prompt.shape[1]
    dec_seq = x_sample.shape[1]
    past = cache_a_k.shape[2]
    pos_p = jnp.arange(seq, dtype=jnp.int32)
    pos_p_blocks = pos_p.reshape(seq // Q_BLOCK, Q_BLOCK)
    pos_s = past + jnp.arange(dec_seq, dtype=jnp.int32)
    n_sel_p = min(TOPK_MAX, seq // 4)
    n_sel_s = min(TOPK_MAX, (past + dec_seq) // 4)

    xp, xs = x_prompt, x_sample
    pak, pav, paki, pbk, pbv = [], [], [], [], []
    sak, sav, saki, sbk, sbv = [], [], [], [], []
    for i in range(DEPTH):
        h = rmsnorm(xp, norm_mix[i])
        qa, ka, va, qi, ki, wi, qb, kb, vb, ga, gb = mixer_inputs(h, pos_p, w_in[i], g_qa[i], g_ka[i])
        o_a = from_blocks(lax.map(
            lambda t: dsa_block(t[0], t[1], t[2], t[3], ka, va, ki, n_sel_p),
            (to_blocks(qa), to_blocks(qi), to_blocks(wi), pos_p_blocks)))
        o_b = from_blocks(lax.map(
            lambda t: stick_breaking_block(t[0], t[1], kb, vb),
            (to_blocks(qb), pos_p_blocks)))
        xp = xp + merge(o_a, o_b, ga, gb, proj_a[i], proj_b[i], w_out[i])
        xp = xp + channel_mlp(xp, norm_ffn[i], w_up[i], w_down[i])
        xp = xp + per_layer_embed(xp, p_prompt[i], norm_ple[i], w_ple[i], w_ple_gate[i])
        pak.append(ka); pav.append(va); paki.append(ki); pbk.append(kb); pbv.append(vb)

        h = rmsnorm(xs, norm_mix[i])
        qa, ka, va, qi, ki, wi, qb, kb, vb, ga, gb = mixer_inputs(h, pos_s, w_in[i], g_qa[i], g_ka[i])
        ka_all = jnp.concatenate([cache_a_k[i], ka], axis=1)
        va_all = jnp.concatenate([cache_a_v[i], va], axis=1)
        ki_all = jnp.concatenate([cache_a_kidx[i], ki], axis=1)
        kb_all = jnp.concatenate([cache_b_k[i], kb], axis=1)
        vb_all = jnp.concatenate([cache_b_v[i], vb], axis=1)
        o_a = dsa_block(qa, qi, wi, pos_s, ka_all, va_all, ki_all, n_sel_s)
        o_b = stick_breaking_block(qb, pos_s, kb_all, vb_all)
        xs = xs + merge(o_a, o_b, ga, gb, proj_a[i], proj_b[i], w_out[i])
        xs = xs + channel_mlp(xs, norm_ffn[i], w_up[i], w_down[i])
        xs = xs + per_layer_embed(xs, p_sample[i], norm_ple[i], w_ple[i], w_ple_gate[i])
        sak.append(ka); sav.append(va); saki.append(ki); sbk.append(kb); sbv.append(vb)

    new_a_k_p = jnp.stack(pak, 0)
    new_a_v_p = jnp.stack(pav, 0)
    new_a_kidx_p = jnp.stack(paki, 0)
    new_b_k_p = jnp.stack(pbk, 0)
    new_b_v_p = jnp.stack(pbv, 0)
    new_a_k_s = jnp.stack(sak, 0)
    new_a_v_s = jnp.stack(sav, 0)
    new_a_kidx_s = jnp.stack(saki, 0)
    new_b_k_s = jnp.stack(sbk, 0)
    new_b_v_s = jnp.stack(sbv, 0)
    return (xp, xs, new_a_k_p, new_a_v_p, new_a_kidx_p, new_b_k_p, new_b_v_p,
            new_a_k_s, new_a_v_s, new_a_kidx_s, new_b_k_s, new_b_v_s)
```

```python
import numpy as np
import ml_dtypes
import concourse.bass as bass
import concourse.mybir as mybir
from concourse.bass_utils import run_bass_kernel_spmd

F32 = mybir.dt.float32
BF16 = mybir.dt.bfloat16
AF = mybir.ActivationFunctionType
ALU = mybir.AluOpType
AX = mybir.AxisListType

EPOCH = 16000
ENGS = ("pe", "act", "dve", "pool", "sp")


class Sync:
    def __init__(self, nc):
        self.nc = nc
        self.cnt = {e: 0 for e in ENGS}
        self.eng_sems = {e: [] for e in ENGS}
        self.streams = {}

    def eng_sem(self, e, n):
        k = (n - 1) // EPOCH
        while len(self.eng_sems[e]) <= k:
            self.eng_sems[e].append(self.nc.alloc_semaphore(name=f"s_{e}_{len(self.eng_sems[e])}"))
        return self.eng_sems[e][k], n - k * EPOCH

    def stream(self, name):
        if name not in self.streams:
            self.streams[name] = [self.nc.alloc_semaphore(name=f"d_{len(self.streams)}"), 0]
        return self.streams[name]


class Op:
    __slots__ = ("eng", "fn", "reads", "writes", "dma", "deps", "need_inc", "semval", "idx")

    def __init__(self, eng, fn, reads, writes, dma):
        self.eng = eng
        self.fn = fn
        self.reads = reads
        self.writes = writes
        self.dma = dma
        self.deps = ()
        self.need_inc = dma is not None
        self.semval = 0


class Prog:
    def __init__(self, sync):
        self.sync = sync
        self.nc = sync.nc
        self.ops = []

    def add(self, eng, fn, reads=(), writes=(), dma=None):
        self.ops.append(Op(eng, fn, tuple(reads), tuple(writes), dma))

    def mm(self, out, lhsT, rhs, start, stop, reads, writes):
        self.add("pe", lambda e: e.matmul(out, lhsT, rhs, start=start, stop=stop), reads, writes)

    def dma(self, eng, out, in_, stream, reads, writes):
        self.add(eng, lambda e: e.dma_start(out=out, in_=in_), reads, writes, dma=stream)

    def act(self, out, in_, func, reads, writes, **kw):
        self.add("act", lambda e: e.activation(out, in_, func, **kw), reads, writes)

    def stt(self, out, in0, scalar, in1, op0, op1, reads, writes):
        self.add("dve", lambda e: e.scalar_tensor_tensor(out, in0, scalar, in1, op0, op1), reads, writes)

    def tt(self, eng, out, in0, in1, op, reads, writes):
        self.add(eng, lambda e: e.tensor_tensor(out, in0, in1, op), reads, writes)

    def ts(self, eng, out, in0, s1, s2, op0, op1, reads, writes, accum_out=None):
        if accum_out is None:
            if op1 is None:
                self.add(eng, lambda e: e.tensor_scalar(out, in0, s1, None, op0), reads, writes)
            else:
                self.add(eng, lambda e: e.tensor_scalar(out, in0, s1, s2, op0, op1), reads, writes)
        else:
            self.add(eng, lambda e: e.tensor_scalar(out, in0, s1, s2, op0, op1, accum_out=accum_out), reads, writes)

    def copy(self, eng, out, in_, reads, writes):
        if eng == "act":
            self.add(eng, lambda e: e.copy(out, in_), reads, writes)
        else:
            self.add(eng, lambda e: e.tensor_copy(out, in_), reads, writes)

    def memset(self, eng, out, val, reads, writes):
        self.add(eng, lambda e: e.memset(out, val), reads, writes)

    def run(self):
        sync = self.sync
        ops = self.ops
        last_write = {}
        readers = {}
        for i, op in enumerate(ops):
            op.idx = i
            deps = set()
            for r in op.reads:
                lw = last_write.get(r)
                if lw is not None:
                    deps.add(lw)
            for w in op.writes:
                lw = last_write.get(w)
                if lw is not None:
                    deps.add(lw)
                for rd in readers.get(w, ()):
                    deps.add(rd)
            deps.discard(i)
            if op.eng == "pe" and op.dma is None:
                deps = {d for d in deps if not (ops[d].eng == "pe" and ops[d].dma is None)}
            op.deps = tuple(sorted(deps))
            for d in op.deps:
                ops[d].need_inc = True
            for r in op.reads:
                readers.setdefault(r, []).append(i)
            for w in op.writes:
                last_write[w] = i
                readers[w] = []
        per_eng = {e: [] for e in ENGS}
        for op in ops:
            per_eng[op.eng].append(op)
        for e in ENGS:
            for op in reversed(per_eng[e]):
                if op.dma is None:
                    op.need_inc = True
                    break
        used_streams = []
        for op in ops:
            if op.dma is not None:
                st = sync.stream(op.dma)
                st[1] += 1
                op.semval = 16 * st[1]
                if op.dma not in used_streams:
                    used_streams.append(op.dma)
            elif op.need_inc:
                sync.cnt[op.eng] += 1
                op.semval = sync.cnt[op.eng]

        def sem_of(op):
            if op.dma is not None:
                return sync.streams[op.dma][0], op.semval
            return sync.eng_sem(op.eng, op.semval)

        barrier = []
        for e in ENGS:
            if sync.cnt[e] > 0:
                barrier.append(sync.eng_sem(e, sync.cnt[e]))
        for s in used_streams:
            st = sync.streams[s]
            barrier.append((st[0], 16 * st[1]))

        def run_eng(eng_name, e):
            waited = {}
            for op in per_eng[eng_name]:
                need = {}
                for d in op.deps:
                    sem, val = sem_of(ops[d])
                    key = id(sem)
                    if need.get(key, (None, 0))[1] < val:
                        need[key] = (sem, val)
                for key, (sem, val) in need.items():
                    if waited.get(key, 0) >= val:
                        continue
                    e.wait_ge(sem, val)
                    waited[key] = val
                ins = op.fn(e)
                if op.dma is not None:
                    sem, _ = sem_of(op)
                    ins.then_inc(sem, 16)
                elif op.need_inc:
                    sem, _ = sem_of(op)
                    ins.then_inc(sem, 1)
            for sem, val in barrier:
                if waited.get(id(sem), 0) >= val:
                    continue
                e.wait_ge(sem, val)

        with self.nc.Block() as block:
            @block.tensor
            def _(e):
                run_eng("pe", e)

            @block.scalar
            def _(e):
                run_eng("act", e)

            @block.vector
            def _(e):
                run_eng("dve", e)

            @block.gpsimd
            def _(e):
                run_eng("pool", e)

            @block.sync
            def _(e):
                run_eng("sp", e)
        self.ops = []


import os
import numpy as np
import ml_dtypes

D = 1024
SEQ = 16384
NB = 2
H = 8
HD = 64
NKV = 2
NIH = 8
IDD = 32
PLE = 256
DFF = 4096
PAST = 4096
DSEQ = 32
DBATCH = 32
THETA = 500000.0
EPS = 1e-6
OFF = dict(qa=0, ka=512, va=640, qi=768, ki=1024, wi=1056, qb=1064, kb=1576, vb=2088, ga=2600, gb=3624)
NOWN = 4224
NTILE = 9
TOPK = 256
NEG = -1.0e30

C_ONES, C_BD64, C_R128, C_TRI, C_ID, C_R32, C_LT = 0, 128, 256, 384, 512, 640, 672
NCONST = 800

STAGE = int(os.environ.get("KSTAGE", "9"))


def rope_tab(pos, d):
    r = d // 4
    hr = r // 2
    inv = np.power(np.float32(THETA), -np.arange(hr, dtype=np.float32) * np.float32(2.0 / r)).astype(np.float32)
    ang = pos.astype(np.float32)[:, None] * inv[None, :]
    cos = np.cos(ang).astype(np.float32)
    sin = np.sin(ang).astype(np.float32)
    C = np.ones((d, pos.shape[0]), np.float32)
    S = np.zeros((d, pos.shape[0]), np.float32)
    C[:hr] = cos.T
    C[hr:r] = cos.T
    S[:hr] = sin.T
    S[hr:r] = sin.T
    return C, S


def rot_lhsT(d):
    r = d // 4
    hr = r // 2
    m = np.zeros((d, d), np.float32)
    for i in range(hr):
        m[i + hr, i] = -1.0
        m[i, i + hr] = 1.0
    return m


def make_consts():
    c = np.zeros((128, NCONST), np.float32)
    c[:, C_ONES:C_ONES + 128] = 1.0
    c[0:64, C_BD64:C_BD64 + 64] = 1.0
    c[64:128, C_BD64 + 64:C_BD64 + 128] = 1.0
    r64 = rot_lhsT(64)
    c[0:64, C_R128:C_R128 + 64] = r64
    c[64:128, C_R128 + 64:C_R128 + 128] = r64
    k = np.arange(128)
    c[:, C_TRI:C_TRI + 128] = (k[:, None] >= k[None, :]).astype(np.float32)
    c[:, C_ID:C_ID + 128] = np.eye(128, dtype=np.float32)
    c[0:32, C_R32:C_R32 + 32] = rot_lhsT(32)
    c[:, C_LT:C_LT + 128] = (k[:, None] < k[None, :]).astype(np.float32)
    return c.astype(ml_dtypes.bfloat16)


def own_tiles(c):
    out = []
    for t in range(8):
        b, m = divmod(t, 4)
        g = 8 * m + c
        out.append((b, 512 * g))
    return out


def gain_cols(g):
    return np.ascontiguousarray(g.reshape(-1, 128).T).astype(np.float32)


class Rot:
    def __init__(self, tiles, name, keys=None):
        self.tiles = tiles
        self.name = name
        self.keys = keys
        self.i = 0

    def next(self):
        j = self.i % len(self.tiles)
        self.i += 1
        return self.tiles[j], (self.keys[j] if self.keys else f"{self.name}{j}")


class Cx:
    pass


def declare(nc):
    cx = Cx()
    cx.nc = nc
    cx.sync = Sync(nc)

    def din(name, shape, dt=F32):
        return nc.dram_tensor(name, list(shape), dt, kind="ExternalInput").ap()

    def dout(name, shape, dt=F32):
        return nc.dram_tensor(name, list(shape), dt, kind="ExternalOutput").ap()

    def dscr(name, shape, dt=BF16):
        return nc.dram_tensor(name, list(shape), dt, kind=("ExternalOutput" if os.environ.get("KDBG") else "Internal")).ap()

    cx.xT_all = din("xT_all", [NB, D, SEQ])
    cx.xT_own = din("xT_own", [D, NOWN])
    cx.pT_own = din("pT_own", [PLE, NOWN])
    cx.w_in = din("w_in", [D, 4648])
    cx.g_mix = din("g_mix", [128, 8])
    cx.g_ffn = din("g_ffn", [128, 8])
    cx.g_ple = din("g_ple", [128, 8])
    cx.g_ka = din("g_ka", [128, 1])
    cx.g_qa = din("g_qa", [64, 1])
    cx.consts = din("consts", [128, NCONST], BF16)
    cx.c64a = din("c64a", [128, SEQ])
    cx.s64a = din("s64a", [128, SEQ])
    cx.c32a = din("c32a", [32, SEQ])
    cx.s32a = din("s32a", [32, SEQ])
    cx.c64o = din("c64o", [128, NOWN])
    cx.s64o = din("s64o", [128, NOWN])
    cx.c32o = din("c32o", [32, NOWN])
    cx.s32o = din("s32o", [32, NOWN])
    cx.proj_a = din("proj_a", [512, D])
    cx.proj_b = din("proj_b", [512, D])
    cx.w_out = din("w_out", [D, D])
    cx.w_up = din("w_up", [D, DFF])
    cx.w_down = din("w_down", [DFF, D])
    cx.w_ple = din("w_ple", [PLE, D])
    cx.w_pg = din("w_pg", [D, D])
    cx.negb = din("negb", [4, 128, 4096], BF16)
    cx.maskb = din("maskb", [128, 32, 512], BF16)
    cx.cakT = din("cakT", [4, NKV, HD, PAST])
    cx.cav = din("cav", [4, PAST, NKV * HD])
    cx.ckiT = din("ckiT", [4, IDD, PAST])
    cx.cbkT = din("cbkT", [4, H, HD, PAST])
    cx.cbv = din("cbv", [4, PAST, H * HD])
    cx.yT = dout("yT", [D, NOWN])
    cx.o_kaT = dout("o_kaT", [128, NOWN])
    cx.o_va = dout("o_va", [NOWN, 128])
    cx.o_kiT = dout("o_kiT", [32, NOWN])
    cx.o_kbT = dout("o_kbT", [512, NOWN])
    cx.o_vb = dout("o_vb", [NOWN, 512])
    cx.KBT = dscr("KBT", [NB, H, HD, SEQ])
    cx.VBS = dscr("VBS", [NB, H, 128, 128, HD])
    cx.KAT = dscr("KAT", [NB, NKV, HD, SEQ])
    cx.VAS = dscr("VAS", [NB, NKV, 128, 128, HD])
    cx.KIT = dscr("KIT", [NB, IDD, SEQ])
    if os.environ.get("KDBG"):
        cx.dbg_ob = dout("dbg_ob", [64, 8, 512], BF16)
        cx.dbg_oa = dout("dbg_oa", [64, 8, 512], BF16)
        cx.dbg_q = dout("dbg_q", [64, 8, 512], BF16)
    cx.w_in_v = cx.w_in.rearrange("(k p) c -> p k c", p=128)
    cx.ps = [nc.alloc_psum_tensor(f"ps{i}", [128, 512], F32) for i in range(8)]
    return cx


def norm_ops(P, cx, xt, xkey, N, sq, lnv, rstd, hT, gcols, hkey, psb, tag):
    cst = cx.cst
    P.act(sq[:, :, 0:N], xt[:, :, 0:N], AF.Square, [xkey], [tag + "sq"])
    for k in range(8):
        P.mm(cx.ps[psb][:, 0:N], cst[:, C_ONES:C_ONES + 128], sq[:, k, 0:N], k == 0, k == 7,
             [tag + "sq", "cst"], [f"ps{psb}"])
    P.act(lnv[:, 0:N], cx.ps[psb][:, 0:N], AF.Ln, [f"ps{psb}"], [tag + "lnv"], bias=EPS, scale=1.0 / D)
    P.act(rstd[:, 0:N], lnv[:, 0:N], AF.Exp, [tag + "lnv"], [tag + "rstd"], scale=-0.5)
    for k in range(8):
        P.stt(hT[:, k, 0:N], xt[:, k, 0:N], gcols[:, k:k + 1], rstd[:, 0:N], ALU.mult, ALU.mult,
              [xkey, tag + "rstd", "gains"], [f"{hkey}{k}"])


def rope_ops(P, cx, np_, N, src_f32, src_key, srcb, rlhsT, psb, rc, rs, rkey, t1, t2, out, out_key, tag):
    P.copy("pool", srcb[0:np_, 0:N], src_f32[0:np_, 0:N], [src_key], [tag + "srcb"])
    P.mm(cx.ps[psb][0:np_, 0:N], rlhsT, srcb[0:np_, 0:N], True, True, [tag + "srcb", "cst"], [f"ps{psb}"])
    P.tt("dve", t1[0:np_, 0:N], src_f32[0:np_, 0:N], rc[0:np_, 0:N], ALU.mult, [src_key, rkey], [tag + "t1"])
    P.tt("dve", t2[0:np_, 0:N], cx.ps[psb][0:np_, 0:N], rs[0:np_, 0:N], ALU.mult, [f"ps{psb}", rkey], [tag + "t2"])
    P.tt("pool", out, t1[0:np_, 0:N], t2[0:np_, 0:N], ALU.add, [tag + "t1", tag + "t2"], [out_key])


def headnorm_ops(P, cx, np_, N, psrc, pkey, ka_sb, ksq, bdl, psb, klnv, krstd, gcol, kn, tag, ebias=None):
    P.copy("act", ka_sb[0:np_, 0:N], psrc, [pkey], [tag + "ka_sb"])
    P.act(ksq[0:np_, 0:N], psrc, AF.Square, [pkey], [tag + "ksq"])
    P.mm(cx.ps[psb][0:np_, 0:N], bdl, ksq[0:np_, 0:N], True, True, [tag + "ksq", "cst"], [f"ps{psb}"])
    P.act(klnv[0:np_, 0:N], cx.ps[psb][0:np_, 0:N], AF.Ln, [f"ps{psb}"], [tag + "klnv"], bias=EPS, scale=1.0 / HD)
    if ebias is None:
        P.act(krstd[0:np_, 0:N], klnv[0:np_, 0:N], AF.Exp, [tag + "klnv"], [tag + "krstd"], scale=-0.5)
    else:
        P.act(krstd[0:np_, 0:N], klnv[0:np_, 0:N], AF.Exp, [tag + "klnv", "gains"], [tag + "krstd"], scale=-0.5, bias=ebias)
    P.stt(kn[0:np_, 0:N], ka_sb[0:np_, 0:N], gcol, krstd[0:np_, 0:N], ALU.mult, ALU.mult,
          [tag + "ka_sb", tag + "krstd", "gains"], [tag + "kn"])


def phase1(cx, ntiles=64):
    from contextlib import ExitStack
    nc = cx.nc
    cst = cx.cst
    with ExitStack() as es:
        def sb(name, shape, dt):
            return es.enter_context(nc.sbuf_tensor("p1_" + name, shape, dt))
        wkv = sb("wkv", [128, 8, 1312], BF16)
        xts = Rot([sb(f"xt{i}", [128, 8, 512], F32) for i in range(2)], "xt")
        sqs = [sb(f"sq{i}", [128, 8, 512], BF16) for i in range(2)]
        lnvs = [sb(f"lnv{i}", [128, 512], F32) for i in range(2)]
        rstds = [sb(f"rstd{i}", [128, 512], F32) for i in range(2)]
        hTs = [sb(f"hT{i}", [128, 8, 512], BF16) for i in range(2)]
        ksts = Rot([sb(f"kst{i}", [128, 512], BF16) for i in range(3)], "kst")
        vbss = Rot([sb(f"vbs{i}", [128, 8, 4, 64], BF16) for i in range(2)], "vbs")
        vass = Rot([sb(f"vas{i}", [128, 2, 4, 64], BF16) for i in range(2)], "vas")
        ka_sb = sb("ka_sb", [128, 512], F32)
        ksq = sb("ksq", [128, 512], BF16)
        klnv = sb("klnv", [128, 512], F32)
        krstd = sb("krstd", [128, 512], F32)
        kn = sb("kn", [128, 512], F32)
        knb = sb("knb", [128, 512], BF16)
        t1 = sb("t1", [128, 512], F32)
        t2 = sb("t2", [128, 512], F32)
        rcs = Rot([sb(f"rc{i}", [128, 512], F32) for i in range(2)], "rc")
        rss = Rot([sb(f"rs{i}", [128, 512], F32) for i in range(2)], "rs")
        kaos = Rot([sb(f"kao{i}", [128, 512], BF16) for i in range(2)], "kao")
        ki_sb = sb("ki_sb", [32, 512], F32)
        kib = sb("kib", [32, 512], BF16)
        t1i = sb("t1i", [32, 512], F32)
        t2i = sb("t2i", [32, 512], F32)
        rcis = Rot([sb(f"rci{i}", [32, 512], F32) for i in range(2)], "rci")
        rsis = Rot([sb(f"rsi{i}", [32, 512], F32) for i in range(2)], "rsi")
        kios = Rot([sb(f"kio{i}", [32, 512], BF16) for i in range(2)], "kio")

        P = Prog(cx.sync)
        P.memset("pool", cx.oaT[64:128, :, :], 0.0, [], ["oaTpad"])
        P.memset("pool", cx.obT[64:128, :, :], 0.0, [], ["obTpad"])
        WC = dict(kb=0, vb=512, ka=1024, va=1152, ki=1280)
        WN = dict(kb=512, vb=512, ka=128, va=128, ki=32)
        for nm in WC:
            P.dma("pool", wkv[:, :, WC[nm]:WC[nm] + WN[nm]], cx.w_in_v[:, :, OFF[nm]:OFF[nm] + WN[nm]],
                  "p1w_" + nm, [], ["w_" + nm])
        kps = Rot([cx.ps[1], cx.ps[2]], "psk")
        vps = Rot([cx.ps[3], cx.ps[4]], "psv")
        for ti in range(ntiles):
            b, tt_ = divmod(ti, 32)
            t0 = tt_ * 512
            if ti == 0:
                pend = xts.next()
                P.dma("sp", pend[0][:], cx.xT_all[b].rearrange("(k p) t -> p k t", p=128)[:, :, t0:t0 + 512], pend[1], [],
                      [pend[1]])
            xt, xkey = pend
            rc, rck = rcs.next()
            rs, rsk = rss.next()
            rci, rcik = rcis.next()
            rsi, rsik = rsis.next()
            P.dma("sp", rc[:], cx.c64a[:, t0:t0 + 512], rck, [], [rck])
            P.dma("sp", rs[:], cx.s64a[:, t0:t0 + 512], rsk, [], [rsk])
            P.dma("sp", rci[:], cx.c32a[:, t0:t0 + 512], rcik, [], [rcik])
            P.dma("sp", rsi[:], cx.s32a[:, t0:t0 + 512], rsik, [], [rsik])
            if ti + 1 < ntiles:
                b2, tt2 = divmod(ti + 1, 32)
                pend = xts.next()
                P.dma("sp", pend[0][:], cx.xT_all[b2].rearrange("(k p) t -> p k t", p=128)[:, :, tt2 * 512:(tt2 + 1) * 512],
                      pend[1], [], [pend[1]])
            pp = ti % 2
            hT = hTs[pp]
            norm_ops(P, cx, xt, xkey, 512, sqs[pp], lnvs[pp], rstds[pp], hT, cx.g_mix_t, f"hT{pp}_", 0, f"p1{pp}")
            hkeys = [f"hT{pp}_{k}" for k in range(8)]
            for i in range(4):
                pk, pkk = kps.next()
                for k in range(8):
                    P.mm(pk[:], wkv[:, k, WC["kb"] + i * 128: WC["kb"] + (i + 1) * 128], hT[:, k, :], k == 0, k == 7,
                         [hkeys[k], "w_kb"], [pkk])
                kst, kstk = ksts.next()
                P.copy("act", kst[:], pk[:], [pkk], [kstk])
                P.dma("sp", cx.KBT[b, 2 * i:2 * i + 2].rearrange("h d t -> (h d) t")[:, t0:t0 + 512], kst[:], kstk,
                      [kstk], [])
            pka, pkak = kps.next()
            for k in range(8):
                P.mm(pka[:], wkv[:, k, WC["ka"]:WC["ka"] + 128], hT[:, k, :], k == 0, k == 7, [hkeys[k], "w_ka"], [pkak])
            P.copy("act", ka_sb[:], pka[:], [pkak], ["p1kaka_sb"])
            P.act(ksq[:], pka[:], AF.Square, [pkak], ["p1kaksq"])
            pki, pkik = kps.next()
            for k in range(8):
                P.mm(pki[0:32, :], wkv[:, k, WC["ki"]:WC["ki"] + 32], hT[:, k, :], k == 0, k == 7, [hkeys[k], "w_ki"], [pkik])
            P.copy("act", ki_sb[:], pki[0:32, :], [pkik], ["p1ki_sb"])
            P.copy("pool", kib[:], ki_sb[:], ["p1ki_sb"], ["p1kib"])
            vbs, vbsk = vbss.next()
            vas, vask = vass.next()
            for j in range(4):
                pv, pvk = vps.next()
                for k in range(8):
                    P.mm(pv[:], hT[:, k, j * 128:(j + 1) * 128], wkv[:, k, WC["vb"]:WC["vb"] + 512], k == 0, k == 7,
                         [hkeys[k], "w_vb"], [pvk])
                P.copy("dve", vbs[:, :, j, :], pv[:].rearrange("p (h d) -> p h d", h=8), [pvk], [vbsk])
                pv, pvk = vps.next()
                for k in range(8):
                    P.mm(pv[:, 0:128], hT[:, k, j * 128:(j + 1) * 128], wkv[:, k, WC["va"]:WC["va"] + 128], k == 0, k == 7,
                         [hkeys[k], "w_va"], [pvk])
                P.copy("dve", vas[:, :, j, :], pv[:, 0:128].rearrange("p (h d) -> p h d", h=2), [pvk], [vask])
            P.mm(cx.ps[5][:], cst[:, C_BD64:C_BD64 + 128], ksq[:], True, True, ["p1kaksq", "cst"], ["ps5"])
            P.act(klnv[:], cx.ps[5][:], AF.Ln, ["ps5"], ["p1kaklnv"], bias=EPS, scale=1.0 / HD)
            P.act(krstd[:], klnv[:], AF.Exp, ["p1kaklnv"], ["p1kakrstd"], scale=-0.5)
            P.stt(kn[:], ka_sb[:], cx.g_ka_t[:, 0:1], krstd[:], ALU.mult, ALU.mult, ["p1kaka_sb", "p1kakrstd", "gains"],
                  ["p1kakn"])
            kao, kaok = kaos.next()
            P.copy("pool", knb[:], kn[:], ["p1kakn"], ["p1kasrcb"])
            P.mm(cx.ps[6][:], cst[:, C_R128:C_R128 + 128], knb[:], True, True, ["p1kasrcb", "cst"], ["ps6"])
            P.tt("dve", t1[:], kn[:], rc[:], ALU.mult, ["p1kakn", rck], ["p1t1"])
            P.tt("dve", t2[:], cx.ps[6][:], rs[:], ALU.mult, ["ps6", rsk], ["p1t2"])
            P.tt("pool", kao[:], t1[:], t2[:], ALU.add, ["p1t1", "p1t2"], [kaok])
            P.dma("sp", cx.KAT[b].rearrange("h d t -> (h d) t")[:, t0:t0 + 512], kao[:], kaok, [kaok], [])
            P.mm(cx.ps[7][0:32, :], cst[0:32, C_R32:C_R32 + 32], kib[:], True, True, ["p1kib", "cst"], ["ps7"])
            kio, kiok = kios.next()
            P.tt("dve", t1i[:], ki_sb[:], rci[:], ALU.mult, ["p1ki_sb", rcik], ["p1t1i"])
            P.tt("dve", t2i[:], cx.ps[7][0:32, :], rsi[:], ALU.mult, ["ps7", rsik], ["p1t2i"])
            P.tt("pool", kio[:], t1i[:], t2i[:], ALU.add, ["p1t1i", "p1t2i"], [kiok])
            P.dma("sp", cx.KIT[b][:, t0:t0 + 512], kio[:], kiok, [kiok], [])
            blk0 = t0 // 128
            P.dma("sp", cx.VBS[b, :, :, blk0:blk0 + 4, :].rearrange("h p j d -> p h j d"), vbs[:], vbsk, [vbsk], [])
            P.dma("sp", cx.VAS[b, :, :, blk0:blk0 + 4, :].rearrange("h p j d -> p h j d"), vas[:], vask, [vask], [])
        P.run()


def load_consts(cx, es):
    nc = cx.nc

    def sb(name, shape, dt):
        return es.enter_context(nc.sbuf_tensor("c_" + name, shape, dt))
    cx.cst = sb("cst", [128, NCONST], BF16)
    cx.g_mix_t = sb("g_mix", [128, 8], F32)
    cx.g_ffn_t = sb("g_ffn", [128, 8], F32)
    cx.g_ple_t = sb("g_ple", [128, 8], F32)
    cx.g_ka_t = sb("g_ka", [128, 1], F32)
    cx.g_qa_t = sb("g_qa", [64, 1], F32)
    P = Prog(cx.sync)
    P.dma("sp", cx.cst[:], cx.consts, "c0", [], ["cst"])
    P.dma("sp", cx.g_mix_t[:], cx.g_mix, "c1", [], ["gains"])
    P.dma("sp", cx.g_ffn_t[:], cx.g_ffn, "c2", [], ["gains"])
    P.dma("sp", cx.g_ple_t[:], cx.g_ple, "c3", [], ["gains"])
    P.dma("sp", cx.g_ka_t[:], cx.g_ka, "c4", [], ["gains"])
    P.dma("sp", cx.g_qa_t[:], cx.g_qa, "c5", [], ["gains"])
    P.run()


def build(stage=STAGE, p1_tiles=64, tiles=range(9)):
    from contextlib import ExitStack
    nc = bass.Bass("TRN2", target_bir_lowering=False)
    cx = declare(nc)
    with ExitStack() as es:
        load_consts(cx, es)
        alloc_persist(cx, es)
        phase1(cx, p1_tiles)
        if stage >= 2:
            for ti in tiles:
                with ExitStack() as es_t:
                    alloc_q(cx, es_t, ti)
                    proj_stage(cx, ti)
                    if stage >= 3 and ti < 8:
                        battn_prompt(cx, ti)
                    if stage >= 4 and ti < 8:
                        aattn_prompt(cx, ti)
                    if stage >= 4 and ti == 8:
                        attn_sample(cx)
                if stage >= 5:
                    stage4(cx, ti)
    return nc


def host_shared(inp):
    sh = {}
    sh["xT_all"] = np.ascontiguousarray(inp["x_prompt"].transpose(0, 2, 1))
    sh["pT_all"] = np.ascontiguousarray(inp["p_prompt"][0].transpose(0, 2, 1))
    sh["consts"] = make_consts()
    pos = np.arange(SEQ)
    c64, s64 = rope_tab(pos, 64)
    c32, s32 = rope_tab(pos, 32)
    sh["c64a"] = np.ascontiguousarray(np.concatenate([c64, c64], 0))
    sh["s64a"] = np.ascontiguousarray(np.concatenate([s64, s64], 0))
    sh["c32a"] = c32
    sh["s32a"] = s32
    sh["tabs"] = (c64, s64, c32, s32)
    poss = PAST + np.arange(DSEQ)
    sh["tabs_s"] = rope_tab(poss, 64) + rope_tab(poss, 32)
    for nm in ("w_in", "proj_a", "proj_b", "w_out", "w_up", "w_down", "w_ple"):
        sh[nm] = np.ascontiguousarray(inp[nm][0])
    sh["w_pg"] = np.ascontiguousarray(inp["w_ple_gate"][0])
    sh["g_mix"] = gain_cols(inp["norm_mix"][0])
    sh["g_ffn"] = gain_cols(inp["norm_ffn"][0])
    sh["g_ple"] = gain_cols(inp["norm_ple"][0])
    sh["g_ka"] = np.ascontiguousarray(np.tile(inp["g_ka"][0], 2)[:, None]).astype(np.float32)
    sh["g_qa"] = np.ascontiguousarray(inp["g_qa"][0][:, None]).astype(np.float32)
    return sh


def host_core(c, inp, sh):
    m = {}
    for nm in ("xT_all", "consts", "c64a", "s64a", "c32a", "s32a", "w_in", "proj_a", "proj_b", "w_out", "w_up",
               "w_down", "w_ple", "w_pg", "g_mix", "g_ffn", "g_ple", "g_ka", "g_qa"):
        m[nm] = sh[nm]
    tiles = own_tiles(c)
    xo = np.empty((D, NOWN), np.float32)
    po = np.empty((PLE, NOWN), np.float32)
    c64, s64, c32, s32 = sh["tabs"]
    c64o = np.empty((128, NOWN), np.float32)
    s64o = np.empty((128, NOWN), np.float32)
    c32o = np.empty((32, NOWN), np.float32)
    s32o = np.empty((32, NOWN), np.float32)
    for t, (b, q0) in enumerate(tiles):
        sl = slice(t * 512, (t + 1) * 512)
        xo[:, sl] = sh["xT_all"][b][:, q0:q0 + 512]
        po[:, sl] = sh["pT_all"][b][:, q0:q0 + 512]
        c64o[0:64, sl] = c64[:, q0:q0 + 512]
        s64o[0:64, sl] = s64[:, q0:q0 + 512]
        c32o[:, sl] = c32[:, q0:q0 + 512]
        s32o[:, sl] = s32[:, q0:q0 + 512]
    sl = slice(4096, NOWN)
    xo[:, sl] = inp["x_sample"][4 * c:4 * c + 4].reshape(128, D).T
    po[:, sl] = inp["p_sample"][0][4 * c:4 * c + 4].reshape(128, PLE).T
    cs64, ss64, cs32, ss32 = sh["tabs_s"]
    c64o[0:64, sl] = np.tile(cs64, (1, 4))
    s64o[0:64, sl] = np.tile(ss64, (1, 4))
    c32o[:, sl] = np.tile(cs32, (1, 4))
    s32o[:, sl] = np.tile(ss32, (1, 4))
    c64o[64:128] = c64o[0:64]
    s64o[64:128] = s64o[0:64]
    m["xT_own"] = xo
    m["pT_own"] = po
    m["c64o"], m["s64o"], m["c32o"], m["s32o"] = c64o, s64o, c32o, s32o
    kz = np.arange(4096)[None, :]
    q = np.arange(128)[:, None]
    negb = np.empty((4, 128, 4096), np.float32)
    for j in range(4):
        lim = 512 * c + 128 * j + (q // 64 + 1) * 64
        negb[j] = np.where(kz < lim, 0.0, NEG)
    m["negb"] = negb.astype(ml_dtypes.bfloat16)
    kk = np.arange(128)[:, None, None]
    bl = np.arange(32)[None, :, None]
    qq = np.arange(512)[None, None, :]
    m["maskb"] = ((128 * bl + kk) < (512 * c + qq)).astype(np.float32).astype(ml_dtypes.bfloat16)
    sq_ = slice(4 * c, 4 * c + 4)
    m["cakT"] = np.ascontiguousarray(inp["cache_a_k"][0][sq_].transpose(0, 2, 3, 1))
    m["cav"] = np.ascontiguousarray(inp["cache_a_v"][0][sq_].reshape(4, PAST, NKV * HD))
    m["ckiT"] = np.ascontiguousarray(inp["cache_a_kidx"][0][sq_].transpose(0, 2, 1))
    m["cbkT"] = np.ascontiguousarray(inp["cache_b_k"][0][sq_].transpose(0, 2, 3, 1))
    m["cbv"] = np.ascontiguousarray(inp["cache_b_v"][0][sq_].reshape(4, PAST, H * HD))
    return m


import math


def alloc_persist(cx, es):
    nc = cx.nc

    def sb(name, shape, dt):
        return es.enter_context(nc.sbuf_tensor("pp_" + name, shape, dt))
    cx.hT = sb("hT", [128, 8, 512], BF16)
    cx.oaT = sb("oaT", [128, 8, 512], BF16)
    cx.obT = sb("obT", [128, 8, 512], BF16)
    cx.ones_f = sb("ones_f", [128, 64], F32)
    cx.kbnT = sb("kbnT", [64, 8, 128], BF16)
    cx.kanT = sb("kanT", [64, 2, 128], BF16)
    cx.kinT = sb("kinT", [32, 128], BF16)
    cx.vbn = sb("vbn", [32, 4, 512], BF16)
    cx.van = sb("van", [32, 4, 2, 65], BF16)


def alloc_q(cx, es, ti):
    nc = cx.nc

    def sb(name, shape, dt):
        return es.enter_context(nc.sbuf_tensor(f"q{ti}_" + name, shape, dt))
    cx.qbT = sb("qbT", [128, 8, 512], BF16)
    cx.qaT = sb("qaT", [128, 8, 512], BF16)
    cx.qiT = sb("qiT", [128, 8, 512], BF16)
    cx.wiT = sb("wiT", [128, 4, 8], F32)


def proj_stage(cx, ti):
    from contextlib import ExitStack
    nc = cx.nc
    cst = cx.cst
    sample = ti == 8
    N = 128 if sample else 512
    c0 = ti * 512
    gs = 32 if sample else 128
    ng = N // gs
    with ExitStack() as es:
        def sb(name, shape, dt):
            return es.enter_context(nc.sbuf_tensor(f"pj{ti}_" + name, shape, dt))
        wq = sb("wq", [128, 8, 2600], BF16)
        xt = sb("xt", [128, 8, 512], F32)
        sq = sb("sq", [128, 8, 512], BF16)
        lnv = sb("lnv", [128, 512], F32)
        rstd = sb("rstd", [128, 512], F32)
        rc64 = sb("rc64", [128, 512], F32)
        rs64 = sb("rs64", [128, 512], F32)
        rc32 = sb("rc32", [32, 512], F32)
        rs32 = sb("rs32", [32, 512], F32)
        ka_sb = sb("ka_sb", [64, 512], F32)
        ksq = sb("ksq", [64, 512], BF16)
        klnv = sb("klnv", [64, 512], F32)
        krstd = sb("krstd", [64, 512], F32)
        kn = sb("kn", [64, 512], F32)
        knb = sb("knb", [64, 512], BF16)
        t1 = sb("t1", [64, 512], F32)
        t2 = sb("t2", [64, 512], F32)
        kouts = Rot([sb(f"kout{i}", [64, 512], F32) for i in range(2)], "kout")
        vouts = Rot([sb(f"vout{i}", [128, 640], F32) for i in range(2)], "vout")
        P = Prog(cx.sync)
        for i, (a, b_) in enumerate([(0, 650), (650, 1300), (1300, 1950), (1950, 2600)]):
            P.dma("pool", wq[:, :, a:b_], cx.w_in_v[:, :, a:b_], f"pjw{i}", [], [f"wq{i}"])
        P.dma("sp", xt[:, :, 0:N], cx.xT_own.rearrange("(k p) t -> p k t", p=128)[:, :, c0:c0 + N], "pjx", [], ["xt"])
        P.dma("sp", rc64[:, 0:N], cx.c64o[:, c0:c0 + N], "pjr0", [], ["rt_c64"])
        P.dma("sp", rs64[:, 0:N], cx.s64o[:, c0:c0 + N], "pjr1", [], ["rt_s64"])
        P.dma("sp", rc32[:, 0:N], cx.c32o[:, c0:c0 + N], "pjr2", [], ["rt_c32"])
        P.dma("sp", rs32[:, 0:N], cx.s32o[:, c0:c0 + N], "pjr3", [], ["rt_s32"])
        P.memset("pool", cx.ones_f[:], 1.0, [], ["ones_f"])
        P.memset("pool", cx.qiT[32:64, :, :], 0.0, [], ["qiTpad"])
        P.memset("pool", cx.qiT[64:128, :, :], 0.0, [], ["qiTpad"])
        P.memset("pool", cx.qbT[64:128, :, :], 0.0, [], ["qbTpad"])
        P.memset("pool", cx.qaT[64:128, :, :], 0.0, [], ["qaTpad"])
        norm_ops(P, cx, xt, "xt", N, sq, lnv, rstd, cx.hT, cx.g_mix_t, "hT", 0, "pj")
        hkeys = [f"hT{k}" for k in range(8)]
        pss = Rot([cx.ps[1], cx.ps[2], cx.ps[3], cx.ps[4]], "psj")

        def proj_fm(col, M):
            pk, pkk = pss.next()
            for k in range(8):
                P.mm(pk[:, 0:N], wq[:, k, col:col + 128], cx.hT[:, k, 0:N], k == 0, k == 7, [hkeys[k], "wq0", "wq1", "wq2", "wq3"], [pkk])
            return pk, pkk

        def rope(np_, src, skey, rl, rc, rs, out, okey, tag):
            P.copy("pool", knb[0:np_, 0:N], src[0:np_, 0:N], [skey], [tag + "knb"])
            P.mm(cx.ps[6][0:np_, 0:N], rl, knb[0:np_, 0:N], True, True, [tag + "knb", "cst"], ["ps6"])
            P.tt("dve", t1[0:np_, 0:N], src[0:np_, 0:N], rc[0:np_, 0:N], ALU.mult, [skey, "rt_c64" if np_ == 64 else "rt_c32"], [tag + "t1"])
            P.tt("dve", t2[0:np_, 0:N], cx.ps[6][0:np_, 0:N], rs[0:np_, 0:N], ALU.mult, ["ps6", "rt_s64" if np_ == 64 else "rt_s32"], [tag + "t2"])
            P.tt("pool", out, t1[0:np_, 0:N], t2[0:np_, 0:N], ALU.add, [tag + "t1", tag + "t2"], [okey])

        r64l = cst[0:64, C_R128:C_R128 + 64]
        r32l = cst[0:32, C_R32:C_R32 + 32]
        ones64 = cst[0:64, C_ONES:C_ONES + 64]
        for h in range(8):
            pk, pkk = proj_fm(OFF["qb"] + 64 * h, 64)
            P.add("act", (lambda o, i: lambda e: e.mul(o, i, 0.125))(cx.qbT[0:64, h, 0:N], pk[0:64, 0:N]), [pkk], [f"qbT{h}"])
        for h in range(8):
            pk, pkk = proj_fm(OFF["qa"] + 64 * h, 64)
            headnorm_ops(P, cx, 64, N, pk[0:64, 0:N], pkk, ka_sb, ksq, ones64, 5, klnv, krstd, cx.g_qa_t[:, 0:1], kn,
                         "pj", ebias=math.log(0.125))
            rope(64, kn, "pjkn", r64l, rc64, rs64, cx.qaT[0:64, h, 0:N], f"qaT{h}", "pj")
        for h in range(8):
            pk, pkk = proj_fm(OFF["qi"] + 32 * h, 32)
            P.add("act", (lambda o, i: lambda e: e.mul(o, i, IDD ** -0.5))(kn[0:32, 0:N], pk[0:32, 0:N]), [pkk], ["pjkn"])
            rope(32, kn, "pjkn", r32l, rc32, rs32, cx.qiT[0:32, h, 0:N], f"qiT{h}", "pj")
        for jb in range(ng):
            pk, pkk = pss.next()
            for k in range(8):
                P.mm(pk[0:gs, 0:8], cx.hT[:, k, jb * gs:(jb + 1) * gs], wq[:, k, OFF["wi"]:OFF["wi"] + 8], k == 0, k == 7,
                     [hkeys[k], "wq0", "wq1", "wq2", "wq3"], [pkk])
            P.ts("dve", cx.wiT[0:gs, jb, :], pk[0:gs, 0:8], NIH ** -0.5, None, ALU.mult, None, [pkk], ["wiT"])
        for h in range(8):
            pk, pkk = proj_fm(OFF["kb"] + 64 * h, 64)
            ko, kok = kouts.next()
            P.copy("act", ko[0:64, 0:N], pk[0:64, 0:N], [pkk], [kok])
            P.dma("sp", cx.o_kbT[64 * h:64 * h + 64, c0:c0 + N], ko[0:64, 0:N], kok, [kok], [])
            if sample:
                P.copy("pool", cx.kbnT[0:64, h, :], ko[0:64, 0:N], [kok], [f"kbnT{h}"])
        for kv in range(2):
            pk, pkk = proj_fm(OFF["ka"] + 64 * kv, 64)
            headnorm_ops(P, cx, 64, N, pk[0:64, 0:N], pkk, ka_sb, ksq, ones64, 5, klnv, krstd, cx.g_ka_t[0:64, 0:1], kn,
                         "pj")
            ko, kok = kouts.next()
            rope(64, kn, "pjkn", r64l, rc64, rs64, ko[0:64, 0:N], kok, "pj")
            P.dma("sp", cx.o_kaT[64 * kv:64 * kv + 64, c0:c0 + N], ko[0:64, 0:N], kok, [kok], [])
            if sample:
                P.copy("pool", cx.kanT[0:64, kv, :], ko[0:64, 0:N], [kok], [f"kanT{kv}"])
        pk, pkk = proj_fm(OFF["ki"], 32)
        P.copy("act", kn[0:32, 0:N], pk[0:32, 0:N], [pkk], ["pjkn"])
        ko, kok = kouts.next()
        rope(32, kn, "pjkn", r32l, rc32, rs32, ko[0:32, 0:N], kok, "pj")
        P.dma("sp", cx.o_kiT[:, c0:c0 + N], ko[0:32, 0:N], kok, [kok], [])
        if sample:
            P.copy("pool", cx.kinT[0:32, :], ko[0:32, 0:N], [kok], ["kinT"])
        for g in range(ng):
            pv, pvk = pss.next()
            pa, pak = pss.next()
            for k in range(8):
                P.mm(pv[0:gs, :], cx.hT[:, k, g * gs:(g + 1) * gs], wq[:, k, OFF["vb"]:OFF["vb"] + 512], k == 0, k == 7,
                     [hkeys[k], "wq0", "wq1", "wq2", "wq3"], [pvk])
            for k in range(8):
                P.mm(pa[0:gs, 0:128], cx.hT[:, k, g * gs:(g + 1) * gs], wq[:, k, OFF["va"]:OFF["va"] + 128], k == 0, k == 7,
                     [hkeys[k], "wq0", "wq1", "wq2", "wq3"], [pak])
            vo, vok = vouts.next()
            P.copy("dve", vo[0:gs, 0:512], pv[0:gs, :], [pvk], [vok + "b"])
            P.copy("dve", vo[0:gs, 512:640], pa[0:gs, 0:128], [pak], [vok + "a"])
            P.dma("sp", cx.o_vb[c0 + g * gs:c0 + (g + 1) * gs, :], vo[0:gs, 0:512], vok + "b", [vok + "b"], [])
            P.dma("sp", cx.o_va[c0 + g * gs:c0 + (g + 1) * gs, :], vo[0:gs, 512:640], vok + "a", [vok + "a"], [])
            if sample:
                P.copy("pool", cx.vbn[0:32, g, :], vo[0:32, 0:512], [vok + "b"], [f"vbn{g}"])
                P.copy("pool", cx.van[0:32, g, :, 0:64], vo[0:32, 512:640].rearrange("p (h d) -> p h d", h=2),
                       [vok + "a"], [f"van{g}"])
                P.memset("pool", cx.van[0:32, g, :, 64:65], 1.0, [], [f"van1{g}"])
        P.run()


def battn_prompt(cx, ti):
    from contextlib import ExitStack
    nc = cx.nc
    cst = cx.cst
    b, m = divmod(ti, 4)
    nkb = 32 * m + 32
    z0 = 32 * m
    with ExitStack() as es:
        def sb(name, shape, dt):
            return es.enter_context(nc.sbuf_tensor(f"ba{ti}_" + name, shape, dt))
        mk = sb("mk", [128, 32, 512], BF16)
        kch_t = [sb(f"kch{i}", [128, 2048], BF16) for i in range(3)]
        vch_t = [sb(f"vch{i}", [128, 16, 128], BF16) for i in range(3)]
        kchs = Rot(kch_t, "kch")
        vchs = Rot(vch_t, "vch")
        e2s = Rot([sb(f"e2{i}", [128, 512], F32) for i in range(4)], "e2")
        nlks = Rot([sb(f"nlk{i}", [128, 512], BF16) for i in range(3)], "nlk")
        ggs = Rot([sb(f"gg{i}", [128, 512], F32) for i in range(2)], "gg")
        aas = Rot([sb(f"aa{i}", [128, 512], BF16) for i in range(3)], "aa")
        Ss = [sb(f"S{i}", [128, 512], BF16) for i in range(2)]
        P = Prog(cx.sync)
        P.dma("sp", mk[:], cx.maskb, "bamk", [], ["mk"])
        for i in range(3):
            P.memset("pool", kch_t[i][64:128, :], 0.0, [], [f"kch{i}pad"])
            P.memset("pool", vch_t[i][:, :, 64:128], 0.0, [], [f"vch{i}pad"])
        pzs = Rot([cx.ps[0], cx.ps[1]], "pz")
        pcs = Rot([cx.ps[2], cx.ps[3]], "pc")
        pos_ = [cx.ps[4], cx.ps[5]]
        tri = cst[:, C_TRI:C_TRI + 128]
        ones = cst[:, C_ONES:C_ONES + 128]
        steps = []
        for h in range(8):
            for kc in reversed(range(nkb // 16)):
                for bi in reversed(range(16)):
                    steps.append(dict(h=h, kc=kc, bi=bi, blk=kc * 16 + bi,
                                      first=(kc == nkb // 16 - 1 and bi == 15), last=(kc == 0 and bi == 0)))
        n = len(steps)
        chunk_of = {}
        order = []
        for st in steps:
            key = (st["h"], st["kc"])
            if key not in chunk_of:
                chunk_of[key] = None
                order.append(key)

        def load_chunk(idx):
            h, kc = order[idx]
            kch, kchk = kchs.next()
            vch, vchk = vchs.next()
            P.dma("sp", kch[0:64, :], cx.KBT[b, h, :, kc * 2048:(kc + 1) * 2048], kchk, [], [kchk])
            P.dma("sp", vch[:, :, 0:64], cx.VBS[b, h, :, kc * 16:(kc + 1) * 16, :], vchk, [], [vchk])
            chunk_of[(h, kc)] = (kch, kchk, vch, vchk)
        load_chunk(0)
        if len(order) > 1:
            load_chunk(1)
        next_load = [2]

        def st_z(i):
            st = steps[i]
            if st["bi"] == 15 and next_load[0] < len(order) and order.index((st["h"], st["kc"])) + 2 == next_load[0] + 0:
                pass
            kch, kchk, vch, vchk = chunk_of[(st["h"], st["kc"])]
            pz, pzk = pzs.next()
            st["pz"] = (pz, pzk)
            P.mm(pz[:], kch[:, st["bi"] * 128:(st["bi"] + 1) * 128], cx.qbT[:, st["h"], :], True, True,
                 [kchk, kchk + "pad", "qbTpad", f"qbT{st['h']}"], [pzk])
            if st["bi"] == 0 and next_load[0] < len(order):
                load_chunk(next_load[0])
                next_load[0] += 1

        def st_act1(i):
            st = steps[i]
            pz, pzk = st["pz"]
            e2, e2k = e2s.next()
            st["e2"] = (e2, e2k)
            P.act(e2[:], pz[:], AF.Exp, [pzk], [e2k])
            if st["blk"] >= z0:
                P.tt("pool", e2[:], e2[:], mk[:, st["blk"] - z0, :], ALU.mult, [e2k, "mk"], [e2k])

        def st_ln(i):
            st = steps[i]
            e2, e2k = st["e2"]
            nlk, nlkk = nlks.next()
            st["nlk"] = (nlk, nlkk)
            P.act(nlk[:], e2[:], AF.Ln, [e2k], [nlkk], bias=1.0)

        def st_c(i):
            st = steps[i]
            S = Ss[st["h"] % 2]
            Sk = f"S{st['h'] % 2}"
            nlk, nlkk = st["nlk"]
            pc, pck = pcs.next()
            st["pc"] = (pc, pck)
            if st["first"]:
                P.mm(pc[:], tri, nlk[:], True, True, [nlkk, "cst"], [pck])
                P.copy("dve", S[:], nlk[:], [nlkk], [Sk])
            else:
                P.mm(pc[:], tri, nlk[:], True, False, [nlkk, "cst"], [pck])
                P.mm(pc[:], ones, S[:], False, True, [Sk, "cst"], [pck])
                if not st["last"]:
                    P.tt("dve", S[:], S[:], nlk[:], ALU.add, [Sk, nlkk], [Sk])

        def st_act2(i):
            st = steps[i]
            pc, pck = st["pc"]
            gg, ggk = ggs.next()
            st["gg"] = (gg, ggk)
            P.act(gg[:], pc[:], AF.Exp, [pck], [ggk], scale=-1.0)

        def st_a(i):
            st = steps[i]
            e2, e2k = st["e2"]
            gg, ggk = st["gg"]
            aa, aak = aas.next()
            st["aa"] = (aa, aak)
            P.tt("dve", aa[:], e2[:], gg[:], ALU.mult, [e2k, ggk], [aak])

        def st_av(i):
            st = steps[i]
            h = st["h"]
            kch, kchk, vch, vchk = chunk_of[(h, st["kc"])]
            aa, aak = st["aa"]
            po = pos_[h % 2]
            P.mm(po[:, :], vch[:, st["bi"], :], aa[:], st["first"], st["last"], [aak, vchk, vchk + "pad"], [f"po{h % 2}"])
            if st["last"]:
                P.copy("dve", cx.obT[0:64, h, :], po[0:64, :], [f"po{h % 2}"], [f"obT{h}"])

        st_z(0)
        if n > 1:
            st_z(1)
        st_act1(0)
        for t in range(n + 2):
            if t + 2 < n:
                st_z(t + 2)
            if t + 1 < n:
                st_act1(t + 1)
            if t < n:
                st_ln(t)
            if 0 <= t - 1 < n:
                st_c(t - 1)
                st_act2(t - 1)
                st_a(t - 1)
            if 0 <= t - 2 < n:
                st_av(t - 2)
        if os.environ.get("KDBG"):
            P.dma("sp", cx.dbg_ob, cx.obT[0:64], "dbgob", [f"obT{h}" for h in range(8)], [])
        P.run()


NBIS = 16


def topk_threshold(P, cx, isc, ikeys, nk, sm, junk, tag, np_=128):
    lo, hi, mid, cnt, pred, d1, d2 = (sm[k][0:np_] for k in ("lo", "hi", "mid", "cnt", "pred", "d1", "d2"))
    for it in range(NBIS):
        P.tt("dve", mid[:], lo[:], hi[:], ALU.add, [tag + "lo", tag + "hi"], [tag + "mid"])
        P.ts("dve", mid[:], mid[:], 0.5, None, ALU.mult, None, [tag + "mid"], [tag + "mid"])
        nch = (nk + 2047) // 2048
        for cch in range(nch):
            w = min(2048, nk - cch * 2048)
            seed = 0.0 if cch == 0 else cnt[:, 0:1]
            rk = [ikeys[i] for i in range(cch * 4, min(len(ikeys), cch * 4 + 4))]
            P.ts("dve", junk[0:np_, 0:w], isc[0:np_, cch * 2048:cch * 2048 + w], mid[:, 0:1], seed, ALU.is_gt, ALU.add,
                 rk + [tag + "mid", tag + "cnt"], [tag + "junk", tag + "cnt"], accum_out=cnt[:, 0:1])
        P.ts("dve", pred[:], cnt[:], float(TOPK), None, ALU.is_ge, None, [tag + "cnt"], [tag + "pred"])
        P.tt("dve", d1[:], mid[:], lo[:], ALU.subtract, [tag + "mid", tag + "lo"], [tag + "d1"])
        P.tt("dve", d2[:], hi[:], mid[:], ALU.subtract, [tag + "mid", tag + "hi"], [tag + "d2"])
        P.stt(lo[:], d1[:], pred[:, 0:1], lo[:], ALU.mult, ALU.add, [tag + "d1", tag + "pred", tag + "lo"], [tag + "lo"])
        P.stt(hi[:], d2[:], pred[:, 0:1], mid[:], ALU.mult, ALU.add, [tag + "d2", tag + "pred", tag + "mid"], [tag + "hi"])


def topk_threshold_split(P, cx, isc, ikeys, nk, sm, junk, junkA, sgn, tag):
    lo, hi, mid, cnt, pred, d1, d2, nmid, ssum, tot = (sm[k] for k in ("lo", "hi", "mid", "cnt", "pred", "d1", "d2",
                                                                          "nmid", "ssum", "tot"))
    nD = nk // 2
    nA = nk - nD
    nchA = (nA + 2047) // 2048
    for it in range(NBIS):
        P.tt("dve", d1[:], lo[:], hi[:], ALU.add, [tag + "lo", tag + "hi"], [tag + "d1"])
        P.ts("dve", mid[:], d1[:], 0.5, None, ALU.mult, None, [tag + "d1"], [tag + "mid"])
        P.ts("dve", nmid[:], d1[:], -0.5, None, ALU.mult, None, [tag + "d1"], [tag + "nmid"])
        for cch in range(nchA):
            w = min(2048, nA - cch * 2048)
            a0_ = nD + cch * 2048
            rk = [ikeys[i] for i in range(a0_ // 512, (a0_ + w) // 512)]
            P.act(junkA[:, 0:w], isc[:, a0_:a0_ + w], AF.Sign, rk + [tag + "nmid"], [tag + "junkA", f"{tag}sgn{cch}"],
                  bias=nmid[:, 0:1], scale=1.0, accum_out=sgn[:, cch:cch + 1])
        nch = (nD + 2047) // 2048
        for cch in range(nch):
            w = min(2048, nD - cch * 2048)
            seed = 0.0 if cch == 0 else cnt[:, 0:1]
            rk = [ikeys[i] for i in range(cch * 4, min(len(ikeys), cch * 4 + 4))]
            P.ts("dve", junk[:, 0:w], isc[:, cch * 2048:cch * 2048 + w], mid[:, 0:1], seed, ALU.is_gt, ALU.add,
                 rk + [tag + "mid", tag + "cnt"], [tag + "junk", tag + "cnt"], accum_out=cnt[:, 0:1])
        if nchA > 1:
            P.add("dve", lambda e: e.tensor_reduce(ssum[:], sgn[:, 0:nchA], AX.X, ALU.add),
                  [f"{tag}sgn{c_}" for c_ in range(nchA)], [tag + "ssum"])
            srd, srk = ssum, [tag + "ssum"]
        else:
            srd, srk = sgn, [f"{tag}sgn0"]
        P.stt(tot[:], cnt[:], 2.0, srd[:, 0:1], ALU.mult, ALU.add, [tag + "cnt"] + srk, [tag + "tot"])
        P.ts("dve", pred[:], tot[:], float(2 * TOPK - nA), None, ALU.is_ge, None, [tag + "tot"], [tag + "pred"])
        P.tt("dve", d1[:], mid[:], lo[:], ALU.subtract, [tag + "mid", tag + "lo"], [tag + "d1"])
        P.tt("dve", d2[:], hi[:], mid[:], ALU.subtract, [tag + "mid", tag + "hi"], [tag + "d2"])
        P.stt(lo[:], d1[:], pred[:, 0:1], lo[:], ALU.mult, ALU.add, [tag + "d1", tag + "pred", tag + "lo"], [tag + "lo"])
        P.stt(hi[:], d2[:], pred[:, 0:1], mid[:], ALU.mult, ALU.add, [tag + "d2", tag + "pred", tag + "mid"], [tag + "hi"])


def aattn_prompt(cx, ti):
    from contextlib import ExitStack
    nc = cx.nc
    cst = cx.cst
    b, m = divmod(ti, 4)
    nkb = 32 * m + 32
    nk = 128 * nkb
    z0k = 32 * m * 128
    with ExitStack() as es:
        def sb(name, shape, dt):
            return es.enter_context(nc.sbuf_tensor(f"aa{ti}_" + name, shape, dt))
        isc = sb("isc", [128, 16384], F32)
        msks = Rot([sb(f"msk{i}", [128, 2048], BF16) for i in range(2)], "msk")
        junk = sb("junk", [128, 2048], BF16)
        nb = sb("nb", [128, 2048], BF16)
        rls = Rot([sb(f"rl{i}", [128, 512], BF16) for i in range(4)], "rl")
        dg = sb("dg", [128, 8, 128], BF16)
        kich_t = [sb(f"kich{i}", [128, 2048], BF16) for i in range(2)]
        kichs = Rot(kich_t, "kich")
        kach_t = [[sb(f"kach{kv}{i}", [128, 2048], BF16) for i in range(2)] for kv in range(2)]
        kachs = [Rot(kach_t[kv], f"kach{kv}") for kv in range(2)]
        vach_t = [[sb(f"vach{kv}{i}", [128, 16, 128], BF16) for i in range(2)] for kv in range(2)]
        vachs = [Rot(vach_t[kv], f"vach{kv}") for kv in range(2)]
        ess = Rot([sb(f"es{i}", [128, 512], BF16) for i in range(4)], "es")
        pms = Rot([sb(f"pm{i}", [128, 512], BF16) for i in range(6)], "pm")
        mTs = Rot([sb(f"mT{i}", [128, 128], BF16) for i in range(4)], "mT")
        sm = {k: sb("sm_" + k, [128, 1], F32) for k in ("lo", "hi", "mid", "cnt", "pred", "d1", "d2", "mx", "mn",
                                                          "nmid", "ssum", "tot")}
        junkA = sb("junkA", [128, 2048], BF16)
        sgn = sb("sgn", [128, 8], F32)
        rd = sb("rd", [128, 512], F32)
        num = sb("num", [64, 512], F32)
        P = Prog(cx.sync)
        for kv in range(2):
            for i in range(2):
                P.memset("pool", vach_t[kv][i][:, :, 65:128], 0.0, [], [f"vach{kv}{i}one"])
                P.memset("pool", vach_t[kv][i][:, :, 64:65], 1.0, [], [f"vach{kv}{i}one"])
                P.memset("pool", kach_t[kv][i][64:128, :], 0.0, [], [f"kach{kv}{i}pad"])
        for i in range(2):
            P.memset("pool", kich_t[i][32:64, :], 0.0, [], [f"kich{i}pad"])
            P.memset("pool", kich_t[i][64:128, :], 0.0, [], [f"kich{i}pad"])
        pls = Rot([cx.ps[0], cx.ps[1], cx.ps[2]], "pss")
        paccs = Rot([cx.ps[3], cx.ps[4]], "pacc", keys=["pss3", "pmt0"])
        pmts = Rot([cx.ps[4], cx.ps[7]], "pmt")
        psss = Rot([cx.ps[0], cx.ps[1], cx.ps[2], cx.ps[3]], "pss")
        poa = [cx.ps[5], cx.ps[6]]
        pb = cx.ps[7]
        ident = cst[:, C_ID:C_ID + 128]
        nchunk = nkb // 16
        for j in range(4):
            qs = slice(j * 128, (j + 1) * 128)
            ikeys = [f"isc{i}" for i in range(nk // 512)]
            for h in range(8):
                P.ts("dve", dg[:, h, :], ident, cx.wiT[:, j, h:h + 1], None, ALU.mult, None, ["cst", "wiT"], [f"dg{h}"])
            units = [(kc, c4, h) for kc in range(nchunk) for c4 in range(4) for h in range(8)]
            nu = len(units)
            kiload = {}

            def ki_load(kc):
                if kc < nchunk and kc not in kiload:
                    kich, kichk = kichs.next()
                    P.dma("sp", kich[0:32, :], cx.KIT[b][:, kc * 2048:(kc + 1) * 2048], kichk, [], [kichk])
                    kiload[kc] = (kich, kichk)
            ki_load(0)
            ust = [dict() for _ in units]

            def u_pl(i):
                kc, c4, h = units[i]
                if c4 == 0 and h == 0:
                    ki_load(kc + 1)
                kich, kichk = kiload[kc]
                pl, plk = pls.next()
                P.mm(pl[:], cx.qiT[:, h, qs], kich[:, c4 * 512:(c4 + 1) * 512], True, True,
                     [f"qiT{h}", "qiTpad", kichk, kichk + "pad"], [plk])
                ust[i]["pl"] = (pl, plk)

            def u_relu(i):
                kc, c4, h = units[i]
                pl, plk = ust[i]["pl"]
                rl, rlk = rls.next()
                if h % 2 == 0:
                    P.act(rl[:], pl[:], AF.Relu, [plk], [rlk])
                else:
                    P.ts("dve", rl[:], pl[:], 0.0, None, ALU.max, None, [plk], [rlk])
                ust[i]["rl"] = (rl, rlk)

            def u_acc(i):
                kc, c4, h = units[i]
                ci = kc * 4 + c4
                if h == 0:
                    ust[i]["pacc"] = paccs.next()
                else:
                    ust[i]["pacc"] = ust[i - 1]["pacc"]
                pacc, pacck = ust[i]["pacc"]
                rl, rlk = ust[i]["rl"]
                P.mm(pacc[:], dg[:, h, :], rl[:], h == 0, h == 7, [f"dg{h}", rlk], [pacck])
                if h == 7:
                    ksl = slice(ci * 512, (ci + 1) * 512)
                    P.copy("act" if ci % 2 == 0 else "dve", isc[:, ksl], pacc[:], [pacck], [ikeys[ci]])

            u_pl(0)
            if nu > 1:
                u_pl(1)
            u_relu(0)
            for t in range(nu):
                if t + 2 < nu:
                    u_pl(t + 2)
                if t + 1 < nu:
                    u_relu(t + 1)
                u_acc(t)
            P.add("dve", lambda e: e.tensor_reduce(sm["mx"][:], isc[:, 0:nk], AX.X, ALU.max), ikeys, ["aamx"])
            P.add("dve", lambda e: e.tensor_reduce(sm["mn"][:], isc[:, 0:nk], AX.X, ALU.min), ikeys, ["aamn"])
            P.ts("dve", sm["hi"][:], sm["mx"][:], 1.0, None, ALU.add, None, ["aamx"], ["aahi"])
            P.ts("dve", sm["lo"][:], sm["mn"][:], -1.0, None, ALU.add, None, ["aamn"], ["aalo"])
            for hf in range(2):
                P.dma("sp", nb[:], cx.negb[j][:, hf * 2048:(hf + 1) * 2048], "aanb", [], ["nb"])
                for i4 in range(4):
                    ci = z0k // 512 + hf * 4 + i4
                    ksl = slice(ci * 512, (ci + 1) * 512)
                    P.tt("pool", isc[:, ksl], isc[:, ksl], nb[:, i4 * 512:(i4 + 1) * 512], ALU.add, [ikeys[ci], "nb"],
                         [ikeys[ci]])
            topk_threshold_split(P, cx, isc, ikeys, nk, sm, junk, junkA, sgn, "aa")
            asteps = [dict(kc=kc, bi=bi) for kc in range(nchunk) for bi in range(16)]
            na = len(asteps)
            chunks = {}

            def a_load(kc):
                msk, mskk = msks.next()
                P.ts("dve", msk[:], isc[:, kc * 2048:(kc + 1) * 2048], sm["lo"][:, 0:1], None, ALU.is_gt, None,
                     ikeys[kc * 4:kc * 4 + 4] + ["aalo"], [mskk])
                ent = dict(msk=(msk, mskk), kach=[], vach=[])
                for kv in range(2):
                    ka_ = kachs[kv].next()
                    va_ = vachs[kv].next()
                    P.dma("sp", ka_[0][0:64, :], cx.KAT[b, kv, :, kc * 2048:(kc + 1) * 2048], ka_[1], [], [ka_[1]])
                    P.dma("sp", va_[0][:, :, 0:64], cx.VAS[b, kv, :, kc * 16:(kc + 1) * 16, :], va_[1], [], [va_[1]])
                    ent["kach"].append(ka_)
                    ent["vach"].append(va_)
                chunks[kc] = ent

            def a0(i):
                st = asteps[i]
                kc, bi = st["kc"], st["bi"]
                if bi == 0 and kc not in chunks:
                    a_load(kc)
                ent = chunks[kc]
                msk, mskk = ent["msk"]
                pmt_, pmtk = pmts.next()
                P.mm(pmt_[:, 0:128], msk[:, bi * 128:(bi + 1) * 128], ident, True, True, [mskk, "cst"], [pmtk])
                mT, mTk = mTs.next()
                st["mT"] = (mT, mTk)
                P.copy("dve", mT[:], pmt_[:, 0:128], [pmtk], [mTk])
                st["pss"] = []
                for kv in range(2):
                    pss, pssk = psss.next()
                    ka_ = ent["kach"][kv]
                    P.mm(pss[:], ka_[0][:, bi * 128:(bi + 1) * 128], cx.qaT[:, 4 * kv:4 * kv + 4, qs],
                         True, True, [ka_[1], ka_[1] + "pad", "qaTpad"] + [f"qaT{h}" for h in range(4 * kv, 4 * kv + 4)],
                         [pssk])
                    st["pss"].append((pss, pssk))

            def a1(i):
                st = asteps[i]
                st["es"] = []
                for kv in range(2):
                    pss, pssk = st["pss"][kv]
                    es_, esk = ess.next()
                    P.act(es_[:], pss[:], AF.Exp, [pssk], [esk])
                    st["es"].append((es_, esk))

            def a2(i):
                st = asteps[i]
                mT, mTk = st["mT"]
                st["pm"] = []
                for kv in range(2):
                    es_, esk = st["es"][kv]
                    pm, pmk = pms.next()
                    P.tt("pool" if kv else "dve", pm[:].rearrange("p (h q) -> p h q", h=4),
                         es_[:].rearrange("p (h q) -> p h q", h=4),
                         mT[:, :].unsqueeze(1).broadcast_to([128, 4, 128]), ALU.mult, [esk, mTk], [pmk])
                    st["pm"].append((pm, pmk))

            def a3(i):
                st = asteps[i]
                ent = chunks[st["kc"]]
                for kv in range(2):
                    pm, pmk = st["pm"][kv]
                    va_ = ent["vach"][kv]
                    P.mm(poa[kv][:, :], va_[0][:, st["bi"], :], pm[:], i == 0, i == na - 1,
                         [pmk, va_[1], va_[1] + "one"], [f"poa{kv}"])
                if st["bi"] == 15:
                    chunks.pop(st["kc"], None) if False else None

            a0(0)
            for t in range(na + 2):
                if t + 1 < na:
                    a0(t + 1)
                if t < na:
                    a1(t)
                if 0 <= t - 1 < na:
                    a2(t - 1)
                if 0 <= t - 2 < na:
                    a3(t - 2)
            for kv in range(2):
                P.add("dve", (lambda kv: lambda e: e.reciprocal(rd[64:65, :], poa[kv][64:65, :]))(kv), [f"poa{kv}"], ["rd"])
                P.mm(pb[0:64, :], cx.ones_f[64:65, 0:64], rd[64:65, :], True, True, ["rd", "ones_f"], ["pmt1"])
                P.copy("act", num[:], poa[kv][0:64, :], [f"poa{kv}", "rd"], ["num"])
                P.tt("dve", cx.oaT[0:64, 4 * kv:4 * kv + 4, qs], num[:].rearrange("p (h q) -> p h q", h=4),
                     pb[0:64, :].rearrange("p (h q) -> p h q", h=4), ALU.mult, ["num", "pmt1"],
                     [f"oaT{h}" for h in range(4 * kv, 4 * kv + 4)])
        if os.environ.get("KDBG"):
            P.dma("sp", cx.dbg_oa, cx.oaT[0:64], "dbgoa", [f"oaT{h}" for h in range(8)], [])
        P.run()


def stage4(cx, ti):
    from contextlib import ExitStack
    nc = cx.nc
    sample = ti == 8
    N = 128 if sample else 512
    c0 = ti * 512
    with ExitStack() as es:
        def sb(name, shape, dt):
            return es.enter_context(nc.sbuf_tensor(f"s4{ti}_" + name, shape, dt))
        slab_t = [sb(f"slab{i}", [128, 8, 512], BF16) for i in range(6)]
        slabs = Rot(slab_t, "slab")
        xt = sb("xt", [128, 8, 512], F32)
        pTt = sb("pTt", [128, 2, 512], BF16)
        mT = sb("mT", [128, 8, 512], BF16)
        sq = sb("sq", [128, 8, 512], BF16)
        lnv = sb("lnv", [128, 512], F32)
        rstd = sb("rstd", [128, 512], F32)
        h2 = sb("h2", [128, 8, 512], BF16)
        uT = sb("uT", [128, 32, 512], BF16)
        sgas = Rot([sb(f"sga{i}", [128, 512], F32) for i in range(2)], "sga")
        sgbs = Rot([sb(f"sgb{i}", [128, 512], F32) for i in range(2)], "sgb")
        tAs = Rot([sb(f"tA{i}", [128, 512], F32) for i in range(2)], "tA")
        tBs = Rot([sb(f"tB{i}", [128, 512], F32) for i in range(2)], "tB")
        rls = Rot([sb(f"rl{i}", [128, 512], BF16) for i in range(2)], "rl")
        yos = Rot([sb(f"yo{i}", [128, 512], F32) for i in range(2)], "yo")
        P = Prog(cx.sync)
        slab_n = [0]

        issued = []

        def issue_upto(n_):
            while len(issued) < min(n_, len(plan)):
                src, np_, nk = plan[len(issued)]
                sl, slk = slabs.next()
                P.dma("pool", sl[0:np_, 0:nk, :], src, slk, [], [slk])
                issued.append((sl, slk))

        def load_slab(src=None, np_=128, nk=8):
            idx = slab_n[0]
            slab_n[0] += 1
            issue_upto(idx + 3)
            return issued[idx]

        def fm(src_ap):
            return src_ap.rearrange("(k p) c -> p k c", p=128)

        P.dma("sp", xt[:, :, 0:N], fm(cx.xT_own)[:, :, c0:c0 + N], "s4x", [], [f"x{j}" for j in range(8)])
        P.dma("pool", pTt[:, :, 0:N], fm(cx.pT_own)[:, :, c0:c0 + N], "s4p", [], ["pTt"])
        hk = [f"hT{k}" for k in range(8)]
        ps4 = Rot([cx.ps[0], cx.ps[1], cx.ps[2], cx.ps[3]], "bank")
        pa_v = cx.proj_a.rearrange("(h d) c -> d h c", d=64)
        pb_v = cx.proj_b.rearrange("(h d) c -> d h c", d=64)
        wd_v = cx.w_down.rearrange("(g k p) c -> g p k c", k=8, p=128)
        plan = []
        for J in range(2):
            cs_ = slice(J * 512, (J + 1) * 512)
            plan.append((cx.w_in_v[:, :, OFF["ga"] + J * 512:OFF["ga"] + (J + 1) * 512], 128, 8))
            plan.append((cx.w_in_v[:, :, OFF["gb"] + J * 512:OFF["gb"] + (J + 1) * 512], 128, 8))
            plan.append((pa_v[:, :, cs_], 64, 8))
            plan.append((pb_v[:, :, cs_], 64, 8))
        for J in range(2):
            plan.append((fm(cx.w_out)[:, :, J * 512:(J + 1) * 512], 128, 8))
        for J in range(8):
            plan.append((fm(cx.w_up)[:, :, J * 512:(J + 1) * 512], 128, 8))
        for J in range(2):
            for kg in range(4):
                plan.append((wd_v[kg][:, :, J * 512:(J + 1) * 512], 128, 8))
        for J in range(2):
            plan.append((fm(cx.w_pg)[:, :, J * 512:(J + 1) * 512], 128, 8))
            plan.append((fm(cx.w_ple)[:, :, J * 512:(J + 1) * 512], 128, 2))
        for i in range(6):
            P.memset("pool", slab_t[i][64:128, :, :], 0.0, [], [f"slab{i}"])
        issue_upto(4)
        for J in range(2):
            cs = slice(J * 512, (J + 1) * 512)
            wga, wgak = load_slab(cx.w_in_v[:, :, OFF["ga"] + J * 512:OFF["ga"] + (J + 1) * 512])
            wgb, wgbk = load_slab(cx.w_in_v[:, :, OFF["gb"] + J * 512:OFF["gb"] + (J + 1) * 512])
            wpa, wpak = load_slab(pa_v[:, :, cs], 64)
            wpb, wpbk = load_slab(pb_v[:, :, cs], 64)
            for jj in range(4):
                j = 4 * J + jj
                js = slice(jj * 128, (jj + 1) * 128)
                pga, pgak = ps4.next()
                for k in range(8):
                    P.mm(pga[:, 0:N], wga[:, k, js], cx.hT[:, k, 0:N], k == 0, k == 7, [wgak, hk[k]], [pgak])
                sga, sgak = sgas.next()
                P.act(sga[:, 0:N], pga[:, 0:N], AF.Sigmoid, [pgak], [sgak])
                pgb, pgbk = ps4.next()
                for k in range(8):
                    P.mm(pgb[:, 0:N], wgb[:, k, js], cx.hT[:, k, 0:N], k == 0, k == 7, [wgbk, hk[k]], [pgbk])
                sgb, sgbk = sgbs.next()
                P.act(sgb[:, 0:N], pgb[:, 0:N], AF.Sigmoid, [pgbk], [sgbk])
                ppa, ppak = ps4.next()
                for h in range(8):
                    P.mm(ppa[:, 0:N], wpa[:, h, js], cx.oaT[:, h, 0:N], h == 0, h == 7, [wpak, f"oaT{h}"], [ppak])
                tA, tAk = tAs.next()
                P.tt("dve", tA[:, 0:N], ppa[:, 0:N], sga[:, 0:N], ALU.mult, [ppak, sgak], [tAk])
                ppb, ppbk = ps4.next()
                for h in range(8):
                    P.mm(ppb[:, 0:N], wpb[:, h, js], cx.obT[:, h, 0:N], h == 0, h == 7, [wpbk, f"obT{h}"], [ppbk])
                tB, tBk = tBs.next()
                P.tt("dve", tB[:, 0:N], ppb[:, 0:N], sgb[:, 0:N], ALU.mult, [ppbk, sgbk], [tBk])
                P.tt("pool", mT[:, j, 0:N], tA[:, 0:N], tB[:, 0:N], ALU.add, [tAk, tBk], [f"mT{j}"])
        for J in range(2):
            wo, wok = load_slab(fm(cx.w_out)[:, :, J * 512:(J + 1) * 512])
            for jj in range(4):
                j = 4 * J + jj
                pm, pmk = ps4.next()
                for k in range(8):
                    P.mm(pm[:, 0:N], wo[:, k, jj * 128:(jj + 1) * 128], mT[:, k, 0:N], k == 0, k == 7, [wok, f"mT{k}"], [pmk])
                P.tt("dve", xt[:, j, 0:N], xt[:, j, 0:N], pm[:, 0:N], ALU.add, [f"x{j}", pmk], [f"x{j}"])
        xkeys = [f"x{j}" for j in range(8)]
        norm_ops_k(P, cx, xt, xkeys, N, sq, lnv, rstd, h2, cx.g_ffn_t, "h2_", 4, "s4")
        h2k = [f"h2_{k}" for k in range(8)]
        for J in range(8):
            wu, wuk = load_slab(fm(cx.w_up)[:, :, J * 512:(J + 1) * 512])
            for jj in range(4):
                j = 4 * J + jj
                pu, puk = ps4.next()
                for k in range(8):
                    P.mm(pu[:, 0:N], wu[:, k, jj * 128:(jj + 1) * 128], h2[:, k, 0:N], k == 0, k == 7, [wuk, h2k[k]], [puk])
                rl, rlk = rls.next()
                P.act(rl[:, 0:N], pu[:, 0:N], AF.Relu, [puk], [rlk])
                P.tt("pool", uT[:, j, 0:N], rl[:, 0:N], rl[:, 0:N], ALU.mult, [rlk], [f"uT{j}"])
        wd_v = cx.w_down.rearrange("(g k p) c -> g p k c", k=8, p=128)
        for J in range(2):
            for kg in range(4):
                wd, wdk = load_slab(wd_v[kg][:, :, J * 512:(J + 1) * 512])
                for jj in range(4):
                    for k in range(8):
                        P.mm(cx.ps[4 + jj][:, 0:N], wd[:, k, jj * 128:(jj + 1) * 128], uT[:, kg * 8 + k, 0:N],
                             kg == 0 and k == 0, kg == 3 and k == 7, [wdk, f"uT{kg * 8 + k}"], [f"bank{4 + jj}"])
            for jj in range(4):
                j = 4 * J + jj
                P.tt("dve", xt[:, j, 0:N], xt[:, j, 0:N], cx.ps[4 + jj][:, 0:N], ALU.add, [f"x{j}", f"bank{4 + jj}"], [f"x{j}"])
        norm_ops_k(P, cx, xt, xkeys, N, sq, lnv, rstd, h2, cx.g_ple_t, "h2_", 0, "s4")
        for J in range(2):
            wg, wgk = load_slab(fm(cx.w_pg)[:, :, J * 512:(J + 1) * 512])
            wp, wpk = load_slab(fm(cx.w_ple)[:, :, J * 512:(J + 1) * 512], 128, 2)
            for jj in range(4):
                j = 4 * J + jj
                js = slice(jj * 128, (jj + 1) * 128)
                pg, pgk = ps4.next()
                for k in range(8):
                    P.mm(pg[:, 0:N], wg[:, k, js], h2[:, k, 0:N], k == 0, k == 7, [wgk, h2k[k]], [pgk])
                sga, sgak = sgas.next()
                P.act(sga[:, 0:N], pg[:, 0:N], AF.Sigmoid, [pgk], [sgak])
                pe_, pek = ps4.next()
                for k in range(2):
                    P.mm(pe_[:, 0:N], wp[:, k, js], pTt[:, k, 0:N], k == 0, k == 1, [wpk, "pTt"], [pek])
                tA, tAk = tAs.next()
                P.tt("dve", tA[:, 0:N], pe_[:, 0:N], sga[:, 0:N], ALU.mult, [pek, sgak], [tAk])
                yo, yok = yos.next()
                P.tt("pool", yo[:, 0:N], tA[:, 0:N], xt[:, j, 0:N], ALU.add, [tAk, f"x{j}"], [yok])
                P.dma("sp", cx.yT[j * 128:(j + 1) * 128, c0:c0 + N], yo[:, 0:N], yok, [yok], [])
        P.run()


def norm_ops_k(P, cx, xt, xkeys, N, sq, lnv, rstd, hT, gcols, hkey, psb, tag):
    cst = cx.cst
    P.act(sq[:, :, 0:N], xt[:, :, 0:N], AF.Square, xkeys, [tag + "sq"])
    for k in range(8):
        P.mm(cx.ps[psb][:, 0:N], cst[:, C_ONES:C_ONES + 128], sq[:, k, 0:N], k == 0, k == 7,
             [tag + "sq", "cst"], [f"bank{psb}"])
    P.act(lnv[:, 0:N], cx.ps[psb][:, 0:N], AF.Ln, [f"bank{psb}"], [tag + "lnv"], bias=EPS, scale=1.0 / D)
    P.act(rstd[:, 0:N], lnv[:, 0:N], AF.Exp, [tag + "lnv"], [tag + "rstd"], scale=-0.5)
    for k in range(8):
        P.stt(hT[:, k, 0:N], xt[:, k, 0:N], gcols[:, k:k + 1], rstd[:, 0:N], ALU.mult, ALU.mult,
              [xkeys[k], tag + "rstd", "gains"], [f"{hkey}{k}"])


def attn_sample(cx):
    from contextlib import ExitStack
    nc = cx.nc
    cst = cx.cst
    NKS = PAST + DSEQ
    with ExitStack() as es:
        def sb(name, shape, dt):
            return es.enter_context(nc.sbuf_tensor("as_" + name, shape, dt))
        kchs = Rot([sb(f"kch{i}", [64, 8, 1024], BF16) for i in range(2)], "kch")
        vchs = Rot([sb(f"vch{i}", [128, 8, 512], BF16) for i in range(2)], "vch")
        e2s = Rot([sb(f"e2{i}", [128, 256], F32) for i in range(4)], "e2")
        nlks = Rot([sb(f"nlk{i}", [128, 256], BF16) for i in range(3)], "nlk")
        ggs = Rot([sb(f"gg{i}", [128, 256], F32) for i in range(2)], "gg")
        aas = Rot([sb(f"aa{i}", [128, 256], BF16) for i in range(3)], "aa")
        S = sb("S", [128, 256], BF16)
        S2 = sb("S2", [128, 256], BF16)
        isc = sb("isc", [32, 4608], F32)
        msk = sb("msk", [32, 4608], BF16)
        junk = sb("junk", [32, 2048], BF16)
        tmps = Rot([sb(f"tmp{i}", [32, 512], F32) for i in range(2)], "tmp")
        kich = sb("kich", [32, 4096], BF16)
        kach = [sb(f"kach{kv}", [64, 4096], BF16) for kv in range(2)]
        vach = sb("vach", [128, 32, 2, 65], BF16)
        ess = Rot([sb(f"es{i}", [128, 128], BF16) for i in range(4)], "es")
        pms = Rot([sb(f"pm{i}", [128, 128], BF16) for i in range(6)], "pm")
        mTs = Rot([sb(f"mT{i}", [128, 32], BF16) for i in range(3)], "mT")
        sm = {k: sb("sm_" + k, [32, 1], F32) for k in ("lo", "hi", "mid", "cnt", "pred", "d1", "d2", "mx", "mn")}
        rd = sb("rd", [128, 128], F32)
        num = sb("num", [64, 128], F32)
        P = Prog(cx.sync)
        tri = cst[:, C_TRI:C_TRI + 128]
        ones = cst[:, C_ONES:C_ONES + 128]
        ident = cst[:, C_ID:C_ID + 128]
        lt32 = cst[0:32, C_LT:C_LT + 32]
        pzs = Rot([cx.ps[0], cx.ps[1]], "pz")
        pcs = Rot([cx.ps[2], cx.ps[3]], "pc")
        pos2 = [cx.ps[4], cx.ps[5]]
        bsteps = []
        for s in (range(4) if "b" in os.environ.get("KAS", "ab") else []):
            blocks = [("new", 0, 0)]
            for kc in reversed(range(4)):
                for bi in reversed(range(8)):
                    blocks.append(("cache", kc, bi))
            for bidx, (kind, kc, bi) in enumerate(blocks):
                bsteps.append(dict(s=s, kind=kind, kc=kc, bi=bi, first=bidx == 0, last=bidx == len(blocks) - 1,
                                   nkp=32 if kind == "new" else 128))
        nb_ = len(bsteps)
        chunkmap = {}
        corder = []
        for st in bsteps:
            if st["kind"] == "cache" and (st["s"], st["kc"]) not in chunkmap:
                chunkmap[(st["s"], st["kc"])] = None
                corder.append((st["s"], st["kc"]))
        cnext = [0]

        def c_load():
            if cnext[0] < len(corder):
                s_, kc = corder[cnext[0]]
                cnext[0] += 1
                kch, kchk = kchs.next()
                vch, vchk = vchs.next()
                P.dma("pool", kch[:], cx.cbkT[s_][:, :, kc * 1024:(kc + 1) * 1024].rearrange("h d t -> d h t"), kchk,
                      [], [kchk])
                P.dma("pool", vch[:], cx.cbv[s_][kc * 1024:(kc + 1) * 1024, :].rearrange("(j p) c -> p j c", p=128),
                      vchk, [], [vchk])
                chunkmap[(s_, kc)] = (kch, kchk, vch, vchk)
        c_load()
        c_load()
        Ssb = [S, S2]

        def sb_z(i):
            st = bsteps[i]
            s_, nkp = st["s"], st["nkp"]
            qs = slice(32 * s_, 32 * s_ + 32)
            pz, pzk = pzs.next()
            st["pz"] = (pz, pzk)
            for h in range(8):
                if st["kind"] == "new":
                    lhsT = cx.kbnT[0:64, h, qs]
                    rk = [f"kbnT{h}"]
                else:
                    kch, kchk, vch, vchk = chunkmap[(s_, st["kc"])]
                    lhsT = kch[0:64, h, st["bi"] * 128:(st["bi"] + 1) * 128]
                    rk = [kchk]
                P.mm(pz[0:nkp, h * 32:(h + 1) * 32], lhsT, cx.qbT[0:64, h, qs], True, True, rk + [f"qbT{h}"], [pzk])

        def sb_e(i):
            st = bsteps[i]
            nkp = st["nkp"]
            pz, pzk = st["pz"]
            e2, e2k = e2s.next()
            st["e2"] = (e2, e2k)
            P.act(e2[0:nkp, :], pz[0:nkp, 0:256], AF.Exp, [pzk], [e2k])
            if st["kind"] == "new":
                P.tt("pool", e2[0:32, :].rearrange("p (h q) -> p h q", h=8), e2[0:32, :].rearrange("p (h q) -> p h q", h=8),
                     lt32.unsqueeze(1).broadcast_to([32, 8, 32]), ALU.mult, [e2k, "cst"], [e2k])

        def sb_ln(i):
            st = bsteps[i]
            nkp = st["nkp"]
            e2, e2k = st["e2"]
            nlk, nlkk = nlks.next()
            st["nlk"] = (nlk, nlkk)
            P.act(nlk[0:nkp, :], e2[0:nkp, :], AF.Ln, [e2k], [nlkk], bias=1.0)

        def sb_c(i):
            st = bsteps[i]
            nkp = st["nkp"]
            Sx = Ssb[st["s"] % 2]
            Sk = f"S{st['s'] % 2}"
            nlk, nlkk = st["nlk"]
            pc, pck = pcs.next()
            st["pc"] = (pc, pck)
            if st["first"]:
                P.mm(pc[0:nkp, 0:256], tri[0:nkp, 0:nkp], nlk[0:nkp, :], True, True, [nlkk, "cst"], [pck])
                P.memset("pool", Sx[:], 0.0, [], [Sk])
                P.copy("pool", Sx[0:32, :], nlk[0:32, :], [nlkk, Sk], [Sk])
            else:
                P.mm(pc[0:nkp, 0:256], tri[0:nkp, 0:nkp], nlk[0:nkp, :], True, False, [nlkk, "cst"], [pck])
                P.mm(pc[0:nkp, 0:256], ones[:, 0:nkp], Sx[:], False, True, [Sk, "cst"], [pck])
                if not st["last"]:
                    P.tt("dve", Sx[:], Sx[:], nlk[:], ALU.add, [Sk, nlkk], [Sk])
            gg, ggk = ggs.next()
            P.act(gg[0:nkp, :], pc[0:nkp, 0:256], AF.Exp, [pck], [ggk], scale=-1.0)
            e2, e2k = st["e2"]
            aa, aak = aas.next()
            st["aa"] = (aa, aak)
            P.tt("dve", aa[0:nkp, :], e2[0:nkp, :], gg[0:nkp, :], ALU.mult, [e2k, ggk], [aak])

        def sb_av(i):
            st = bsteps[i]
            s_, nkp = st["s"], st["nkp"]
            qs = slice(32 * s_, 32 * s_ + 32)
            aa, aak = st["aa"]
            po_ = pos2[s_ % 2]
            pok = f"po{s_ % 2}"
            for h in range(8):
                if st["kind"] == "new":
                    lv = cx.vbn[0:32, s_, h * 64:(h + 1) * 64]
                    rk = [f"vbn{s_}"]
                else:
                    kch, kchk, vch, vchk = chunkmap[(s_, st["kc"])]
                    lv = vch[:, st["bi"], h * 64:(h + 1) * 64]
                    rk = [vchk]
                P.mm(po_[0:64, h * 32:(h + 1) * 32], lv, aa[0:nkp, h * 32:(h + 1) * 32], st["first"] and h == 0, st["last"],
                     rk + [aak], [pok])
            if st["kind"] == "cache" and st["bi"] == 0:
                c_load()
            if st["last"]:
                P.copy("dve", cx.obT[0:64, :, qs], po_[0:64, 0:256].rearrange("p (h q) -> p h q", h=8), [pok],
                       [f"obT{h}" for h in range(8)])

        if nb_:
            sb_z(0)
            if nb_ > 1:
                sb_z(1)
            sb_e(0)
            for t in range(nb_ + 2):
                if t + 2 < nb_:
                    sb_z(t + 2)
                if t + 1 < nb_:
                    sb_e(t + 1)
                if t < nb_:
                    sb_ln(t)
                if 0 <= t - 1 < nb_:
                    sb_c(t - 1)
                if 0 <= t - 2 < nb_:
                    sb_av(t - 2)
        pls = Rot([cx.ps[0], cx.ps[1]], "pz")
        pmt = cx.ps[5]
        psss = Rot([cx.ps[6], cx.ps[7], cx.ps[0], cx.ps[1]], "pss", keys=["pss0", "pss1", "pz0", "pz1"])
        poa = [cx.ps[2], cx.ps[3]]
        pb = cx.ps[4]
        for s in (range(4) if "a" in os.environ.get("KAS", "ab") else []):
            qs = slice(32 * s, 32 * s + 32)
            P.dma("pool", kich[:], cx.ckiT[s], "askich", [], ["kich"])
            for kv in range(2):
                P.dma("pool", kach[kv][:], cx.cakT[s, kv], f"askach{kv}", [], [f"kach{kv}"])
            cav_v = cx.cav[s].rearrange("(j p) (kv d) -> p j kv d", p=128, kv=2)
            for kv in range(2):
                for jh in range(2):
                    P.dma("pool", vach[:, jh * 16:(jh + 1) * 16, kv, 0:64], cav_v[:, jh * 16:(jh + 1) * 16, kv, :],
                          f"asvach{kv}{jh}", [], ["vach"])
            if s == 0:
                P.memset("pool", vach[:, :, :, 64:65], 1.0, [], ["vach1"])
            nchunks = 9
            ikeys = [f"isc{i}" for i in range(12)]
            for ci in range(nchunks):
                w = 512 if ci < 8 else 32
                ksl = slice(ci * 512, ci * 512 + w)
                for h in range(8):
                    pl, plk = pls.next()
                    if ci < 8:
                        rhs = kich[0:32, ksl]
                        rk = ["kich"]
                    else:
                        rhs = cx.kinT[0:32, qs]
                        rk = ["kinT"]
                    P.mm(pl[0:32, 0:w], cx.qiT[0:32, h, qs], rhs, True, True, [f"qiT{h}"] + rk, [plk])
                    if h == 0:
                        P.ts("dve", isc[:, ksl], pl[0:32, 0:w], 0.0, cx.wiT[0:32, s, 0:1], ALU.max, ALU.mult, [plk, "wiT"],
                             [ikeys[ci]])
                    else:
                        tmp, tmpk = tmps.next()
                        P.ts("dve", tmp[:, 0:w], pl[0:32, 0:w], 0.0, cx.wiT[0:32, s, h:h + 1], ALU.max, ALU.mult, [plk, "wiT"],
                             [tmpk])
                        P.tt("pool", isc[:, ksl], isc[:, ksl], tmp[:, 0:w], ALU.add, [ikeys[ci], tmpk], [ikeys[ci]])
            ik = ikeys[0:9]
            if os.environ.get("KAL", "3") < "2":
                continue
            P.add("dve", lambda e: e.tensor_reduce(sm["mx"][:], isc[:, 0:NKS], AX.X, ALU.max), ik, ["asmx"])
            P.add("dve", lambda e: e.tensor_reduce(sm["mn"][:], isc[:, 0:NKS], AX.X, ALU.min), ik, ["asmn"])
            P.ts("dve", sm["hi"][:], sm["mx"][:], 1.0, None, ALU.add, None, ["asmx"], ["ashi"])
            P.ts("dve", sm["lo"][:], sm["mn"][:], -1.0, None, ALU.add, None, ["asmn"], ["aslo"])
            topk_threshold(P, cx, isc, ikeys, NKS, sm, junk, "as", np_=32)
            P.ts("dve", msk[:, 0:NKS], isc[:, 0:NKS], sm["lo"][:, 0:1], None, ALU.is_gt, None, ik + ["aslo"], ["msk"])
            if os.environ.get("KAL", "3") < "3":
                continue
            sst = [dict() for _ in range(33)]

            def sa0(blk):
                nkp = 128 if blk < 32 else 32
                P.mm(pmt[0:nkp, 0:32], msk[0:32, blk * 128:blk * 128 + nkp], ident[0:32, 0:32], True, True, ["msk", "cst"],
                     ["po1"])
                mT, mTk = mTs.next()
                sst[blk]["mT"] = (mT, mTk)
                P.copy("dve", mT[0:nkp, :], pmt[0:nkp, 0:32], ["po1"], [mTk])
                sst[blk]["pss"] = []
                for kv in range(2):
                    pss, pssk = psss.next()
                    if blk < 32:
                        lhsT = kach[kv][0:64, blk * 128:(blk + 1) * 128]
                        rk = [f"kach{kv}"]
                    else:
                        lhsT = cx.kanT[0:64, kv, qs]
                        rk = [f"kanT{kv}"]
                    P.mm(pss[0:nkp, 0:128], lhsT, cx.qaT[0:64, 4 * kv:4 * kv + 4, qs], True, True,
                         rk + [f"qaT{h}" for h in range(4 * kv, 4 * kv + 4)], [pssk])
                    sst[blk]["pss"].append((pss, pssk))

            def sa1(blk):
                nkp = 128 if blk < 32 else 32
                mT, mTk = sst[blk]["mT"]
                sst[blk]["pm"] = []
                for kv in range(2):
                    pss, pssk = sst[blk]["pss"][kv]
                    es_, esk = ess.next()
                    P.act(es_[0:nkp, :], pss[0:nkp, 0:128], AF.Exp, [pssk], [esk])
                    pm, pmk = pms.next()
                    P.tt("pool" if kv else "dve", pm[0:nkp, :].rearrange("p (h q) -> p h q", h=4),
                         es_[0:nkp, :].rearrange("p (h q) -> p h q", h=4),
                         mT[0:nkp, :].unsqueeze(1).broadcast_to([nkp, 4, 32]), ALU.mult, [esk, mTk], [pmk])
                    sst[blk]["pm"].append((pm, pmk))

            def sa2(blk):
                nkp = 128 if blk < 32 else 32
                for kv in range(2):
                    pm, pmk = sst[blk]["pm"][kv]
                    if blk < 32:
                        lv = vach[:, blk, kv, :]
                        rv = ["vach", "vach1"]
                    else:
                        lv = cx.van[0:32, s, kv, :]
                        rv = [f"van{s}", f"van1{s}"]
                    P.mm(poa[kv][0:65, 0:128], lv, pm[0:nkp, :], blk == 0, blk == 32, [pmk] + rv, [f"pc{kv}"])

            if os.environ.get("KSEQ"):
                for t in range(33):
                    sa0(t)
                    sa1(t)
                    sa2(t)
            else:
                sa0(0)
                for t in range(33 + 1):
                    if t + 1 < 33:
                        sa0(t + 1)
                    if t < 33:
                        sa1(t)
                    if 0 <= t - 1 < 33:
                        sa2(t - 1)
            for kv in range(2):
                P.add("dve", (lambda kv: lambda e: e.reciprocal(rd[64:65, :], poa[kv][64:65, 0:128]))(kv), [f"pc{kv}"], ["rd"])
                P.mm(pb[0:64, 0:128], cx.ones_f[64:65, 0:64], rd[64:65, :], True, True, ["rd", "ones_f"], ["po0"])
                P.copy("act", num[:], poa[kv][0:64, 0:128], [f"pc{kv}", "rd"], ["num"])
                P.tt("dve", cx.oaT[0:64, 4 * kv:4 * kv + 4, qs], num[:].rearrange("p (h q) -> p h q", h=4),
                     pb[0:64, 0:128].rearrange("p (h q) -> p h q", h=4), ALU.mult, ["num", "po0"],
                     [f"oaT{h}" for h in range(4 * kv, 4 * kv + 4)])
        if os.environ.get("KDBG"):
            P.dma("sp", cx.dbg_oa, cx.oaT[0:64], "dbgoa", [f"oaT{h}" for h in range(8)], [])
            P.dma("sp", cx.dbg_ob, cx.obT[0:64], "dbgob", [f"obT{h}" for h in range(8)], [])
        P.run()


def kernel(**inputs):
    inp = {k: np.asarray(v) for k, v in inputs.items()}
    sh = host_shared(inp)
    nc = build(9, 64, range(9))
    in_maps = [host_core(c, inp, sh) for c in range(8)]
    res = run_bass_kernel_spmd(nc, in_maps, core_ids=list(range(8)))
    f32 = np.float32
    y_p = np.empty((NB, SEQ, D), f32)
    y_s = np.empty((DBATCH, DSEQ, D), f32)
    ka_p = np.empty((1, NB, SEQ, NKV, HD), f32)
    va_p = np.empty((1, NB, SEQ, NKV, HD), f32)
    ki_p = np.empty((1, NB, SEQ, IDD), f32)
    kb_p = np.empty((1, NB, SEQ, H, HD), f32)
    vb_p = np.empty((1, NB, SEQ, H, HD), f32)
    ka_s = np.empty((1, DBATCH, DSEQ, NKV, HD), f32)
    va_s = np.empty((1, DBATCH, DSEQ, NKV, HD), f32)
    ki_s = np.empty((1, DBATCH, DSEQ, IDD), f32)
    kb_s = np.empty((1, DBATCH, DSEQ, H, HD), f32)
    vb_s = np.empty((1, DBATCH, DSEQ, H, HD), f32)
    for c in range(8):
        r = res.results[c]
        yT, kaT, va, kiT, kbT, vb = (np.asarray(r[k]) for k in ("yT", "o_kaT", "o_va", "o_kiT", "o_kbT", "o_vb"))
        for t, (b, q0) in enumerate(own_tiles(c)):
            sl = slice(t * 512, (t + 1) * 512)
            qs = slice(q0, q0 + 512)
            y_p[b, qs] = yT[:, sl].T
            ka_p[0, b, qs] = kaT[:, sl].T.reshape(512, NKV, HD)
            va_p[0, b, qs] = va[sl].reshape(512, NKV, HD)
            ki_p[0, b, qs] = kiT[:, sl].T
            kb_p[0, b, qs] = kbT[:, sl].T.reshape(512, H, HD)
            vb_p[0, b, qs] = vb[sl].reshape(512, H, HD)
        sl = slice(4096, NOWN)
        ss = slice(4 * c, 4 * c + 4)
        y_s[ss] = yT[:, sl].T.reshape(4, DSEQ, D)
        ka_s[0, ss] = kaT[:, sl].T.reshape(4, DSEQ, NKV, HD)
        va_s[0, ss] = va[sl].reshape(4, DSEQ, NKV, HD)
        ki_s[0, ss] = kiT[:, sl].T.reshape(4, DSEQ, IDD)
        kb_s[0, ss] = kbT[:, sl].T.reshape(4, DSEQ, H, HD)
        vb_s[0, ss] = vb[sl].reshape(4, DSEQ, H, HD)
    return (y_p, y_s, ka_p, va_p, ki_p, kb_p, vb_p, ka_s, va_s, ki_s, kb_s, vb_s)
```

```python
import numpy as np
import ml_dtypes
import concourse.bass as bass
import concourse.mybir as mybir
from concourse.bass_utils import run_bass_kernel_spmd

F32 = mybir.dt.float32
BF16 = mybir.dt.bfloat16
AF = mybir.ActivationFunctionType
ALU = mybir.AluOpType
AX = mybir.AxisListType

EPOCH = 16000
ENGS = ("pe", "act", "dve", "pool", "sp")


class Sync:
    def __init__(self, nc):
        self.nc = nc
        self.cnt = {e: 0 for e in ENGS}
        self.eng_sems = {e: [] for e in ENGS}
        self.streams = {}

    def eng_sem(self, e, n):
        k = (n - 1) // EPOCH
        while len(self.eng_sems[e]) <= k:
            self.eng_sems[e].append(self.nc.alloc_semaphore(name=f"s_{e}_{len(self.eng_sems[e])}"))
        return self.eng_sems[e][k], n - k * EPOCH

    def stream(self, name):
        if name not in self.streams:
            self.streams[name] = [self.nc.alloc_semaphore(name=f"d_{len(self.streams)}"), 0]
        return self.streams[name]


class Op:
    __slots__ = ("eng", "fn", "reads", "writes", "dma", "deps", "need_inc", "semval", "idx")

    def __init__(self, eng, fn, reads, writes, dma):
        self.eng = eng
        self.fn = fn
        self.reads = reads
        self.writes = writes
        self.dma = dma
        self.deps = ()
        self.need_inc = dma is not None
        self.semval = 0


class Prog:
    def __init__(self, sync):
        self.sync = sync
        self.nc = sync.nc
        self.ops = []

    def add(self, eng, fn, reads=(), writes=(), dma=None):
        self.ops.append(Op(eng, fn, tuple(reads), tuple(writes), dma))

    def mm(self, out, lhsT, rhs, start, stop, reads, writes):
        self.add("pe", lambda e: e.matmul(out, lhsT, rhs, start=start, stop=stop), reads, writes)

    def dma(self, eng, out, in_, stream, reads, writes):
        self.add(eng, lambda e: e.dma_start(out=out, in_=in_), reads, writes, dma=stream)

    def act(self, out, in_, func, reads, writes, **kw):
        self.add("act", lambda e: e.activation(out, in_, func, **kw), reads, writes)

    def stt(self, out, in0, scalar, in1, op0, op1, reads, writes):
        self.add("dve", lambda e: e.scalar_tensor_tensor(out, in0, scalar, in1, op0, op1), reads, writes)

    def tt(self, eng, out, in0, in1, op, reads, writes):
        self.add(eng, lambda e: e.tensor_tensor(out, in0, in1, op), reads, writes)

    def ts(self, eng, out, in0, s1, s2, op0, op1, reads, writes, accum_out=None):
        if accum_out is None:
            if op1 is None:
                self.add(eng, lambda e: e.tensor_scalar(out, in0, s1, None, op0), reads, writes)
            else:
                self.add(eng, lambda e: e.tensor_scalar(out, in0, s1, s2, op0, op1), reads, writes)
        else:
            self.add(eng, lambda e: e.tensor_scalar(out, in0, s1, s2, op0, op1, accum_out=accum_out), reads, writes)

    def copy(self, eng, out, in_, reads, writes):
        if eng == "act":
            self.add(eng, lambda e: e.copy(out, in_), reads, writes)
        else:
            self.add(eng, lambda e: e.tensor_copy(out, in_), reads, writes)

    def memset(self, eng, out, val, reads, writes):
        self.add(eng, lambda e: e.memset(out, val), reads, writes)

    def run(self):
        sync = self.sync
        ops = self.ops
        last_write = {}
        readers = {}
        for i, op in enumerate(ops):
            op.idx = i
            deps = set()
            for r in op.reads:
                lw = last_write.get(r)
                if lw is not None:
                    deps.add(lw)
            for w in op.writes:
                lw = last_write.get(w)
                if lw is not None:
                    deps.add(lw)
                for rd in readers.get(w, ()):
                    deps.add(rd)
            deps.discard(i)
            if op.eng == "pe" and op.dma is None:
                deps = {d for d in deps if not (ops[d].eng == "pe" and ops[d].dma is None)}
            op.deps = tuple(sorted(deps))
            for d in op.deps:
                ops[d].need_inc = True
            for r in op.reads:
                readers.setdefault(r, []).append(i)
            for w in op.writes:
                last_write[w] = i
                readers[w] = []
        per_eng = {e: [] for e in ENGS}
        for op in ops:
            per_eng[op.eng].append(op)
        for e in ENGS:
            for op in reversed(per_eng[e]):
                if op.dma is None:
                    op.need_inc = True
                    break
        used_streams = []
        for op in ops:
            if op.dma is not None:
                st = sync.stream(op.dma)
                st[1] += 1
                op.semval = 16 * st[1]
                if op.dma not in used_streams:
                    used_streams.append(op.dma)
            elif op.need_inc:
                sync.cnt[op.eng] += 1
                op.semval = sync.cnt[op.eng]

        def sem_of(op):
            if op.dma is not None:
                return sync.streams[op.dma][0], op.semval
            return sync.eng_sem(op.eng, op.semval)

        barrier = []
        for e in ENGS:
            if sync.cnt[e] > 0:
                barrier.append(sync.eng_sem(e, sync.cnt[e]))
        for s in used_streams:
            st = sync.streams[s]
            barrier.append((st[0], 16 * st[1]))

        def run_eng(eng_name, e):
            waited = {}
            for op in per_eng[eng_name]:
                need = {}
                for d in op.deps:
                    sem, val = sem_of(ops[d])
                    key = id(sem)
                    if need.get(key, (None, 0))[1] < val:
                        need[key] = (sem, val)
                for key, (sem, val) in need.items():
                    if waited.get(key, 0) >= val:
                        continue
                    e.wait_ge(sem, val)
                    waited[key] = val
                ins = op.fn(e)
                if op.dma is not None:
                    sem, _ = sem_of(op)
                    ins.then_inc(sem, 16)
                elif op.need_inc:
                    sem, _ = sem_of(op)
                    ins.then_inc(sem, 1)
            for sem, val in barrier:
                if waited.get(id(sem), 0) >= val:
                    continue
                e.wait_ge(sem, val)

        with self.nc.Block() as block:
            @block.tensor
            def _(e):
                run_eng("pe", e)

            @block.scalar
            def _(e):
                run_eng("act", e)

            @block.vector
            def _(e):
                run_eng("dve", e)

            @block.gpsimd
            def _(e):
                run_eng("pool", e)

            @block.sync
            def _(e):
                run_eng("sp", e)
        self.ops = []


import os
import numpy as np
import ml_dtypes

D = 1024
SEQ = 16384
NB = 2
H = 8
HD = 64
NKV = 2
NIH = 8
IDD = 32
PLE = 256
DFF = 4096
PAST = 4096
DSEQ = 32
DBATCH = 32
THETA = 500000.0
EPS = 1e-6
OFF = dict(qa=0, ka=512, va=640, qi=768, ki=1024, wi=1056, qb=1064, kb=1576, vb=2088, ga=2600, gb=3624)
NOWN = 4224
NTILE = 9
TOPK = 256
NEG = -1.0e30

C_ONES, C_BD64, C_R128, C_TRI, C_ID, C_R32, C_LT = 0, 128, 256, 384, 512, 640, 672
NCONST = 800

STAGE = int(os.environ.get("KSTAGE", "9"))


def rope_tab(pos, d):
    r = d // 4
    hr = r // 2
    inv = np.power(np.float32(THETA), -np.arange(hr, dtype=np.float32) * np.float32(2.0 / r)).astype(np.float32)
    ang = pos.astype(np.float32)[:, None] * inv[None, :]
    cos = np.cos(ang).astype(np.float32)
    sin = np.sin(ang).astype(np.float32)
    C = np.ones((d, pos.shape[0]), np.float32)
    S = np.zeros((d, pos.shape[0]), np.float32)
    C[:hr] = cos.T
    C[hr:r] = cos.T
    S[:hr] = sin.T
    S[hr:r] = sin.T
    return C, S


def rot_lhsT(d):
    r = d // 4
    hr = r // 2
    m = np.zeros((d, d), np.float32)
    for i in range(hr):
        m[i + hr, i] = -1.0
        m[i, i + hr] = 1.0
    return m


def make_consts():
    c = np.zeros((128, NCONST), np.float32)
    c[:, C_ONES:C_ONES + 128] = 1.0
    c[0:64, C_BD64:C_BD64 + 64] = 1.0
    c[64:128, C_BD64 + 64:C_BD64 + 128] = 1.0
    r64 = rot_lhsT(64)
    c[0:64, C_R128:C_R128 + 64] = r64
    c[64:128, C_R128 + 64:C_R128 + 128] = r64
    k = np.arange(128)
    c[:, C_TRI:C_TRI + 128] = (k[:, None] >= k[None, :]).astype(np.float32)
    c[:, C_ID:C_ID + 128] = np.eye(128, dtype=np.float32)
    c[0:32, C_R32:C_R32 + 32] = rot_lhsT(32)
    c[:, C_LT:C_LT + 128] = (k[:, None] < k[None, :]).astype(np.float32)
    return c.astype(ml_dtypes.bfloat16)


def own_tiles(c):
    out = []
    for t in range(8):
        b, m = divmod(t, 4)
        g = 8 * m + c
        out.append((b, 512 * g))
    return out


def gain_cols(g):
    return np.ascontiguousarray(g.reshape(-1, 128).T).astype(np.float32)


class Rot:
    def __init__(self, tiles, name, keys=None):
        self.tiles = tiles
        self.name = name
        self.keys = keys
        self.i = 0

    def next(self):
        j = self.i % len(self.tiles)
        self.i += 1
        return self.tiles[j], (self.keys[j] if self.keys else f"{self.name}{j}")


class Cx:
    pass


def declare(nc):
    cx = Cx()
    cx.nc = nc
    cx.sync = Sync(nc)

    def din(name, shape, dt=F32):
        return nc.dram_tensor(name, list(shape), dt, kind="ExternalInput").ap()

    def dout(name, shape, dt=F32):
        return nc.dram_tensor(name, list(shape), dt, kind="ExternalOutput").ap()

    def dscr(name, shape, dt=BF16):
        return nc.dram_tensor(name, list(shape), dt, kind=("ExternalOutput" if os.environ.get("KDBG") else "Internal")).ap()

    cx.xT_all = din("xT_all", [NB, D, SEQ])
    cx.xT_own = din("xT_own", [D, NOWN])
    cx.pT_own = din("pT_own", [PLE, NOWN])
    cx.w_in = din("w_in", [D, 4648])
    cx.g_mix = din("g_mix", [128, 8])
    cx.g_ffn = din("g_ffn", [128, 8])
    cx.g_ple = din("g_ple", [128, 8])
    cx.g_ka = din("g_ka", [128, 1])
    cx.g_qa = din("g_qa", [64, 1])
    cx.consts = din("consts", [128, NCONST], BF16)
    cx.c64a = din("c64a", [128, SEQ])
    cx.s64a = din("s64a", [128, SEQ])
    cx.c32a = din("c32a", [32, SEQ])
    cx.s32a = din("s32a", [32, SEQ])
    cx.c64o = din("c64o", [128, NOWN])
    cx.s64o = din("s64o", [128, NOWN])
    cx.c32o = din("c32o", [32, NOWN])
    cx.s32o = din("s32o", [32, NOWN])
    cx.proj_a = din("proj_a", [512, D])
    cx.proj_b = din("proj_b", [512, D])
    cx.w_out = din("w_out", [D, D])
    cx.w_up = din("w_up", [D, DFF])
    cx.w_down = din("w_down", [DFF, D])
    cx.w_ple = din("w_ple", [PLE, D])
    cx.w_pg = din("w_pg", [D, D])
    cx.negb = din("negb", [4, 128, 4096], BF16)
    cx.maskb = din("maskb", [128, 32, 512], BF16)
    cx.cakT = din("cakT", [4, NKV, HD, PAST])
    cx.cav = din("cav", [4, PAST, NKV * HD])
    cx.ckiT = din("ckiT", [4, IDD, PAST])
    cx.cbkT = din("cbkT", [4, H, HD, PAST])
    cx.cbv = din("cbv", [4, PAST, H * HD])
    cx.yT = dout("yT", [D, NOWN])
    cx.o_kaT = dout("o_kaT", [128, NOWN])
    cx.o_va = dout("o_va", [NOWN, 128])
    cx.o_kiT = dout("o_kiT", [32, NOWN])
    cx.o_kbT = dout("o_kbT", [512, NOWN])
    cx.o_vb = dout("o_vb", [NOWN, 512])
    cx.KBT = dscr("KBT", [NB, H, HD, SEQ])
    cx.VBS = dscr("VBS", [NB, H, 128, 128, HD])
    cx.KAT = dscr("KAT", [NB, NKV, HD, SEQ])
    cx.VAS = dscr("VAS", [NB, NKV, 128, 128, HD])
    cx.KIT = dscr("KIT", [NB, IDD, SEQ])
    cx.WS = dscr("WS", [30, 128, 8, 512])
    if os.environ.get("KDBG"):
        cx.dbg_ob = dout("dbg_ob", [64, 8, 512], BF16)
        cx.dbg_oa = dout("dbg_oa", [64, 8, 512], BF16)
        cx.dbg_q = dout("dbg_q", [64, 8, 512], BF16)
    cx.w_in_v = cx.w_in.rearrange("(k p) c -> p k c", p=128)
    cx.ps = [nc.alloc_psum_tensor(f"ps{i}", [128, 512], F32) for i in range(8)]
    return cx


def norm_ops(P, cx, xt, xkey, N, sq, lnv, rstd, hT, gcols, hkey, psb, tag):
    cst = cx.cst
    P.act(sq[:, :, 0:N], xt[:, :, 0:N], AF.Square, [xkey], [tag + "sq"])
    for k in range(8):
        P.mm(cx.ps[psb][:, 0:N], cst[:, C_ONES:C_ONES + 128], sq[:, k, 0:N], k == 0, k == 7,
             [tag + "sq", "cst"], [f"ps{psb}"])
    P.act(lnv[:, 0:N], cx.ps[psb][:, 0:N], AF.Ln, [f"ps{psb}"], [tag + "lnv"], bias=EPS, scale=1.0 / D)
    P.act(rstd[:, 0:N], lnv[:, 0:N], AF.Exp, [tag + "lnv"], [tag + "rstd"], scale=-0.5)
    for k in range(8):
        P.stt(hT[:, k, 0:N], xt[:, k, 0:N], gcols[:, k:k + 1], rstd[:, 0:N], ALU.mult, ALU.mult,
              [xkey, tag + "rstd", "gains"], [f"{hkey}{k}"])


def rope_ops(P, cx, np_, N, src_f32, src_key, srcb, rlhsT, psb, rc, rs, rkey, t1, t2, out, out_key, tag):
    P.copy("pool", srcb[0:np_, 0:N], src_f32[0:np_, 0:N], [src_key], [tag + "srcb"])
    P.mm(cx.ps[psb][0:np_, 0:N], rlhsT, srcb[0:np_, 0:N], True, True, [tag + "srcb", "cst"], [f"ps{psb}"])
    P.tt("dve", t1[0:np_, 0:N], src_f32[0:np_, 0:N], rc[0:np_, 0:N], ALU.mult, [src_key, rkey], [tag + "t1"])
    P.tt("dve", t2[0:np_, 0:N], cx.ps[psb][0:np_, 0:N], rs[0:np_, 0:N], ALU.mult, [f"ps{psb}", rkey], [tag + "t2"])
    P.tt("pool", out, t1[0:np_, 0:N], t2[0:np_, 0:N], ALU.add, [tag + "t1", tag + "t2"], [out_key])


def headnorm_ops(P, cx, np_, N, psrc, pkey, ka_sb, ksq, bdl, psb, klnv, krstd, gcol, kn, tag, ebias=None):
    P.copy("act", ka_sb[0:np_, 0:N], psrc, [pkey], [tag + "ka_sb"])
    P.act(ksq[0:np_, 0:N], psrc, AF.Square, [pkey], [tag + "ksq"])
    P.mm(cx.ps[psb][0:np_, 0:N], bdl, ksq[0:np_, 0:N], True, True, [tag + "ksq", "cst"], [f"ps{psb}"])
    P.act(klnv[0:np_, 0:N], cx.ps[psb][0:np_, 0:N], AF.Ln, [f"ps{psb}"], [tag + "klnv"], bias=EPS, scale=1.0 / HD)
    if ebias is None:
        P.act(krstd[0:np_, 0:N], klnv[0:np_, 0:N], AF.Exp, [tag + "klnv"], [tag + "krstd"], scale=-0.5)
    else:
        P.act(krstd[0:np_, 0:N], klnv[0:np_, 0:N], AF.Exp, [tag + "klnv", "gains"], [tag + "krstd"], scale=-0.5, bias=ebias)
    P.stt(kn[0:np_, 0:N], ka_sb[0:np_, 0:N], gcol, krstd[0:np_, 0:N], ALU.mult, ALU.mult,
          [tag + "ka_sb", tag + "krstd", "gains"], [tag + "kn"])


def phase1(cx, ntiles=64):
    from contextlib import ExitStack
    nc = cx.nc
    cst = cx.cst
    with ExitStack() as es:
        def sb(name, shape, dt):
            return es.enter_context(nc.sbuf_tensor("p1_" + name, shape, dt))
        wkv = sb("wkv", [128, 8, 1312], BF16)
        xts = Rot([sb(f"xt{i}", [128, 8, 512], F32) for i in range(2)], "xt")
        sqs = [sb(f"sq{i}", [128, 8, 512], BF16) for i in range(2)]
        lnvs = [sb(f"lnv{i}", [128, 512], F32) for i in range(2)]
        rstds = [sb(f"rstd{i}", [128, 512], F32) for i in range(2)]
        hTs = [sb(f"hT{i}", [128, 8, 512], BF16) for i in range(2)]
        ksts = Rot([sb(f"kst{i}", [128, 512], BF16) for i in range(3)], "kst")
        vbss = Rot([sb(f"vbs{i}", [128, 8, 4, 64], BF16) for i in range(2)], "vbs")
        vass = Rot([sb(f"vas{i}", [128, 2, 4, 64], BF16) for i in range(2)], "vas")
        ka_sb = sb("ka_sb", [128, 512], F32)
        ksq = sb("ksq", [128, 512], BF16)
        klnv = sb("klnv", [128, 512], F32)
        krstd = sb("krstd", [128, 512], F32)
        kn = sb("kn", [128, 512], F32)
        knb = sb("knb", [128, 512], BF16)
        t1 = sb("t1", [128, 512], F32)
        t2 = sb("t2", [128, 512], F32)
        rcs = Rot([sb(f"rc{i}", [128, 512], F32) for i in range(2)], "rc")
        rss = Rot([sb(f"rs{i}", [128, 512], F32) for i in range(2)], "rs")
        kaos = Rot([sb(f"kao{i}", [128, 512], BF16) for i in range(2)], "kao")
        ki_sb = sb("ki_sb", [32, 512], F32)
        kib = sb("kib", [32, 512], BF16)
        t1i = sb("t1i", [32, 512], F32)
        t2i = sb("t2i", [32, 512], F32)
        rcis = Rot([sb(f"rci{i}", [32, 512], F32) for i in range(2)], "rci")
        rsis = Rot([sb(f"rsi{i}", [32, 512], F32) for i in range(2)], "rsi")
        kios = Rot([sb(f"kio{i}", [32, 512], BF16) for i in range(2)], "kio")

        P = Prog(cx.sync)
        WC = dict(kb=0, vb=512, ka=1024, va=1152, ki=1280)
        WN = dict(kb=512, vb=512, ka=128, va=128, ki=32)
        for nm in WC:
            P.dma("pool", wkv[:, :, WC[nm]:WC[nm] + WN[nm]], cx.w_in_v[:, :, OFF[nm]:OFF[nm] + WN[nm]],
                  "p1w_" + nm, [], ["w_" + nm])
        kps = Rot([cx.ps[1], cx.ps[2]], "psk")
        vps = Rot([cx.ps[3], cx.ps[4]], "psv")
        for ti in range(ntiles):
            b, tt_ = divmod(ti, 32)
            t0 = tt_ * 512
            if ti == 0:
                pend = xts.next()
                P.dma("sp", pend[0][:], cx.xT_all[b].rearrange("(k p) t -> p k t", p=128)[:, :, t0:t0 + 512], pend[1], [],
                      [pend[1]])
            xt, xkey = pend
            rc, rck = rcs.next()
            rs, rsk = rss.next()
            rci, rcik = rcis.next()
            rsi, rsik = rsis.next()
            P.dma("sp", rc[:], cx.c64a[:, t0:t0 + 512], rck, [], [rck])
            P.dma("sp", rs[:], cx.s64a[:, t0:t0 + 512], rsk, [], [rsk])
            P.dma("sp", rci[:], cx.c32a[:, t0:t0 + 512], rcik, [], [rcik])
            P.dma("sp", rsi[:], cx.s32a[:, t0:t0 + 512], rsik, [], [rsik])
            if ti + 1 < ntiles:
                b2, tt2 = divmod(ti + 1, 32)
                pend = xts.next()
                P.dma("sp", pend[0][:], cx.xT_all[b2].rearrange("(k p) t -> p k t", p=128)[:, :, tt2 * 512:(tt2 + 1) * 512],
                      pend[1], [], [pend[1]])
            pp = ti % 2
            hT = hTs[pp]
            norm_ops(P, cx, xt, xkey, 512, sqs[pp], lnvs[pp], rstds[pp], hT, cx.g_mix_t, f"hT{pp}_", 0, f"p1{pp}")
            hkeys = [f"hT{pp}_{k}" for k in range(8)]
            for i in range(4):
                pk, pkk = kps.next()
                for k in range(8):
                    P.mm(pk[:], wkv[:, k, WC["kb"] + i * 128: WC["kb"] + (i + 1) * 128], hT[:, k, :], k == 0, k == 7,
                         [hkeys[k], "w_kb"], [pkk])
                kst, kstk = ksts.next()
                P.copy("act", kst[:], pk[:], [pkk], [kstk])
                P.dma("sp", cx.KBT[b, 2 * i:2 * i + 2].rearrange("h d t -> (h d) t")[:, t0:t0 + 512], kst[:], kstk,
                      [kstk], [])
            pka, pkak = kps.next()
            for k in range(8):
                P.mm(pka[:], wkv[:, k, WC["ka"]:WC["ka"] + 128], hT[:, k, :], k == 0, k == 7, [hkeys[k], "w_ka"], [pkak])
            P.copy("act", ka_sb[:], pka[:], [pkak], ["p1kaka_sb"])
            P.act(ksq[:], pka[:], AF.Square, [pkak], ["p1kaksq"])
            pki, pkik = kps.next()
            for k in range(8):
                P.mm(pki[0:32, :], wkv[:, k, WC["ki"]:WC["ki"] + 32], hT[:, k, :], k == 0, k == 7, [hkeys[k], "w_ki"], [pkik])
            P.copy("act", ki_sb[:], pki[0:32, :], [pkik], ["p1ki_sb"])
            P.copy("pool", kib[:], ki_sb[:], ["p1ki_sb"], ["p1kib"])
            vbs, vbsk = vbss.next()
            vas, vask = vass.next()
            for j in range(4):
                pv, pvk = vps.next()
                for k in range(8):
                    P.mm(pv[:], hT[:, k, j * 128:(j + 1) * 128], wkv[:, k, WC["vb"]:WC["vb"] + 512], k == 0, k == 7,
                         [hkeys[k], "w_vb"], [pvk])
                P.copy("dve", vbs[:, :, j, :], pv[:].rearrange("p (h d) -> p h d", h=8), [pvk], [vbsk])
                pv, pvk = vps.next()
                for k in range(8):
                    P.mm(pv[:, 0:128], hT[:, k, j * 128:(j + 1) * 128], wkv[:, k, WC["va"]:WC["va"] + 128], k == 0, k == 7,
                         [hkeys[k], "w_va"], [pvk])
                P.copy("dve", vas[:, :, j, :], pv[:, 0:128].rearrange("p (h d) -> p h d", h=2), [pvk], [vask])
            P.mm(cx.ps[5][:], cst[:, C_BD64:C_BD64 + 128], ksq[:], True, True, ["p1kaksq", "cst"], ["ps5"])
            P.act(klnv[:], cx.ps[5][:], AF.Ln, ["ps5"], ["p1kaklnv"], bias=EPS, scale=1.0 / HD)
            P.act(krstd[:], klnv[:], AF.Exp, ["p1kaklnv"], ["p1kakrstd"], scale=-0.5)
            P.stt(kn[:], ka_sb[:], cx.g_ka_t[:, 0:1], krstd[:], ALU.mult, ALU.mult, ["p1kaka_sb", "p1kakrstd", "gains"],
                  ["p1kakn"])
            kao, kaok = kaos.next()
            P.copy("pool", knb[:], kn[:], ["p1kakn"], ["p1kasrcb"])
            P.mm(cx.ps[6][:], cst[:, C_R128:C_R128 + 128], knb[:], True, True, ["p1kasrcb", "cst"], ["ps6"])
            P.tt("dve", t1[:], kn[:], rc[:], ALU.mult, ["p1kakn", rck], ["p1t1"])
            P.tt("dve", t2[:], cx.ps[6][:], rs[:], ALU.mult, ["ps6", rsk], ["p1t2"])
            P.tt("pool", kao[:], t1[:], t2[:], ALU.add, ["p1t1", "p1t2"], [kaok])
            P.dma("sp", cx.KAT[b].rearrange("h d t -> (h d) t")[:, t0:t0 + 512], kao[:], kaok, [kaok], [])
            P.mm(cx.ps[7][0:32, :], cst[0:32, C_R32:C_R32 + 32], kib[:], True, True, ["p1kib", "cst"], ["ps7"])
            kio, kiok = kios.next()
            P.tt("dve", t1i[:], ki_sb[:], rci[:], ALU.mult, ["p1ki_sb", rcik], ["p1t1i"])
            P.tt("dve", t2i[:], cx.ps[7][0:32, :], rsi[:], ALU.mult, ["ps7", rsik], ["p1t2i"])
            P.tt("pool", kio[:], t1i[:], t2i[:], ALU.add, ["p1t1i", "p1t2i"], [kiok])
            P.dma("sp", cx.KIT[b][:, t0:t0 + 512], kio[:], kiok, [kiok], [])
            blk0 = t0 // 128
            P.dma("sp", cx.VBS[b, :, :, blk0:blk0 + 4, :].rearrange("h p j d -> p h j d"), vbs[:], vbsk, [vbsk], [])
            P.dma("sp", cx.VAS[b, :, :, blk0:blk0 + 4, :].rearrange("h p j d -> p h j d"), vas[:], vask, [vask], [])
        P.run()


def load_consts(cx, es):
    nc = cx.nc

    def sb(name, shape, dt):
        return es.enter_context(nc.sbuf_tensor("c_" + name, shape, dt))
    cx.cst = sb("cst", [128, NCONST], BF16)
    cx.g_mix_t = sb("g_mix", [128, 8], F32)
    cx.g_ffn_t = sb("g_ffn", [128, 8], F32)
    cx.g_ple_t = sb("g_ple", [128, 8], F32)
    cx.g_ka_t = sb("g_ka", [128, 1], F32)
    cx.g_qa_t = sb("g_qa", [64, 1], F32)
    P = Prog(cx.sync)
    P.dma("sp", cx.cst[:], cx.consts, "c0", [], ["cst"])
    P.dma("sp", cx.g_mix_t[:], cx.g_mix, "c1", [], ["gains"])
    P.dma("sp", cx.g_ffn_t[:], cx.g_ffn, "c2", [], ["gains"])
    P.dma("sp", cx.g_ple_t[:], cx.g_ple, "c3", [], ["gains"])
    P.dma("sp", cx.g_ka_t[:], cx.g_ka, "c4", [], ["gains"])
    P.dma("sp", cx.g_qa_t[:], cx.g_qa, "c5", [], ["gains"])
    P.run()


def build(stage=STAGE, p1_tiles=64, tiles=range(9)):
    from contextlib import ExitStack
    nc = bass.Bass("TRN2", target_bir_lowering=False)
    cx = declare(nc)
    with ExitStack() as es:
        load_consts(cx, es)
        alloc_persist(cx, es)
        if stage >= 5:
            prep_weights(cx)
        phase1(cx, p1_tiles)
        if stage >= 2:
            for ti in tiles:
                with ExitStack() as es_t:
                    alloc_q(cx, es_t, ti)
                    proj_stage(cx, ti)
                    if stage >= 3 and ti < 8:
                        battn_prompt(cx, ti)
                    if stage >= 4 and ti < 8:
                        aattn_prompt(cx, ti)
                    if stage >= 4 and ti == 8:
                        attn_sample(cx)
                if stage >= 5:
                    stage4(cx, ti)
    return nc


def host_shared(inp):
    sh = {}
    sh["xT_all"] = np.ascontiguousarray(inp["x_prompt"].transpose(0, 2, 1))
    sh["pT_all"] = np.ascontiguousarray(inp["p_prompt"][0].transpose(0, 2, 1))
    sh["consts"] = make_consts()
    pos = np.arange(SEQ)
    c64, s64 = rope_tab(pos, 64)
    c32, s32 = rope_tab(pos, 32)
    sh["c64a"] = np.ascontiguousarray(np.concatenate([c64, c64], 0))
    sh["s64a"] = np.ascontiguousarray(np.concatenate([s64, s64], 0))
    sh["c32a"] = c32
    sh["s32a"] = s32
    sh["tabs"] = (c64, s64, c32, s32)
    poss = PAST + np.arange(DSEQ)
    sh["tabs_s"] = rope_tab(poss, 64) + rope_tab(poss, 32)
    for nm in ("w_in", "proj_a", "proj_b", "w_out", "w_up", "w_down", "w_ple"):
        sh[nm] = np.ascontiguousarray(inp[nm][0])
    sh["w_pg"] = np.ascontiguousarray(inp["w_ple_gate"][0])
    sh["g_mix"] = gain_cols(inp["norm_mix"][0])
    sh["g_ffn"] = gain_cols(inp["norm_ffn"][0])
    sh["g_ple"] = gain_cols(inp["norm_ple"][0])
    sh["g_ka"] = np.ascontiguousarray(np.tile(inp["g_ka"][0], 2)[:, None]).astype(np.float32)
    sh["g_qa"] = np.ascontiguousarray(inp["g_qa"][0][:, None]).astype(np.float32)
    return sh


def host_core(c, inp, sh):
    m = {}
    for nm in ("xT_all", "consts", "c64a", "s64a", "c32a", "s32a", "w_in", "proj_a", "proj_b", "w_out", "w_up",
               "w_down", "w_ple", "w_pg", "g_mix", "g_ffn", "g_ple", "g_ka", "g_qa"):
        m[nm] = sh[nm]
    tiles = own_tiles(c)
    xo = np.empty((D, NOWN), np.float32)
    po = np.empty((PLE, NOWN), np.float32)
    c64, s64, c32, s32 = sh["tabs"]
    c64o = np.empty((128, NOWN), np.float32)
    s64o = np.empty((128, NOWN), np.float32)
    c32o = np.empty((32, NOWN), np.float32)
    s32o = np.empty((32, NOWN), np.float32)
    for t, (b, q0) in enumerate(tiles):
        sl = slice(t * 512, (t + 1) * 512)
        xo[:, sl] = sh["xT_all"][b][:, q0:q0 + 512]
        po[:, sl] = sh["pT_all"][b][:, q0:q0 + 512]
        c64o[0:64, sl] = c64[:, q0:q0 + 512]
        s64o[0:64, sl] = s64[:, q0:q0 + 512]
        c32o[:, sl] = c32[:, q0:q0 + 512]
        s32o[:, sl] = s32[:, q0:q0 + 512]
    sl = slice(4096, NOWN)
    xo[:, sl] = inp["x_sample"][4 * c:4 * c + 4].reshape(128, D).T
    po[:, sl] = inp["p_sample"][0][4 * c:4 * c + 4].reshape(128, PLE).T
    cs64, ss64, cs32, ss32 = sh["tabs_s"]
    c64o[0:64, sl] = np.tile(cs64, (1, 4))
    s64o[0:64, sl] = np.tile(ss64, (1, 4))
    c32o[:, sl] = np.tile(cs32, (1, 4))
    s32o[:, sl] = np.tile(ss32, (1, 4))
    c64o[64:128] = c64o[0:64]
    s64o[64:128] = s64o[0:64]
    m["xT_own"] = xo
    m["pT_own"] = po
    m["c64o"], m["s64o"], m["c32o"], m["s32o"] = c64o, s64o, c32o, s32o
    kz = np.arange(4096)[None, :]
    q = np.arange(128)[:, None]
    negb = np.empty((4, 128, 4096), np.float32)
    for j in range(4):
        lim = 512 * c + 128 * j + (q // 64 + 1) * 64
        negb[j] = np.where(kz < lim, 0.0, NEG)
    m["negb"] = negb.astype(ml_dtypes.bfloat16)
    kk = np.arange(128)[:, None, None]
    bl = np.arange(32)[None, :, None]
    qq = np.arange(512)[None, None, :]
    m["maskb"] = ((128 * bl + kk) < (512 * c + qq)).astype(np.float32).astype(ml_dtypes.bfloat16)
    sq_ = slice(4 * c, 4 * c + 4)
    m["cakT"] = np.ascontiguousarray(inp["cache_a_k"][0][sq_].transpose(0, 2, 3, 1))
    m["cav"] = np.ascontiguousarray(inp["cache_a_v"][0][sq_].reshape(4, PAST, NKV * HD))
    m["ckiT"] = np.ascontiguousarray(inp["cache_a_kidx"][0][sq_].transpose(0, 2, 1))
    m["cbkT"] = np.ascontiguousarray(inp["cache_b_k"][0][sq_].transpose(0, 2, 3, 1))
    m["cbv"] = np.ascontiguousarray(inp["cache_b_v"][0][sq_].reshape(4, PAST, H * HD))
    return m


import math


def alloc_persist(cx, es):
    nc = cx.nc

    def sb(name, shape, dt):
        return es.enter_context(nc.sbuf_tensor("pp_" + name, shape, dt))
    cx.hT = sb("hT", [128, 8, 512], BF16)
    cx.oaT = sb("oaT", [64, 8, 512], BF16)
    cx.obT = sb("obT", [64, 8, 512], BF16)
    cx.ones_f = sb("ones_f", [128, 64], F32)
    cx.kbnT = sb("kbnT", [64, 8, 128], BF16)
    cx.kanT = sb("kanT", [64, 2, 128], BF16)
    cx.kinT = sb("kinT", [32, 128], BF16)
    cx.vbn = sb("vbn", [32, 4, 512], BF16)
    cx.van = sb("van", [32, 4, 2, 65], BF16)


def alloc_q(cx, es, ti):
    nc = cx.nc

    def sb(name, shape, dt):
        return es.enter_context(nc.sbuf_tensor(f"q{ti}_" + name, shape, dt))
    cx.qbT = sb("qbT", [128, 8, 512], BF16)
    cx.qaT = sb("qaT", [128, 8, 512], BF16)
    cx.qiT = sb("qiT", [128, 8, 512], BF16)
    cx.wiT = sb("wiT", [128, 4, 8], F32)


def proj_stage(cx, ti):
    from contextlib import ExitStack
    nc = cx.nc
    cst = cx.cst
    sample = ti == 8
    N = 128 if sample else 512
    c0 = ti * 512
    gs = 32 if sample else 128
    ng = N // gs
    with ExitStack() as es:
        def sb(name, shape, dt):
            return es.enter_context(nc.sbuf_tensor(f"pj{ti}_" + name, shape, dt))
        wq = sb("wq", [128, 8, 2600], BF16)
        xt = sb("xt", [128, 8, 512], F32)
        sq = sb("sq", [128, 8, 512], BF16)
        lnv = sb("lnv", [128, 512], F32)
        rstd = sb("rstd", [128, 512], F32)
        rc64 = sb("rc64", [128, 512], F32)
        rs64 = sb("rs64", [128, 512], F32)
        rc32 = sb("rc32", [32, 512], F32)
        rs32 = sb("rs32", [32, 512], F32)
        ka_sb = sb("ka_sb", [64, 512], F32)
        ksq = sb("ksq", [64, 512], BF16)
        klnv = sb("klnv", [64, 512], F32)
        krstd = sb("krstd", [64, 512], F32)
        kn = sb("kn", [64, 512], F32)
        knb = sb("knb", [64, 512], BF16)
        t1 = sb("t1", [64, 512], F32)
        t2 = sb("t2", [64, 512], F32)
        kouts = Rot([sb(f"kout{i}", [64, 512], F32) for i in range(2)], "kout")
        vouts = Rot([sb(f"vout{i}", [128, 640], F32) for i in range(2)], "vout")
        P = Prog(cx.sync)
        for i, (a, b_) in enumerate([(0, 650), (650, 1300), (1300, 1950), (1950, 2600)]):
            P.dma("pool", wq[:, :, a:b_], cx.w_in_v[:, :, a:b_], f"pjw{i}", [], [f"wq{i}"])
        P.dma("sp", xt[:, :, 0:N], cx.xT_own.rearrange("(k p) t -> p k t", p=128)[:, :, c0:c0 + N], "pjx", [], ["xt"])
        P.dma("sp", rc64[:, 0:N], cx.c64o[:, c0:c0 + N], "pjr0", [], ["rt_c64"])
        P.dma("sp", rs64[:, 0:N], cx.s64o[:, c0:c0 + N], "pjr1", [], ["rt_s64"])
        P.dma("sp", rc32[:, 0:N], cx.c32o[:, c0:c0 + N], "pjr2", [], ["rt_c32"])
        P.dma("sp", rs32[:, 0:N], cx.s32o[:, c0:c0 + N], "pjr3", [], ["rt_s32"])
        P.memset("pool", cx.ones_f[:], 1.0, [], ["ones_f"])
        P.memset("pool", cx.qiT[32:64, :, :], 0.0, [], ["qiTpad"])
        P.memset("pool", cx.qiT[64:128, :, :], 0.0, [], ["qiTpad"])
        P.memset("pool", cx.qbT[64:128, :, :], 0.0, [], ["qbTpad"])
        P.memset("pool", cx.qaT[64:128, :, :], 0.0, [], ["qaTpad"])
        norm_ops(P, cx, xt, "xt", N, sq, lnv, rstd, cx.hT, cx.g_mix_t, "hT", 0, "pj")
        hkeys = [f"hT{k}" for k in range(8)]
        pss = Rot([cx.ps[1], cx.ps[2], cx.ps[3], cx.ps[4]], "psj")

        def proj_fm(col, M):
            pk, pkk = pss.next()
            for k in range(8):
                P.mm(pk[0:M, 0:N], wq[:, k, col:col + M], cx.hT[:, k, 0:N], k == 0, k == 7, [hkeys[k], "wq0", "wq1", "wq2", "wq3"], [pkk])
            return pk, pkk

        def rope(np_, src, skey, rl, rc, rs, out, okey, tag):
            P.copy("pool", knb[0:np_, 0:N], src[0:np_, 0:N], [skey], [tag + "knb"])
            P.mm(cx.ps[6][0:np_, 0:N], rl, knb[0:np_, 0:N], True, True, [tag + "knb", "cst"], ["ps6"])
            P.tt("dve", t1[0:np_, 0:N], src[0:np_, 0:N], rc[0:np_, 0:N], ALU.mult, [skey, "rt_c64" if np_ == 64 else "rt_c32"], [tag + "t1"])
            P.tt("dve", t2[0:np_, 0:N], cx.ps[6][0:np_, 0:N], rs[0:np_, 0:N], ALU.mult, ["ps6", "rt_s64" if np_ == 64 else "rt_s32"], [tag + "t2"])
            P.tt("pool", out, t1[0:np_, 0:N], t2[0:np_, 0:N], ALU.add, [tag + "t1", tag + "t2"], [okey])

        r64l = cst[0:64, C_R128:C_R128 + 64]
        r32l = cst[0:32, C_R32:C_R32 + 32]
        ones64 = cst[0:64, C_ONES:C_ONES + 64]
        for h in range(8):
            pk, pkk = proj_fm(OFF["qb"] + 64 * h, 64)
            P.add("act", (lambda o, i: lambda e: e.mul(o, i, 0.125))(cx.qbT[0:64, h, 0:N], pk[0:64, 0:N]), [pkk], [f"qbT{h}"])
        for h in range(8):
            pk, pkk = proj_fm(OFF["qa"] + 64 * h, 64)
            headnorm_ops(P, cx, 64, N, pk[0:64, 0:N], pkk, ka_sb, ksq, ones64, 5, klnv, krstd, cx.g_qa_t[:, 0:1], kn,
                         "pj", ebias=math.log(0.125))
            rope(64, kn, "pjkn", r64l, rc64, rs64, cx.qaT[0:64, h, 0:N], f"qaT{h}", "pj")
        for h in range(8):
            pk, pkk = proj_fm(OFF["qi"] + 32 * h, 32)
            P.add("act", (lambda o, i: lambda e: e.mul(o, i, IDD ** -0.5))(kn[0:32, 0:N], pk[0:32, 0:N]), [pkk], ["pjkn"])
            rope(32, kn, "pjkn", r32l, rc32, rs32, cx.qiT[0:32, h, 0:N], f"qiT{h}", "pj")
        for jb in range(ng):
            pk, pkk = pss.next()
            for k in range(8):
                P.mm(pk[0:gs, 0:8], cx.hT[:, k, jb * gs:(jb + 1) * gs], wq[:, k, OFF["wi"]:OFF["wi"] + 8], k == 0, k == 7,
                     [hkeys[k], "wq0", "wq1", "wq2", "wq3"], [pkk])
            P.ts("dve", cx.wiT[0:gs, jb, :], pk[0:gs, 0:8], NIH ** -0.5, None, ALU.mult, None, [pkk], ["wiT"])
        for h in range(8):
            pk, pkk = proj_fm(OFF["kb"] + 64 * h, 64)
            ko, kok = kouts.next()
            P.copy("act", ko[0:64, 0:N], pk[0:64, 0:N], [pkk], [kok])
            P.dma("sp", cx.o_kbT[64 * h:64 * h + 64, c0:c0 + N], ko[0:64, 0:N], kok, [kok], [])
            if sample:
                P.copy("pool", cx.kbnT[0:64, h, :], ko[0:64, 0:N], [kok], [f"kbnT{h}"])
        for kv in range(2):
            pk, pkk = proj_fm(OFF["ka"] + 64 * kv, 64)
            headnorm_ops(P, cx, 64, N, pk[0:64, 0:N], pkk, ka_sb, ksq, ones64, 5, klnv, krstd, cx.g_ka_t[0:64, 0:1], kn,
                         "pj")
            ko, kok = kouts.next()
            rope(64, kn, "pjkn", r64l, rc64, rs64, ko[0:64, 0:N], kok, "pj")
            P.dma("sp", cx.o_kaT[64 * kv:64 * kv + 64, c0:c0 + N], ko[0:64, 0:N], kok, [kok], [])
            if sample:
                P.copy("pool", cx.kanT[0:64, kv, :], ko[0:64, 0:N], [kok], [f"kanT{kv}"])
        pk, pkk = proj_fm(OFF["ki"], 32)
        P.copy("act", kn[0:32, 0:N], pk[0:32, 0:N], [pkk], ["pjkn"])
        ko, kok = kouts.next()
        rope(32, kn, "pjkn", r32l, rc32, rs32, ko[0:32, 0:N], kok, "pj")
        P.dma("sp", cx.o_kiT[:, c0:c0 + N], ko[0:32, 0:N], kok, [kok], [])
        if sample:
            P.copy("pool", cx.kinT[0:32, :], ko[0:32, 0:N], [kok], ["kinT"])
        for g in range(ng):
            pv, pvk = pss.next()
            pa, pak = pss.next()
            for k in range(8):
                P.mm(pv[0:gs, :], cx.hT[:, k, g * gs:(g + 1) * gs], wq[:, k, OFF["vb"]:OFF["vb"] + 512], k == 0, k == 7,
                     [hkeys[k], "wq0", "wq1", "wq2", "wq3"], [pvk])
            for k in range(8):
                P.mm(pa[0:gs, 0:128], cx.hT[:, k, g * gs:(g + 1) * gs], wq[:, k, OFF["va"]:OFF["va"] + 128], k == 0, k == 7,
                     [hkeys[k], "wq0", "wq1", "wq2", "wq3"], [pak])
            vo, vok = vouts.next()
            P.copy("dve", vo[0:gs, 0:512], pv[0:gs, :], [pvk], [vok + "b"])
            P.copy("dve", vo[0:gs, 512:640], pa[0:gs, 0:128], [pak], [vok + "a"])
            P.dma("sp", cx.o_vb[c0 + g * gs:c0 + (g + 1) * gs, :], vo[0:gs, 0:512], vok + "b", [vok + "b"], [])
            P.dma("sp", cx.o_va[c0 + g * gs:c0 + (g + 1) * gs, :], vo[0:gs, 512:640], vok + "a", [vok + "a"], [])
            if sample:
                P.copy("pool", cx.vbn[0:32, g, :], vo[0:32, 0:512], [vok + "b"], [f"vbn{g}"])
                P.copy("pool", cx.van[0:32, g, :, 0:64], vo[0:32, 512:640].rearrange("p (h d) -> p h d", h=2),
                       [vok + "a"], [f"van{g}"])
                P.memset("pool", cx.van[0:32, g, :, 64:65], 1.0, [], [f"van1{g}"])
        P.run()


def battn_prompt(cx, ti):
    from contextlib import ExitStack
    nc = cx.nc
    cst = cx.cst
    b, m = divmod(ti, 4)
    nkb = 32 * m + 32
    z0 = 32 * m
    with ExitStack() as es:
        def sb(name, shape, dt):
            return es.enter_context(nc.sbuf_tensor(f"ba{ti}_" + name, shape, dt))
        mk = sb("mk", [128, 32, 512], BF16)
        kch_t = [sb(f"kch{i}", [128, 2048], BF16) for i in range(3)]
        vch_t = [sb(f"vch{i}", [128, 16, 128], BF16) for i in range(3)]
        kchs = Rot(kch_t, "kch")
        vchs = Rot(vch_t, "vch")
        e2s = Rot([sb(f"e2{i}", [128, 512], F32) for i in range(4)], "e2")
        nlks = Rot([sb(f"nlk{i}", [128, 512], BF16) for i in range(3)], "nlk")
        ggs = Rot([sb(f"gg{i}", [128, 512], F32) for i in range(2)], "gg")
        aas = Rot([sb(f"aa{i}", [128, 512], BF16) for i in range(3)], "aa")
        Ss = [sb(f"S{i}", [128, 512], BF16) for i in range(2)]
        P = Prog(cx.sync)
        P.dma("sp", mk[:], cx.maskb, "bamk", [], ["mk"])
        for i in range(3):
            P.memset("pool", kch_t[i][64:128, :], 0.0, [], [f"kch{i}pad"])
            P.memset("pool", vch_t[i][:, :, 64:128], 0.0, [], [f"vch{i}pad"])
        pzs = Rot([cx.ps[0], cx.ps[1]], "pz")
        pcs = Rot([cx.ps[2], cx.ps[3]], "pc")
        pos_ = [cx.ps[4], cx.ps[5]]
        tri = cst[:, C_TRI:C_TRI + 128]
        ones = cst[:, C_ONES:C_ONES + 128]
        steps = []
        for h in range(8):
            for kc in reversed(range(nkb // 16)):
                for bi in reversed(range(16)):
                    steps.append(dict(h=h, kc=kc, bi=bi, blk=kc * 16 + bi,
                                      first=(kc == nkb // 16 - 1 and bi == 15), last=(kc == 0 and bi == 0)))
        n = len(steps)
        chunk_of = {}
        order = []
        for st in steps:
            key = (st["h"], st["kc"])
            if key not in chunk_of:
                chunk_of[key] = None
                order.append(key)

        def load_chunk(idx):
            h, kc = order[idx]
            kch, kchk = kchs.next()
            vch, vchk = vchs.next()
            P.dma("sp", kch[0:64, :], cx.KBT[b, h, :, kc * 2048:(kc + 1) * 2048], kchk, [], [kchk])
            P.dma("sp", vch[:, :, 0:64], cx.VBS[b, h, :, kc * 16:(kc + 1) * 16, :], vchk, [], [vchk])
            chunk_of[(h, kc)] = (kch, kchk, vch, vchk)
        load_chunk(0)
        if len(order) > 1:
            load_chunk(1)
        next_load = [2]

        def st_z(i):
            st = steps[i]
            if st["bi"] == 15 and next_load[0] < len(order) and order.index((st["h"], st["kc"])) + 2 == next_load[0] + 0:
                pass
            kch, kchk, vch, vchk = chunk_of[(st["h"], st["kc"])]
            pz, pzk = pzs.next()
            st["pz"] = (pz, pzk)
            P.mm(pz[:], kch[:, st["bi"] * 128:(st["bi"] + 1) * 128], cx.qbT[:, st["h"], :], True, True,
                 [kchk, kchk + "pad", "qbTpad", f"qbT{st['h']}"], [pzk])
            if st["bi"] == 0 and next_load[0] < len(order):
                load_chunk(next_load[0])
                next_load[0] += 1

        def st_act1(i):
            st = steps[i]
            pz, pzk = st["pz"]
            e2, e2k = e2s.next()
            st["e2"] = (e2, e2k)
            P.act(e2[:], pz[:], AF.Exp, [pzk], [e2k])
            if st["blk"] >= z0:
                P.tt("pool", e2[:], e2[:], mk[:, st["blk"] - z0, :], ALU.mult, [e2k, "mk"], [e2k])

        def st_ln(i):
            st = steps[i]
            e2, e2k = st["e2"]
            nlk, nlkk = nlks.next()
            st["nlk"] = (nlk, nlkk)
            P.act(nlk[:], e2[:], AF.Ln, [e2k], [nlkk], bias=1.0)

        def st_c(i):
            st = steps[i]
            S = Ss[st["h"] % 2]
            Sk = f"S{st['h'] % 2}"
            nlk, nlkk = st["nlk"]
            pc, pck = pcs.next()
            st["pc"] = (pc, pck)
            if st["first"]:
                P.mm(pc[:], tri, nlk[:], True, True, [nlkk, "cst"], [pck])
                P.copy("dve", S[:], nlk[:], [nlkk], [Sk])
            else:
                P.mm(pc[:], tri, nlk[:], True, False, [nlkk, "cst"], [pck])
                P.mm(pc[:], ones, S[:], False, True, [Sk, "cst"], [pck])
                if not st["last"]:
                    P.tt("dve", S[:], S[:], nlk[:], ALU.add, [Sk, nlkk], [Sk])

        def st_act2(i):
            st = steps[i]
            pc, pck = st["pc"]
            gg, ggk = ggs.next()
            st["gg"] = (gg, ggk)
            P.act(gg[:], pc[:], AF.Exp, [pck], [ggk], scale=-1.0)

        def st_a(i):
            st = steps[i]
            e2, e2k = st["e2"]
            gg, ggk = st["gg"]
            aa, aak = aas.next()
            st["aa"] = (aa, aak)
            P.tt("dve", aa[:], e2[:], gg[:], ALU.mult, [e2k, ggk], [aak])

        def st_av(i):
            st = steps[i]
            h = st["h"]
            kch, kchk, vch, vchk = chunk_of[(h, st["kc"])]
            aa, aak = st["aa"]
            po = pos_[h % 2]
            P.mm(po[:, :], vch[:, st["bi"], :], aa[:], st["first"], st["last"], [aak, vchk, vchk + "pad"], [f"po{h % 2}"])
            if st["last"]:
                P.copy("dve", cx.obT[0:64, h, :], po[0:64, :], [f"po{h % 2}"], [f"obT{h}"])

        st_z(0)
        if n > 1:
            st_z(1)
        st_act1(0)
        for t in range(n + 2):
            if t + 2 < n:
                st_z(t + 2)
            if t + 1 < n:
                st_act1(t + 1)
            if t < n:
                st_ln(t)
            if 0 <= t - 1 < n:
                st_c(t - 1)
                st_act2(t - 1)
                st_a(t - 1)
            if 0 <= t - 2 < n:
                st_av(t - 2)
        if os.environ.get("KDBG"):
            P.dma("sp", cx.dbg_ob, cx.obT[:], "dbgob", [f"obT{h}" for h in range(8)], [])
        P.run()


NBIS = 16


def topk_threshold(P, cx, isc, ikeys, nk, sm, junk, tag, np_=128):
    lo, hi, mid, cnt, pred, d1, d2 = (sm[k][0:np_] for k in ("lo", "hi", "mid", "cnt", "pred", "d1", "d2"))
    for it in range(NBIS):
        P.tt("dve", mid[:], lo[:], hi[:], ALU.add, [tag + "lo", tag + "hi"], [tag + "mid"])
        P.ts("dve", mid[:], mid[:], 0.5, None, ALU.mult, None, [tag + "mid"], [tag + "mid"])
        nch = (nk + 2047) // 2048
        for cch in range(nch):
            w = min(2048, nk - cch * 2048)
            seed = 0.0 if cch == 0 else cnt[:, 0:1]
            rk = [ikeys[i] for i in range(cch * 4, min(len(ikeys), cch * 4 + 4))]
            P.ts("dve", junk[0:np_, 0:w], isc[0:np_, cch * 2048:cch * 2048 + w], mid[:, 0:1], seed, ALU.is_gt, ALU.add,
                 rk + [tag + "mid", tag + "cnt"], [tag + "junk", tag + "cnt"], accum_out=cnt[:, 0:1])
        P.ts("dve", pred[:], cnt[:], float(TOPK), None, ALU.is_ge, None, [tag + "cnt"], [tag + "pred"])
        P.tt("dve", d1[:], mid[:], lo[:], ALU.subtract, [tag + "mid", tag + "lo"], [tag + "d1"])
        P.tt("dve", d2[:], hi[:], mid[:], ALU.subtract, [tag + "mid", tag + "hi"], [tag + "d2"])
        P.stt(lo[:], d1[:], pred[:, 0:1], lo[:], ALU.mult, ALU.add, [tag + "d1", tag + "pred", tag + "lo"], [tag + "lo"])
        P.stt(hi[:], d2[:], pred[:, 0:1], mid[:], ALU.mult, ALU.add, [tag + "d2", tag + "pred", tag + "mid"], [tag + "hi"])


def topk_threshold_split(P, cx, isc, ikeys, nk, sm, junk, junkA, sgn, tag):
    lo, hi, mid, cnt, pred, d1, d2, nmid, ssum, tot = (sm[k] for k in ("lo", "hi", "mid", "cnt", "pred", "d1", "d2",
                                                                          "nmid", "ssum", "tot"))
    nD = nk // 2
    nA = nk - nD
    nchA = (nA + 2047) // 2048
    for it in range(NBIS):
        P.tt("dve", d1[:], lo[:], hi[:], ALU.add, [tag + "lo", tag + "hi"], [tag + "d1"])
        P.ts("dve", mid[:], d1[:], 0.5, None, ALU.mult, None, [tag + "d1"], [tag + "mid"])
        P.ts("dve", nmid[:], d1[:], -0.5, None, ALU.mult, None, [tag + "d1"], [tag + "nmid"])
        for cch in range(nchA):
            w = min(2048, nA - cch * 2048)
            a0_ = nD + cch * 2048
            rk = [ikeys[i] for i in range(a0_ // 512, (a0_ + w) // 512)]
            P.act(junkA[:, 0:w], isc[:, a0_:a0_ + w], AF.Sign, rk + [tag + "nmid"], [tag + "junkA", f"{tag}sgn{cch}"],
                  bias=nmid[:, 0:1], scale=1.0, accum_out=sgn[:, cch:cch + 1])
        nch = (nD + 2047) // 2048
        for cch in range(nch):
            w = min(2048, nD - cch * 2048)
            seed = 0.0 if cch == 0 else cnt[:, 0:1]
            rk = [ikeys[i] for i in range(cch * 4, min(len(ikeys), cch * 4 + 4))]
            P.ts("dve", junk[:, 0:w], isc[:, cch * 2048:cch * 2048 + w], mid[:, 0:1], seed, ALU.is_gt, ALU.add,
                 rk + [tag + "mid", tag + "cnt"], [tag + "junk", tag + "cnt"], accum_out=cnt[:, 0:1])
        if nchA > 1:
            P.add("dve", lambda e: e.tensor_reduce(ssum[:], sgn[:, 0:nchA], AX.X, ALU.add),
                  [f"{tag}sgn{c_}" for c_ in range(nchA)], [tag + "ssum"])
            srd, srk = ssum, [tag + "ssum"]
        else:
            srd, srk = sgn, [f"{tag}sgn0"]
        P.stt(tot[:], cnt[:], 2.0, srd[:, 0:1], ALU.mult, ALU.add, [tag + "cnt"] + srk, [tag + "tot"])
        P.ts("dve", pred[:], tot[:], float(2 * TOPK - nA), None, ALU.is_ge, None, [tag + "tot"], [tag + "pred"])
        P.tt("dve", d1[:], mid[:], lo[:], ALU.subtract, [tag + "mid", tag + "lo"], [tag + "d1"])
        P.tt("dve", d2[:], hi[:], mid[:], ALU.subtract, [tag + "mid", tag + "hi"], [tag + "d2"])
        P.stt(lo[:], d1[:], pred[:, 0:1], lo[:], ALU.mult, ALU.add, [tag + "d1", tag + "pred", tag + "lo"], [tag + "lo"])
        P.stt(hi[:], d2[:], pred[:, 0:1], mid[:], ALU.mult, ALU.add, [tag + "d2", tag + "pred", tag + "mid"], [tag + "hi"])


def aattn_prompt(cx, ti):
    from contextlib import ExitStack
    nc = cx.nc
    cst = cx.cst
    b, m = divmod(ti, 4)
    nkb = 32 * m + 32
    nk = 128 * nkb
    z0k = 32 * m * 128
    with ExitStack() as es:
        def sb(name, shape, dt):
            return es.enter_context(nc.sbuf_tensor(f"aa{ti}_" + name, shape, dt))
        isc = sb("isc", [128, 16384], F32)
        msks = Rot([sb(f"msk{i}", [128, 2048], BF16) for i in range(2)], "msk")
        junk = sb("junk", [128, 2048], BF16)
        nb = sb("nb", [128, 2048], BF16)
        rls = Rot([sb(f"rl{i}", [128, 512], BF16) for i in range(4)], "rl")
        dg = sb("dg", [128, 8, 128], BF16)
        kich_t = [sb(f"kich{i}", [128, 2048], BF16) for i in range(2)]
        kichs = Rot(kich_t, "kich")
        kach_t = [[sb(f"kach{kv}{i}", [128, 2048], BF16) for i in range(2)] for kv in range(2)]
        kachs = [Rot(kach_t[kv], f"kach{kv}") for kv in range(2)]
        vach_t = [[sb(f"vach{kv}{i}", [128, 16, 128], BF16) for i in range(2)] for kv in range(2)]
        vachs = [Rot(vach_t[kv], f"vach{kv}") for kv in range(2)]
        ess = Rot([sb(f"es{i}", [128, 512], BF16) for i in range(4)], "es")
        pms = Rot([sb(f"pm{i}", [128, 512], BF16) for i in range(6)], "pm")
        mTs = Rot([sb(f"mT{i}", [128, 128], BF16) for i in range(4)], "mT")
        sm = {k: sb("sm_" + k, [128, 1], F32) for k in ("lo", "hi", "mid", "cnt", "pred", "d1", "d2", "mx", "mn",
                                                          "nmid", "ssum", "tot")}
        junkA = sb("junkA", [128, 2048], BF16)
        sgn = sb("sgn", [128, 8], F32)
        rd = sb("rd", [128, 512], F32)
        num = sb("num", [64, 512], F32)
        P = Prog(cx.sync)
        for kv in range(2):
            for i in range(2):
                P.memset("pool", vach_t[kv][i][:, :, 65:128], 0.0, [], [f"vach{kv}{i}one"])
                P.memset("pool", vach_t[kv][i][:, :, 64:65], 1.0, [], [f"vach{kv}{i}one"])
                P.memset("pool", kach_t[kv][i][64:128, :], 0.0, [], [f"kach{kv}{i}pad"])
        for i in range(2):
            P.memset("pool", kich_t[i][32:64, :], 0.0, [], [f"kich{i}pad"])
            P.memset("pool", kich_t[i][64:128, :], 0.0, [], [f"kich{i}pad"])
        pls = Rot([cx.ps[0], cx.ps[1], cx.ps[2]], "pss")
        paccs = Rot([cx.ps[3], cx.ps[4]], "pacc", keys=["pss3", "pmt0"])
        pmts = Rot([cx.ps[4], cx.ps[7]], "pmt")
        psss = Rot([cx.ps[0], cx.ps[1], cx.ps[2], cx.ps[3]], "pss")
        poa = [cx.ps[5], cx.ps[6]]
        pb = cx.ps[7]
        ident = cst[:, C_ID:C_ID + 128]
        nchunk = nkb // 16
        for j in range(4):
            qs = slice(j * 128, (j + 1) * 128)
            ikeys = [f"isc{i}" for i in range(nk // 512)]
            for h in range(8):
                P.ts("dve", dg[:, h, :], ident, cx.wiT[:, j, h:h + 1], None, ALU.mult, None, ["cst", "wiT"], [f"dg{h}"])
            units = [(kc, c4, h) for kc in range(nchunk) for c4 in range(4) for h in range(8)]
            nu = len(units)
            kiload = {}

            def ki_load(kc):
                if kc < nchunk and kc not in kiload:
                    kich, kichk = kichs.next()
                    P.dma("sp", kich[0:32, :], cx.KIT[b][:, kc * 2048:(kc + 1) * 2048], kichk, [], [kichk])
                    kiload[kc] = (kich, kichk)
            ki_load(0)
            ust = [dict() for _ in units]

            def u_pl(i):
                kc, c4, h = units[i]
                if c4 == 0 and h == 0:
                    ki_load(kc + 1)
                kich, kichk = kiload[kc]
                pl, plk = pls.next()
                P.mm(pl[:], cx.qiT[:, h, qs], kich[:, c4 * 512:(c4 + 1) * 512], True, True,
                     [f"qiT{h}", "qiTpad", kichk, kichk + "pad"], [plk])
                ust[i]["pl"] = (pl, plk)

            def u_relu(i):
                kc, c4, h = units[i]
                pl, plk = ust[i]["pl"]
                rl, rlk = rls.next()
                if h % 2 == 0:
                    P.act(rl[:], pl[:], AF.Relu, [plk], [rlk])
                else:
                    P.ts("dve", rl[:], pl[:], 0.0, None, ALU.max, None, [plk], [rlk])
                ust[i]["rl"] = (rl, rlk)

            def u_acc(i):
                kc, c4, h = units[i]
                ci = kc * 4 + c4
                if h == 0:
                    ust[i]["pacc"] = paccs.next()
                else:
                    ust[i]["pacc"] = ust[i - 1]["pacc"]
                pacc, pacck = ust[i]["pacc"]
                rl, rlk = ust[i]["rl"]
                P.mm(pacc[:], dg[:, h, :], rl[:], h == 0, h == 7, [f"dg{h}", rlk], [pacck])
                if h == 7:
                    ksl = slice(ci * 512, (ci + 1) * 512)
                    P.copy("act" if ci % 2 == 0 else "dve", isc[:, ksl], pacc[:], [pacck], [ikeys[ci]])

            u_pl(0)
            if nu > 1:
                u_pl(1)
            u_relu(0)
            for t in range(nu):
                if t + 2 < nu:
                    u_pl(t + 2)
                if t + 1 < nu:
                    u_relu(t + 1)
                u_acc(t)
            P.add("dve", lambda e: e.tensor_reduce(sm["mx"][:], isc[:, 0:nk], AX.X, ALU.max), ikeys, ["aamx"])
            P.add("dve", lambda e: e.tensor_reduce(sm["mn"][:], isc[:, 0:nk], AX.X, ALU.min), ikeys, ["aamn"])
            P.ts("dve", sm["hi"][:], sm["mx"][:], 1.0, None, ALU.add, None, ["aamx"], ["aahi"])
            P.ts("dve", sm["lo"][:], sm["mn"][:], -1.0, None, ALU.add, None, ["aamn"], ["aalo"])
            for hf in range(2):
                P.dma("sp", nb[:], cx.negb[j][:, hf * 2048:(hf + 1) * 2048], "aanb", [], ["nb"])
                for i4 in range(4):
                    ci = z0k // 512 + hf * 4 + i4
                    ksl = slice(ci * 512, (ci + 1) * 512)
                    P.tt("pool", isc[:, ksl], isc[:, ksl], nb[:, i4 * 512:(i4 + 1) * 512], ALU.add, [ikeys[ci], "nb"],
                         [ikeys[ci]])
            topk_threshold_split(P, cx, isc, ikeys, nk, sm, junk, junkA, sgn, "aa")
            asteps = [dict(kc=kc, bi=bi) for kc in range(nchunk) for bi in range(16)]
            na = len(asteps)
            chunks = {}

            def a_load(kc):
                msk, mskk = msks.next()
                P.ts("dve", msk[:], isc[:, kc * 2048:(kc + 1) * 2048], sm["lo"][:, 0:1], None, ALU.is_gt, None,
                     ikeys[kc * 4:kc * 4 + 4] + ["aalo"], [mskk])
                ent = dict(msk=(msk, mskk), kach=[], vach=[])
                for kv in range(2):
                    ka_ = kachs[kv].next()
                    va_ = vachs[kv].next()
                    P.dma("sp", ka_[0][0:64, :], cx.KAT[b, kv, :, kc * 2048:(kc + 1) * 2048], ka_[1], [], [ka_[1]])
                    P.dma("sp", va_[0][:, :, 0:64], cx.VAS[b, kv, :, kc * 16:(kc + 1) * 16, :], va_[1], [], [va_[1]])
                    ent["kach"].append(ka_)
                    ent["vach"].append(va_)
                chunks[kc] = ent

            def a0(i):
                st = asteps[i]
                kc, bi = st["kc"], st["bi"]
                if bi == 0 and kc not in chunks:
                    a_load(kc)
                ent = chunks[kc]
                msk, mskk = ent["msk"]
                pmt_, pmtk = pmts.next()
                P.mm(pmt_[:, 0:128], msk[:, bi * 128:(bi + 1) * 128], ident, True, True, [mskk, "cst"], [pmtk])
                mT, mTk = mTs.next()
                st["mT"] = (mT, mTk)
                P.copy("dve", mT[:], pmt_[:, 0:128], [pmtk], [mTk])
                st["pss"] = []
                for kv in range(2):
                    pss, pssk = psss.next()
                    ka_ = ent["kach"][kv]
                    P.mm(pss[:], ka_[0][:, bi * 128:(bi + 1) * 128], cx.qaT[:, 4 * kv:4 * kv + 4, qs],
                         True, True, [ka_[1], ka_[1] + "pad", "qaTpad"] + [f"qaT{h}" for h in range(4 * kv, 4 * kv + 4)],
                         [pssk])
                    st["pss"].append((pss, pssk))

            def a1(i):
                st = asteps[i]
                st["es"] = []
                for kv in range(2):
                    pss, pssk = st["pss"][kv]
                    es_, esk = ess.next()
                    P.act(es_[:], pss[:], AF.Exp, [pssk], [esk])
                    st["es"].append((es_, esk))

            def a2(i):
                st = asteps[i]
                mT, mTk = st["mT"]
                st["pm"] = []
                for kv in range(2):
                    es_, esk = st["es"][kv]
                    pm, pmk = pms.next()
                    P.tt("pool" if kv else "dve", pm[:].rearrange("p (h q) -> p h q", h=4),
                         es_[:].rearrange("p (h q) -> p h q", h=4),
                         mT[:, :].unsqueeze(1).broadcast_to([128, 4, 128]), ALU.mult, [esk, mTk], [pmk])
                    st["pm"].append((pm, pmk))

            def a3(i):
                st = asteps[i]
                ent = chunks[st["kc"]]
                for kv in range(2):
                    pm, pmk = st["pm"][kv]
                    va_ = ent["vach"][kv]
                    P.mm(poa[kv][:, :], va_[0][:, st["bi"], :], pm[:], i == 0, i == na - 1,
                         [pmk, va_[1], va_[1] + "one"], [f"poa{kv}"])
                if st["bi"] == 15:
                    chunks.pop(st["kc"], None) if False else None

            a0(0)
            for t in range(na + 2):
                if t + 1 < na:
                    a0(t + 1)
                if t < na:
                    a1(t)
                if 0 <= t - 1 < na:
                    a2(t - 1)
                if 0 <= t - 2 < na:
                    a3(t - 2)
            for kv in range(2):
                P.add("dve", (lambda kv: lambda e: e.reciprocal(rd[64:65, :], poa[kv][64:65, :]))(kv), [f"poa{kv}"], ["rd"])
                P.mm(pb[0:64, :], cx.ones_f[64:65, 0:64], rd[64:65, :], True, True, ["rd", "ones_f"], ["pmt1"])
                P.copy("act", num[:], poa[kv][0:64, :], [f"poa{kv}", "rd"], ["num"])
                P.tt("dve", cx.oaT[0:64, 4 * kv:4 * kv + 4, qs], num[:].rearrange("p (h q) -> p h q", h=4),
                     pb[0:64, :].rearrange("p (h q) -> p h q", h=4), ALU.mult, ["num", "pmt1"],
                     [f"oaT{h}" for h in range(4 * kv, 4 * kv + 4)])
        if os.environ.get("KDBG"):
            P.dma("sp", cx.dbg_oa, cx.oaT[:], "dbgoa", [f"oaT{h}" for h in range(8)], [])
        P.run()


def slab_plan(cx):
    def fm(src_ap):
        return src_ap.rearrange("(k p) c -> p k c", p=128)
    pa_v = cx.proj_a.rearrange("(h d) c -> d h c", d=64)
    pb_v = cx.proj_b.rearrange("(h d) c -> d h c", d=64)
    wd_v = cx.w_down.rearrange("(g k p) c -> g p k c", k=8, p=128)
    plan = []
    for J in range(2):
        cs_ = slice(J * 512, (J + 1) * 512)
        plan.append((cx.w_in_v[:, :, OFF["ga"] + J * 512:OFF["ga"] + (J + 1) * 512], 128, 8))
        plan.append((cx.w_in_v[:, :, OFF["gb"] + J * 512:OFF["gb"] + (J + 1) * 512], 128, 8))
        plan.append((pa_v[:, :, cs_], 64, 8))
        plan.append((pb_v[:, :, cs_], 64, 8))
    for J in range(2):
        plan.append((fm(cx.w_out)[:, :, J * 512:(J + 1) * 512], 128, 8))
    for J in range(8):
        plan.append((fm(cx.w_up)[:, :, J * 512:(J + 1) * 512], 128, 8))
    for J in range(2):
        for kg in range(4):
            plan.append((wd_v[kg][:, :, J * 512:(J + 1) * 512], 128, 8))
    for J in range(2):
        plan.append((fm(cx.w_pg)[:, :, J * 512:(J + 1) * 512], 128, 8))
        plan.append((fm(cx.w_ple)[:, :, J * 512:(J + 1) * 512], 128, 2))
    return plan


def prep_weights(cx):
    from contextlib import ExitStack
    nc = cx.nc
    with ExitStack() as es:
        stg = [es.enter_context(nc.sbuf_tensor(f"p0_stg{i}", [128, 8, 512], BF16)) for i in range(3)]
        P = Prog(cx.sync)
        for i, (src, np_, nk) in enumerate(slab_plan(cx)):
            s_ = stg[i % 3]
            P.dma("pool", s_[0:np_, 0:nk, :], src, f"p0i{i % 3}", [], [f"p0s{i % 3}"])
            P.dma("sp", cx.WS[i][0:np_, 0:nk, :], s_[0:np_, 0:nk, :], f"p0o{i % 3}", [f"p0s{i % 3}"], [])
        P.run()


def stage4(cx, ti):
    from contextlib import ExitStack
    nc = cx.nc
    sample = ti == 8
    N = 128 if sample else 512
    c0 = ti * 512
    with ExitStack() as es:
        def sb(name, shape, dt):
            return es.enter_context(nc.sbuf_tensor(f"s4{ti}_" + name, shape, dt))
        slabs = Rot([sb(f"slab{i}", [128, 8, 512], BF16) for i in range(6)], "slab")
        xt = sb("xt", [128, 8, 512], F32)
        pTt = sb("pTt", [128, 2, 512], BF16)
        mT = sb("mT", [128, 8, 512], BF16)
        sq = sb("sq", [128, 8, 512], BF16)
        lnv = sb("lnv", [128, 512], F32)
        rstd = sb("rstd", [128, 512], F32)
        h2 = sb("h2", [128, 8, 512], BF16)
        uT = sb("uT", [128, 32, 512], BF16)
        sgas = Rot([sb(f"sga{i}", [128, 512], F32) for i in range(2)], "sga")
        sgbs = Rot([sb(f"sgb{i}", [128, 512], F32) for i in range(2)], "sgb")
        tAs = Rot([sb(f"tA{i}", [128, 512], F32) for i in range(2)], "tA")
        tBs = Rot([sb(f"tB{i}", [128, 512], F32) for i in range(2)], "tB")
        rls = Rot([sb(f"rl{i}", [128, 512], BF16) for i in range(2)], "rl")
        yos = Rot([sb(f"yo{i}", [128, 512], F32) for i in range(2)], "yo")
        P = Prog(cx.sync)
        slab_n = [0]

        issued = []

        def issue_upto(n_):
            while len(issued) < min(n_, len(plan)):
                src, np_, nk = plan[len(issued)]
                sl, slk = slabs.next()
                P.dma("sp", sl[0:np_, 0:nk, :], cx.WS[len(issued)][0:np_, 0:nk, :], slk, [], [slk])
                issued.append((sl, slk))

        def load_slab(src=None, np_=128, nk=8):
            idx = slab_n[0]
            slab_n[0] += 1
            issue_upto(idx + 3)
            return issued[idx]

        def fm(src_ap):
            return src_ap.rearrange("(k p) c -> p k c", p=128)

        P.dma("sp", xt[:, :, 0:N], fm(cx.xT_own)[:, :, c0:c0 + N], "s4x", [], [f"x{j}" for j in range(8)])
        P.dma("pool", pTt[:, :, 0:N], fm(cx.pT_own)[:, :, c0:c0 + N], "s4p", [], ["pTt"])
        hk = [f"hT{k}" for k in range(8)]
        ps4 = Rot([cx.ps[0], cx.ps[1], cx.ps[2], cx.ps[3]], "bank")
        pa_v = cx.proj_a.rearrange("(h d) c -> d h c", d=64)
        pb_v = cx.proj_b.rearrange("(h d) c -> d h c", d=64)
        wd_v = cx.w_down.rearrange("(g k p) c -> g p k c", k=8, p=128)
        plan = []
        for J in range(2):
            cs_ = slice(J * 512, (J + 1) * 512)
            plan.append((cx.w_in_v[:, :, OFF["ga"] + J * 512:OFF["ga"] + (J + 1) * 512], 128, 8))
            plan.append((cx.w_in_v[:, :, OFF["gb"] + J * 512:OFF["gb"] + (J + 1) * 512], 128, 8))
            plan.append((pa_v[:, :, cs_], 64, 8))
            plan.append((pb_v[:, :, cs_], 64, 8))
        for J in range(2):
            plan.append((fm(cx.w_out)[:, :, J * 512:(J + 1) * 512], 128, 8))
        for J in range(8):
            plan.append((fm(cx.w_up)[:, :, J * 512:(J + 1) * 512], 128, 8))
        for J in range(2):
            for kg in range(4):
                plan.append((wd_v[kg][:, :, J * 512:(J + 1) * 512], 128, 8))
        for J in range(2):
            plan.append((fm(cx.w_pg)[:, :, J * 512:(J + 1) * 512], 128, 8))
            plan.append((fm(cx.w_ple)[:, :, J * 512:(J + 1) * 512], 128, 2))
        issue_upto(4)
        for J in range(2):
            cs = slice(J * 512, (J + 1) * 512)
            wga, wgak = load_slab(cx.w_in_v[:, :, OFF["ga"] + J * 512:OFF["ga"] + (J + 1) * 512])
            wgb, wgbk = load_slab(cx.w_in_v[:, :, OFF["gb"] + J * 512:OFF["gb"] + (J + 1) * 512])
            wpa, wpak = load_slab(pa_v[:, :, cs], 64)
            wpb, wpbk = load_slab(pb_v[:, :, cs], 64)
            for jj in range(4):
                j = 4 * J + jj
                js = slice(jj * 128, (jj + 1) * 128)
                pga, pgak = ps4.next()
                for k in range(8):
                    P.mm(pga[:, 0:N], wga[:, k, js], cx.hT[:, k, 0:N], k == 0, k == 7, [wgak, hk[k]], [pgak])
                sga, sgak = sgas.next()
                P.act(sga[:, 0:N], pga[:, 0:N], AF.Sigmoid, [pgak], [sgak])
                pgb, pgbk = ps4.next()
                for k in range(8):
                    P.mm(pgb[:, 0:N], wgb[:, k, js], cx.hT[:, k, 0:N], k == 0, k == 7, [wgbk, hk[k]], [pgbk])
                sgb, sgbk = sgbs.next()
                P.act(sgb[:, 0:N], pgb[:, 0:N], AF.Sigmoid, [pgbk], [sgbk])
                ppa, ppak = ps4.next()
                for h in range(8):
                    P.mm(ppa[:, 0:N], wpa[0:64, h, js], cx.oaT[0:64, h, 0:N], h == 0, h == 7, [wpak, f"oaT{h}"], [ppak])
                tA, tAk = tAs.next()
                P.tt("dve", tA[:, 0:N], ppa[:, 0:N], sga[:, 0:N], ALU.mult, [ppak, sgak], [tAk])
                ppb, ppbk = ps4.next()
                for h in range(8):
                    P.mm(ppb[:, 0:N], wpb[0:64, h, js], cx.obT[0:64, h, 0:N], h == 0, h == 7, [wpbk, f"obT{h}"], [ppbk])
                tB, tBk = tBs.next()
                P.tt("dve", tB[:, 0:N], ppb[:, 0:N], sgb[:, 0:N], ALU.mult, [ppbk, sgbk], [tBk])
                P.tt("pool", mT[:, j, 0:N], tA[:, 0:N], tB[:, 0:N], ALU.add, [tAk, tBk], [f"mT{j}"])
        for J in range(2):
            wo, wok = load_slab(fm(cx.w_out)[:, :, J * 512:(J + 1) * 512])
            for jj in range(4):
                j = 4 * J + jj
                pm, pmk = ps4.next()
                for k in range(8):
                    P.mm(pm[:, 0:N], wo[:, k, jj * 128:(jj + 1) * 128], mT[:, k, 0:N], k == 0, k == 7, [wok, f"mT{k}"], [pmk])
                P.tt("dve", xt[:, j, 0:N], xt[:, j, 0:N], pm[:, 0:N], ALU.add, [f"x{j}", pmk], [f"x{j}"])
        xkeys = [f"x{j}" for j in range(8)]
        norm_ops_k(P, cx, xt, xkeys, N, sq, lnv, rstd, h2, cx.g_ffn_t, "h2_", 4, "s4")
        h2k = [f"h2_{k}" for k in range(8)]
        for J in range(8):
            wu, wuk = load_slab(fm(cx.w_up)[:, :, J * 512:(J + 1) * 512])
            for jj in range(4):
                j = 4 * J + jj
                pu, puk = ps4.next()
                for k in range(8):
                    P.mm(pu[:, 0:N], wu[:, k, jj * 128:(jj + 1) * 128], h2[:, k, 0:N], k == 0, k == 7, [wuk, h2k[k]], [puk])
                rl, rlk = rls.next()
                P.act(rl[:, 0:N], pu[:, 0:N], AF.Relu, [puk], [rlk])
                P.tt("pool", uT[:, j, 0:N], rl[:, 0:N], rl[:, 0:N], ALU.mult, [rlk], [f"uT{j}"])
        wd_v = cx.w_down.rearrange("(g k p) c -> g p k c", k=8, p=128)
        for J in range(2):
            for kg in range(4):
                wd, wdk = load_slab(wd_v[kg][:, :, J * 512:(J + 1) * 512])
                for jj in range(4):
                    for k in range(8):
                        P.mm(cx.ps[4 + jj][:, 0:N], wd[:, k, jj * 128:(jj + 1) * 128], uT[:, kg * 8 + k, 0:N],
                             kg == 0 and k == 0, kg == 3 and k == 7, [wdk, f"uT{kg * 8 + k}"], [f"bank{4 + jj}"])
            for jj in range(4):
                j = 4 * J + jj
                P.tt("dve", xt[:, j, 0:N], xt[:, j, 0:N], cx.ps[4 + jj][:, 0:N], ALU.add, [f"x{j}", f"bank{4 + jj}"], [f"x{j}"])
        norm_ops_k(P, cx, xt, xkeys, N, sq, lnv, rstd, h2, cx.g_ple_t, "h2_", 0, "s4")
        for J in range(2):
            wg, wgk = load_slab(fm(cx.w_pg)[:, :, J * 512:(J + 1) * 512])
            wp, wpk = load_slab(fm(cx.w_ple)[:, :, J * 512:(J + 1) * 512], 128, 2)
            for jj in range(4):
                j = 4 * J + jj
                js = slice(jj * 128, (jj + 1) * 128)
                pg, pgk = ps4.next()
                for k in range(8):
                    P.mm(pg[:, 0:N], wg[:, k, js], h2[:, k, 0:N], k == 0, k == 7, [wgk, h2k[k]], [pgk])
                sga, sgak = sgas.next()
                P.act(sga[:, 0:N], pg[:, 0:N], AF.Sigmoid, [pgk], [sgak])
                pe_, pek = ps4.next()
                for k in range(2):
                    P.mm(pe_[:, 0:N], wp[:, k, js], pTt[:, k, 0:N], k == 0, k == 1, [wpk, "pTt"], [pek])
                tA, tAk = tAs.next()
                P.tt("dve", tA[:, 0:N], pe_[:, 0:N], sga[:, 0:N], ALU.mult, [pek, sgak], [tAk])
                yo, yok = yos.next()
                P.tt("pool", yo[:, 0:N], tA[:, 0:N], xt[:, j, 0:N], ALU.add, [tAk, f"x{j}"], [yok])
                P.dma("sp", cx.yT[j * 128:(j + 1) * 128, c0:c0 + N], yo[:, 0:N], yok, [yok], [])
        P.run()


def norm_ops_k(P, cx, xt, xkeys, N, sq, lnv, rstd, hT, gcols, hkey, psb, tag):
    cst = cx.cst
    P.act(sq[:, :, 0:N], xt[:, :, 0:N], AF.Square, xkeys, [tag + "sq"])
    for k in range(8):
        P.mm(cx.ps[psb][:, 0:N], cst[:, C_ONES:C_ONES + 128], sq[:, k, 0:N], k == 0, k == 7,
             [tag + "sq", "cst"], [f"bank{psb}"])
    P.act(lnv[:, 0:N], cx.ps[psb][:, 0:N], AF.Ln, [f"bank{psb}"], [tag + "lnv"], bias=EPS, scale=1.0 / D)
    P.act(rstd[:, 0:N], lnv[:, 0:N], AF.Exp, [tag + "lnv"], [tag + "rstd"], scale=-0.5)
    for k in range(8):
        P.stt(hT[:, k, 0:N], xt[:, k, 0:N], gcols[:, k:k + 1], rstd[:, 0:N], ALU.mult, ALU.mult,
              [xkeys[k], tag + "rstd", "gains"], [f"{hkey}{k}"])


def attn_sample(cx):
    from contextlib import ExitStack
    nc = cx.nc
    cst = cx.cst
    NKS = PAST + DSEQ
    with ExitStack() as es:
        def sb(name, shape, dt):
            return es.enter_context(nc.sbuf_tensor("as_" + name, shape, dt))
        kchs = Rot([sb(f"kch{i}", [64, 8, 1024], BF16) for i in range(2)], "kch")
        vchs = Rot([sb(f"vch{i}", [128, 8, 512], BF16) for i in range(2)], "vch")
        e2s = Rot([sb(f"e2{i}", [128, 256], F32) for i in range(4)], "e2")
        nlks = Rot([sb(f"nlk{i}", [128, 256], BF16) for i in range(3)], "nlk")
        ggs = Rot([sb(f"gg{i}", [128, 256], F32) for i in range(2)], "gg")
        aas = Rot([sb(f"aa{i}", [128, 256], BF16) for i in range(3)], "aa")
        S = sb("S", [128, 256], BF16)
        S2 = sb("S2", [128, 256], BF16)
        isc = sb("isc", [32, 4608], F32)
        msk = sb("msk", [32, 4608], BF16)
        junk = sb("junk", [32, 2048], BF16)
        tmps = Rot([sb(f"tmp{i}", [32, 512], F32) for i in range(2)], "tmp")
        kich = sb("kich", [32, 4096], BF16)
        kach = [sb(f"kach{kv}", [64, 4096], BF16) for kv in range(2)]
        vach = sb("vach", [128, 32, 2, 65], BF16)
        ess = Rot([sb(f"es{i}", [128, 128], BF16) for i in range(4)], "es")
        pms = Rot([sb(f"pm{i}", [128, 128], BF16) for i in range(6)], "pm")
        mTs = Rot([sb(f"mT{i}", [128, 32], BF16) for i in range(3)], "mT")
        sm = {k: sb("sm_" + k, [32, 1], F32) for k in ("lo", "hi", "mid", "cnt", "pred", "d1", "d2", "mx", "mn")}
        rd = sb("rd", [128, 128], F32)
        num = sb("num", [64, 128], F32)
        P = Prog(cx.sync)
        tri = cst[:, C_TRI:C_TRI + 128]
        ones = cst[:, C_ONES:C_ONES + 128]
        ident = cst[:, C_ID:C_ID + 128]
        lt32 = cst[0:32, C_LT:C_LT + 32]
        pzs = Rot([cx.ps[0], cx.ps[1]], "pz")
        pcs = Rot([cx.ps[2], cx.ps[3]], "pc")
        pos2 = [cx.ps[4], cx.ps[5]]
        bsteps = []
        for s in (range(4) if "b" in os.environ.get("KAS", "ab") else []):
            blocks = [("new", 0, 0)]
            for kc in reversed(range(4)):
                for bi in reversed(range(8)):
                    blocks.append(("cache", kc, bi))
            for bidx, (kind, kc, bi) in enumerate(blocks):
                bsteps.append(dict(s=s, kind=kind, kc=kc, bi=bi, first=bidx == 0, last=bidx == len(blocks) - 1,
                                   nkp=32 if kind == "new" else 128))
        nb_ = len(bsteps)
        chunkmap = {}
        corder = []
        for st in bsteps:
            if st["kind"] == "cache" and (st["s"], st["kc"]) not in chunkmap:
                chunkmap[(st["s"], st["kc"])] = None
                corder.append((st["s"], st["kc"]))
        cnext = [0]

        def c_load():
            if cnext[0] < len(corder):
                s_, kc = corder[cnext[0]]
                cnext[0] += 1
                kch, kchk = kchs.next()
                vch, vchk = vchs.next()
                P.dma("pool", kch[:], cx.cbkT[s_][:, :, kc * 1024:(kc + 1) * 1024].rearrange("h d t -> d h t"), kchk,
                      [], [kchk])
                P.dma("pool", vch[:], cx.cbv[s_][kc * 1024:(kc + 1) * 1024, :].rearrange("(j p) c -> p j c", p=128),
                      vchk, [], [vchk])
                chunkmap[(s_, kc)] = (kch, kchk, vch, vchk)
        c_load()
        c_load()
        Ssb = [S, S2]

        def sb_z(i):
            st = bsteps[i]
            s_, nkp = st["s"], st["nkp"]
            qs = slice(32 * s_, 32 * s_ + 32)
            pz, pzk = pzs.next()
            st["pz"] = (pz, pzk)
            for h in range(8):
                if st["kind"] == "new":
                    lhsT = cx.kbnT[0:64, h, qs]
                    rk = [f"kbnT{h}"]
                else:
                    kch, kchk, vch, vchk = chunkmap[(s_, st["kc"])]
                    lhsT = kch[0:64, h, st["bi"] * 128:(st["bi"] + 1) * 128]
                    rk = [kchk]
                P.mm(pz[0:nkp, h * 32:(h + 1) * 32], lhsT, cx.qbT[0:64, h, qs], True, True, rk + [f"qbT{h}"], [pzk])

        def sb_e(i):
            st = bsteps[i]
            nkp = st["nkp"]
            pz, pzk = st["pz"]
            e2, e2k = e2s.next()
            st["e2"] = (e2, e2k)
            P.act(e2[0:nkp, :], pz[0:nkp, 0:256], AF.Exp, [pzk], [e2k])
            if st["kind"] == "new":
                P.tt("pool", e2[0:32, :].rearrange("p (h q) -> p h q", h=8), e2[0:32, :].rearrange("p (h q) -> p h q", h=8),
                     lt32.unsqueeze(1).broadcast_to([32, 8, 32]), ALU.mult, [e2k, "cst"], [e2k])

        def sb_ln(i):
            st = bsteps[i]
            nkp = st["nkp"]
            e2, e2k = st["e2"]
            nlk, nlkk = nlks.next()
            st["nlk"] = (nlk, nlkk)
            P.act(nlk[0:nkp, :], e2[0:nkp, :], AF.Ln, [e2k], [nlkk], bias=1.0)

        def sb_c(i):
            st = bsteps[i]
            nkp = st["nkp"]
            Sx = Ssb[st["s"] % 2]
            Sk = f"S{st['s'] % 2}"
            nlk, nlkk = st["nlk"]
            pc, pck = pcs.next()
            st["pc"] = (pc, pck)
            if st["first"]:
                P.mm(pc[0:nkp, 0:256], tri[0:nkp, 0:nkp], nlk[0:nkp, :], True, True, [nlkk, "cst"], [pck])
                P.memset("pool", Sx[:], 0.0, [], [Sk])
                P.copy("pool", Sx[0:32, :], nlk[0:32, :], [nlkk, Sk], [Sk])
            else:
                P.mm(pc[0:nkp, 0:256], tri[0:nkp, 0:nkp], nlk[0:nkp, :], True, False, [nlkk, "cst"], [pck])
                P.mm(pc[0:nkp, 0:256], ones[:, 0:nkp], Sx[:], False, True, [Sk, "cst"], [pck])
                if not st["last"]:
                    P.tt("dve", Sx[:], Sx[:], nlk[:], ALU.add, [Sk, nlkk], [Sk])
            gg, ggk = ggs.next()
            P.act(gg[0:nkp, :], pc[0:nkp, 0:256], AF.Exp, [pck], [ggk], scale=-1.0)
            e2, e2k = st["e2"]
            aa, aak = aas.next()
            st["aa"] = (aa, aak)
            P.tt("dve", aa[0:nkp, :], e2[0:nkp, :], gg[0:nkp, :], ALU.mult, [e2k, ggk], [aak])

        def sb_av(i):
            st = bsteps[i]
            s_, nkp = st["s"], st["nkp"]
            qs = slice(32 * s_, 32 * s_ + 32)
            aa, aak = st["aa"]
            po_ = pos2[s_ % 2]
            pok = f"po{s_ % 2}"
            for h in range(8):
                if st["kind"] == "new":
                    lv = cx.vbn[0:32, s_, h * 64:(h + 1) * 64]
                    rk = [f"vbn{s_}"]
                else:
                    kch, kchk, vch, vchk = chunkmap[(s_, st["kc"])]
                    lv = vch[:, st["bi"], h * 64:(h + 1) * 64]
                    rk = [vchk]
                P.mm(po_[0:64, h * 32:(h + 1) * 32], lv, aa[0:nkp, h * 32:(h + 1) * 32], st["first"] and h == 0, st["last"],
                     rk + [aak], [pok])
            if st["kind"] == "cache" and st["bi"] == 0:
                c_load()
            if st["last"]:
                P.copy("dve", cx.obT[0:64, :, qs], po_[0:64, 0:256].rearrange("p (h q) -> p h q", h=8), [pok],
                       [f"obT{h}" for h in range(8)])

        if nb_:
            sb_z(0)
            if nb_ > 1:
                sb_z(1)
            sb_e(0)
            for t in range(nb_ + 2):
                if t + 2 < nb_:
                    sb_z(t + 2)
                if t + 1 < nb_:
                    sb_e(t + 1)
                if t < nb_:
                    sb_ln(t)
                if 0 <= t - 1 < nb_:
                    sb_c(t - 1)
                if 0 <= t - 2 < nb_:
                    sb_av(t - 2)
        pls = Rot([cx.ps[0], cx.ps[1]], "pz")
        pmt = cx.ps[5]
        psss = Rot([cx.ps[6], cx.ps[7], cx.ps[0], cx.ps[1]], "pss", keys=["pss0", "pss1", "pz0", "pz1"])
        poa = [cx.ps[2], cx.ps[3]]
        pb = cx.ps[4]
        for s in (range(4) if "a" in os.environ.get("KAS", "ab") else []):
            qs = slice(32 * s, 32 * s + 32)
            P.dma("pool", kich[:], cx.ckiT[s], "askich", [], ["kich"])
            for kv in range(2):
                P.dma("pool", kach[kv][:], cx.cakT[s, kv], f"askach{kv}", [], [f"kach{kv}"])
            cav_v = cx.cav[s].rearrange("(j p) (kv d) -> p j kv d", p=128, kv=2)
            for kv in range(2):
                for jh in range(2):
                    P.dma("pool", vach[:, jh * 16:(jh + 1) * 16, kv, 0:64], cav_v[:, jh * 16:(jh + 1) * 16, kv, :],
                          f"asvach{kv}{jh}", [], ["vach"])
            if s == 0:
                P.memset("pool", vach[:, :, :, 64:65], 1.0, [], ["vach1"])
            nchunks = 9
            ikeys = [f"isc{i}" for i in range(12)]
            for ci in range(nchunks):
                w = 512 if ci < 8 else 32
                ksl = slice(ci * 512, ci * 512 + w)
                for h in range(8):
                    pl, plk = pls.next()
                    if ci < 8:
                        rhs = kich[0:32, ksl]
                        rk = ["kich"]
                    else:
                        rhs = cx.kinT[0:32, qs]
                        rk = ["kinT"]
                    P.mm(pl[0:32, 0:w], cx.qiT[0:32, h, qs], rhs, True, True, [f"qiT{h}"] + rk, [plk])
                    if h == 0:
                        P.ts("dve", isc[:, ksl], pl[0:32, 0:w], 0.0, cx.wiT[0:32, s, 0:1], ALU.max, ALU.mult, [plk, "wiT"],
                             [ikeys[ci]])
                    else:
                        tmp, tmpk = tmps.next()
                        P.ts("dve", tmp[:, 0:w], pl[0:32, 0:w], 0.0, cx.wiT[0:32, s, h:h + 1], ALU.max, ALU.mult, [plk, "wiT"],
                             [tmpk])
                        P.tt("pool", isc[:, ksl], isc[:, ksl], tmp[:, 0:w], ALU.add, [ikeys[ci], tmpk], [ikeys[ci]])
            ik = ikeys[0:9]
            if os.environ.get("KAL", "3") < "2":
                continue
            P.add("dve", lambda e: e.tensor_reduce(sm["mx"][:], isc[:, 0:NKS], AX.X, ALU.max), ik, ["asmx"])
            P.add("dve", lambda e: e.tensor_reduce(sm["mn"][:], isc[:, 0:NKS], AX.X, ALU.min), ik, ["asmn"])
            P.ts("dve", sm["hi"][:], sm["mx"][:], 1.0, None, ALU.add, None, ["asmx"], ["ashi"])
            P.ts("dve", sm["lo"][:], sm["mn"][:], -1.0, None, ALU.add, None, ["asmn"], ["aslo"])
            topk_threshold(P, cx, isc, ikeys, NKS, sm, junk, "as", np_=32)
            P.ts("dve", msk[:, 0:NKS], isc[:, 0:NKS], sm["lo"][:, 0:1], None, ALU.is_gt, None, ik + ["aslo"], ["msk"])
            if os.environ.get("KAL", "3") < "3":
                continue
            sst = [dict() for _ in range(33)]

            def sa0(blk):
                nkp = 128 if blk < 32 else 32
                P.mm(pmt[0:nkp, 0:32], msk[0:32, blk * 128:blk * 128 + nkp], ident[0:32, 0:32], True, True, ["msk", "cst"],
                     ["po1"])
                mT, mTk = mTs.next()
                sst[blk]["mT"] = (mT, mTk)
                P.copy("dve", mT[0:nkp, :], pmt[0:nkp, 0:32], ["po1"], [mTk])
                sst[blk]["pss"] = []
                for kv in range(2):
                    pss, pssk = psss.next()
                    if blk < 32:
                        lhsT = kach[kv][0:64, blk * 128:(blk + 1) * 128]
                        rk = [f"kach{kv}"]
                    else:
                        lhsT = cx.kanT[0:64, kv, qs]
                        rk = [f"kanT{kv}"]
                    P.mm(pss[0:nkp, 0:128], lhsT, cx.qaT[0:64, 4 * kv:4 * kv + 4, qs], True, True,
                         rk + [f"qaT{h}" for h in range(4 * kv, 4 * kv + 4)], [pssk])
                    sst[blk]["pss"].append((pss, pssk))

            def sa1(blk):
                nkp = 128 if blk < 32 else 32
                mT, mTk = sst[blk]["mT"]
                sst[blk]["pm"] = []
                for kv in range(2):
                    pss, pssk = sst[blk]["pss"][kv]
                    es_, esk = ess.next()
                    P.act(es_[0:nkp, :], pss[0:nkp, 0:128], AF.Exp, [pssk], [esk])
                    pm, pmk = pms.next()
                    P.tt("pool" if kv else "dve", pm[0:nkp, :].rearrange("p (h q) -> p h q", h=4),
                         es_[0:nkp, :].rearrange("p (h q) -> p h q", h=4),
                         mT[0:nkp, :].unsqueeze(1).broadcast_to([nkp, 4, 32]), ALU.mult, [esk, mTk], [pmk])
                    sst[blk]["pm"].append((pm, pmk))

            def sa2(blk):
                nkp = 128 if blk < 32 else 32
                for kv in range(2):
                    pm, pmk = sst[blk]["pm"][kv]
                    if blk < 32:
                        lv = vach[:, blk, kv, :]
                        rv = ["vach", "vach1"]
                    else:
                        lv = cx.van[0:32, s, kv, :]
                        rv = [f"van{s}", f"van1{s}"]
                    P.mm(poa[kv][0:65, 0:128], lv, pm[0:nkp, :], blk == 0, blk == 32, [pmk] + rv, [f"pc{kv}"])

            if os.environ.get("KSEQ"):
                for t in range(33):
                    sa0(t)
                    sa1(t)
                    sa2(t)
            else:
                sa0(0)
                for t in range(33 + 1):
                    if t + 1 < 33:
                        sa0(t + 1)
                    if t < 33:
                        sa1(t)
                    if 0 <= t - 1 < 33:
                        sa2(t - 1)
            for kv in range(2):
                P.add("dve", (lambda kv: lambda e: e.reciprocal(rd[64:65, :], poa[kv][64:65, 0:128]))(kv), [f"pc{kv}"], ["rd"])
                P.mm(pb[0:64, 0:128], cx.ones_f[64:65, 0:64], rd[64:65, :], True, True, ["rd", "ones_f"], ["po0"])
                P.copy("act", num[:], poa[kv][0:64, 0:128], [f"pc{kv}", "rd"], ["num"])
                P.tt("dve", cx.oaT[0:64, 4 * kv:4 * kv + 4, qs], num[:].rearrange("p (h q) -> p h q", h=4),
                     pb[0:64, 0:128].rearrange("p (h q) -> p h q", h=4), ALU.mult, ["num", "po0"],
                     [f"oaT{h}" for h in range(4 * kv, 4 * kv + 4)])
        if os.environ.get("KDBG"):
            P.dma("sp", cx.dbg_oa, cx.oaT[:], "dbgoa", [f"oaT{h}" for h in range(8)], [])
            P.dma("sp", cx.dbg_ob, cx.obT[:], "dbgob", [f"obT{h}" for h in range(8)], [])
        P.run()


def kernel(**inputs):
    inp = {k: np.asarray(v) for k, v in inputs.items()}
    sh = host_shared(inp)
    nc = build(9, 64, range(9))
    in_maps = [host_core(c, inp, sh) for c in range(8)]
    res = run_bass_kernel_spmd(nc, in_maps, core_ids=list(range(8)))
    f32 = np.float32
    y_p = np.empty((NB, SEQ, D), f32)
    y_s = np.empty((DBATCH, DSEQ, D), f32)
    ka_p = np.empty((1, NB, SEQ, NKV, HD), f32)
    va_p = np.empty((1, NB, SEQ, NKV, HD), f32)
    ki_p = np.empty((1, NB, SEQ, IDD), f32)
    kb_p = np.empty((1, NB, SEQ, H, HD), f32)
    vb_p = np.empty((1, NB, SEQ, H, HD), f32)
    ka_s = np.empty((1, DBATCH, DSEQ, NKV, HD), f32)
    va_s = np.empty((1, DBATCH, DSEQ, NKV, HD), f32)
    ki_s = np.empty((1, DBATCH, DSEQ, IDD), f32)
    kb_s = np.empty((1, DBATCH, DSEQ, H, HD), f32)
    vb_s = np.empty((1, DBATCH, DSEQ, H, HD), f32)
    for c in range(8):
        r = res.results[c]
        yT, kaT, va, kiT, kbT, vb = (np.asarray(r[k]) for k in ("yT", "o_kaT", "o_va", "o_kiT", "o_kbT", "o_vb"))
        for t, (b, q0) in enumerate(own_tiles(c)):
            sl = slice(t * 512, (t + 1) * 512)
            qs = slice(q0, q0 + 512)
            y_p[b, qs] = yT[:, sl].T
            ka_p[0, b, qs] = kaT[:, sl].T.reshape(512, NKV, HD)
            va_p[0, b, qs] = va[sl].reshape(512, NKV, HD)
            ki_p[0, b, qs] = kiT[:, sl].T
            kb_p[0, b, qs] = kbT[:, sl].T.reshape(512, H, HD)
            vb_p[0, b, qs] = vb[sl].reshape(512, H, HD)
        sl = slice(4096, NOWN)
        ss = slice(4 * c, 4 * c + 4)
        y_s[ss] = yT[:, sl].T.reshape(4, DSEQ, D)
        ka_s[0, ss] = kaT[:, sl].T.reshape(4, DSEQ, NKV, HD)
        va_s[0, ss] = va[sl].reshape(4, DSEQ, NKV, HD)
        ki_s[0, ss] = kiT[:, sl].T.reshape(4, DSEQ, IDD)
        kb_s[0, ss] = kbT[:, sl].T.reshape(4, DSEQ, H, HD)
        vb_s[0, ss] = vb[sl].reshape(4, DSEQ, H, HD)
    return (y_p, y_s, ka_p, va_p, ki_p, kb_p, vb_p, ka_s, va_s, ki_s, kb_s, vb_s)
```
